# Optimizing a Trainium2 kernel written in Bass

```python
import jax, jax.numpy as jnp
from jax import lax
import numpy as np

D_MODEL = 1024
BATCH = 4
SEQ = 4096
DEPTH = 2

N_META = 16
HG_HEADS = 8
HG_DK = 128
HG_DV = D_MODEL // HG_HEADS
HG_DIM_K = HG_HEADS * HG_DK
HG_DIM_V = HG_HEADS * HG_DV
CHUNK = 16
PAD_FRONT = (-N_META) % CHUNK
CONV_DIM = D_MODEL
CONV_WIDTH = 31
D_FF = 2816
N_EXPERTS = 8
TOP_K = 2
D_FF_EXPERT = 3584
N_DENSE = (DEPTH + 1) // 2
N_MOE = DEPTH // 2
DN_ALPHA = (2 * DEPTH) ** 0.25
DN_BETA = (8 * DEPTH) ** -0.25
LN_EPS = 1e-5
RMS_EPS = 1e-6
OFF_Q = 0
OFF_F = OFF_Q + HG_DIM_K
OFF_I = OFF_F + HG_DIM_K
OFF_OG = OFF_I + HG_DIM_V
OFF_CV = OFF_OG + HG_DIM_V
OFF_CG = OFF_CV + CONV_DIM
OFF_GA = OFF_CG + CONV_DIM
OFF_GB = OFF_GA + D_MODEL
IN_COLS = OFF_GB + D_MODEL

kernel_name = "hgrn2_conformer_gated_moe_deepnorm"


def layer_norm(x, g, b):
    xf = x.astype(jnp.float32)
    mu = jnp.mean(xf, axis=-1, keepdims=True)
    var = jnp.mean(jnp.square(xf - mu), axis=-1, keepdims=True)
    y = (xf - mu) * lax.rsqrt(var + LN_EPS) * g.astype(jnp.float32) + b.astype(jnp.float32)
    return y.astype(x.dtype)


def hgrn2_recurrence(q_raw, f_raw, i_raw, lb):
    B, L, _ = q_raw.shape
    f32 = jnp.float32
    z = f_raw.astype(f32)
    lb = lb.astype(f32)
    q = jax.nn.silu(q_raw.astype(f32))
    has_lb = lb > 0
    lb_safe = jnp.where(has_lb, lb, 1.0)
    log_sz = jax.nn.log_sigmoid(z)
    log_f = jnp.where(has_lb, jnp.logaddexp(log_sz, jnp.log(lb_safe) + jax.nn.log_sigmoid(-z)), log_sz)
    k = (1.0 - lb) * jax.nn.sigmoid(-z)
    v = i_raw.astype(f32)
    pad = ((0, 0), (PAD_FRONT, 0), (0, 0))
    q, k, v, log_f = [jnp.pad(a, pad) for a in (q, k, v, log_f)]
    n = (L + PAD_FRONT) // CHUNK

    def to_chunks(a, d):
        return a.reshape(B, n, CHUNK, HG_HEADS, d).transpose(0, 3, 1, 2, 4)

    q, k, log_f = to_chunks(q, HG_DK), to_chunks(k, HG_DK), to_chunks(log_f, HG_DK)
    v = to_chunks(v, HG_DV)
    b = jnp.cumsum(log_f, axis=3)
    b_last = b[:, :, :, -1:, :]
    causal = jnp.tril(jnp.ones((CHUNK, CHUNK), dtype=bool))[:, :, None]
    diff = b[:, :, :, :, None, :] - b[:, :, :, None, :, :]
    decay = jnp.where(causal, jnp.exp(jnp.where(causal, diff, 0.0)), 0.0)
    att = jnp.einsum('bhntd,bhntsd,bhnsd->bhnts', q, decay, k)
    o_intra = jnp.einsum('bhnts,bhnsv->bhntv', att, v)
    k_end = k * jnp.exp(b_last - b)
    d_state = jnp.einsum('bhnsd,bhnsv->bhndv', k_end, v)
    chunk_decay = jnp.exp(b_last[:, :, :, 0, :])

    def step(state, inp):
        dec, ds = inp
        return dec[..., None] * state + ds, state

    s0 = jnp.zeros((B, HG_HEADS, HG_DK, HG_DV), f32)
    _, s_prev = lax.scan(step, s0, (jnp.moveaxis(chunk_decay, 2, 0), jnp.moveaxis(d_state, 2, 0)))
    s_prev = jnp.moveaxis(s_prev, 0, 2)
    o_inter = jnp.einsum('bhntd,bhndv->bhntv', q * jnp.exp(b), s_prev)
    o = (o_intra + o_inter).transpose(0, 2, 3, 1, 4).reshape(B, n * CHUNK, HG_HEADS, HG_DV)
    return o[:, PAD_FRONT:]


def conformer_conv(u_val, u_gate, conv_w, conv_b, ln_g, ln_b, w_pw):
    u = u_val * jax.nn.sigmoid(u_gate)
    y = lax.conv_general_dilated(
        u, conv_w[:, None, :], window_strides=(1,), padding=[(CONV_WIDTH - 1, 0)],
        dimension_numbers=('NWC', 'WIO', 'NWC'), feature_group_count=CONV_DIM) + conv_b
    y = jax.nn.silu(layer_norm(y, ln_g, ln_b))
    return y @ w_pw


def swiglu(x, w_gate, w_up, w_down):
    return (jax.nn.silu(x @ w_gate) * (x @ w_up)) @ w_down


def moe_swiglu(x, w_router, w_gate, w_up, w_down):
    B, L, D = x.shape
    t = x.reshape(B * L, D)
    logits = (t @ w_router).astype(jnp.float32)
    top_val, top_idx = lax.top_k(logits, TOP_K)
    top_w = jax.nn.softmax(top_val, axis=-1)
    gates = jnp.sum(jax.nn.one_hot(top_idx, N_EXPERTS, dtype=jnp.float32) * top_w[..., None], axis=1)
    gates = gates.astype(t.dtype)
    out = jnp.zeros_like(t)
    for e in range(N_EXPERTS):
        out = out + gates[:, e:e + 1] * swiglu(t, w_gate[e], w_up[e], w_down[e])
    return out.reshape(B, L, D)


def setup_inputs(seed: int = 0) -> dict:
    key = jax.random.key(seed)
    ks = jax.random.split(key, 32)
    nrm = lambda k, shape, scale: jax.random.normal(k, shape, jnp.float32) * scale
    gain = lambda k, shape: 1.0 + 0.02 * jax.random.normal(k, shape, jnp.float32)
    col_scale = jnp.concatenate([
        jnp.ones((2 * HG_DIM_K,), jnp.float32),
        jnp.full((HG_DIM_V,), DN_BETA, jnp.float32),
        jnp.ones((HG_DIM_V,), jnp.float32),
        jnp.full((CONV_DIM,), DN_BETA, jnp.float32),
        jnp.ones((CONV_DIM + 2 * D_MODEL,), jnp.float32)])
    return {
        "x": nrm(ks[0], (BATCH, SEQ, D_MODEL), 1.0),
        "meta_tokens": nrm(ks[1], (N_META, D_MODEL), 1.0),
        "ln_in_g": gain(ks[2], (D_MODEL,)),
        "ln_in_b": nrm(ks[3], (D_MODEL,), 0.02),
        "w_in": nrm(ks[4], (DEPTH, D_MODEL, IN_COLS), D_MODEL ** -0.5) * col_scale,
        "lower_bounds": nrm(ks[5], (DEPTH, HG_DIM_K), 0.5),
        "hg_norm_g": gain(ks[6], (DEPTH, HG_DIM_V)),
        "w_hg_out": nrm(ks[7], (DEPTH, HG_DIM_V, D_MODEL), HG_DIM_V ** -0.5 * DN_BETA),
        "conv_w": nrm(ks[8], (DEPTH, CONV_WIDTH, CONV_DIM), CONV_WIDTH ** -0.5),
        "conv_b": nrm(ks[9], (DEPTH, CONV_DIM), 0.02),
        "conv_ln_g": gain(ks[10], (DEPTH, CONV_DIM)),
        "conv_ln_b": nrm(ks[11], (DEPTH, CONV_DIM), 0.02),
        "w_conv_out": nrm(ks[12], (DEPTH, CONV_DIM, D_MODEL), CONV_DIM ** -0.5 * DN_BETA),
        "w_out": nrm(ks[13], (DEPTH, D_MODEL, D_MODEL), D_MODEL ** -0.5 * DN_BETA),
        "ln1_g": gain(ks[14], (DEPTH, D_MODEL)),
        "ln1_b": nrm(ks[15], (DEPTH, D_MODEL), 0.02),
        "ffn_w_gate": nrm(ks[16], (N_DENSE, D_MODEL, D_FF), D_MODEL ** -0.5 * DN_BETA),
        "ffn_w_up": nrm(ks[17], (N_DENSE, D_MODEL, D_FF), D_MODEL ** -0.5 * DN_BETA),
        "ffn_w_down": nrm(ks[18], (N_DENSE, D_FF, D_MODEL), D_FF ** -0.5 * DN_BETA),
        "moe_router": nrm(ks[19], (N_MOE, D_MODEL, N_EXPERTS), D_MODEL ** -0.5),
        "moe_w_gate": nrm(ks[20], (N_MOE, N_EXPERTS, D_MODEL, D_FF_EXPERT), D_MODEL ** -0.5 * DN_BETA),
        "moe_w_up": nrm(ks[21], (N_MOE, N_EXPERTS, D_MODEL, D_FF_EXPERT), D_MODEL ** -0.5 * DN_BETA),
        "moe_w_down": nrm(ks[22], (N_MOE, N_EXPERTS, D_FF_EXPERT, D_MODEL), D_FF_EXPERT ** -0.5 * DN_BETA),
        "ln2_g": gain(ks[23], (DEPTH, D_MODEL)),
        "ln2_b": nrm(ks[24], (DEPTH, D_MODEL), 0.02),
    }


def reference(x, meta_tokens, ln_in_g, ln_in_b, w_in, lower_bounds, hg_norm_g, w_hg_out,
              conv_w, conv_b, conv_ln_g, conv_ln_b, w_conv_out, w_out, ln1_g, ln1_b,
              ffn_w_gate, ffn_w_up, ffn_w_down, moe_router, moe_w_gate, moe_w_up, moe_w_down,
              ln2_g, ln2_b):
    B = x.shape[0]
    meta = jnp.broadcast_to(meta_tokens[None].astype(x.dtype), (B, N_META, D_MODEL))
    h = layer_norm(jnp.concatenate([meta, x], axis=1), ln_in_g, ln_in_b)
    L = h.shape[1]
    p = jax.nn.softmax(lower_bounds.astype(jnp.float32), axis=0)
    lb_all = jnp.cumsum(p, axis=0) - p[0]
    for l in range(DEPTH):
        proj = h @ w_in[l]
        o = hgrn2_recurrence(proj[..., OFF_Q:OFF_F], proj[..., OFF_F:OFF_I],
                             proj[..., OFF_I:OFF_OG], lb_all[l])
        o = o * lax.rsqrt(jnp.mean(jnp.square(o), axis=-1, keepdims=True) + RMS_EPS)
        o = o.reshape(B, L, HG_DIM_V) * hg_norm_g[l].astype(jnp.float32)
        o = (o * jax.nn.silu(proj[..., OFF_OG:OFF_CV].astype(jnp.float32))).astype(h.dtype)
        y_rec = o @ w_hg_out[l]
        y_conv = conformer_conv(proj[..., OFF_CV:OFF_CG], proj[..., OFF_CG:OFF_GA],
                                conv_w[l], conv_b[l], conv_ln_g[l], conv_ln_b[l], w_conv_out[l])
        mixed = (jax.nn.sigmoid(proj[..., OFF_GA:OFF_GB]) * y_rec
                 + jax.nn.sigmoid(proj[..., OFF_GB:IN_COLS]) * y_conv)
        h = layer_norm(DN_ALPHA * h + mixed @ w_out[l], ln1_g[l], ln1_b[l])
        if l % 2 == 0:
            j = l // 2
            f = swiglu(h, ffn_w_gate[j], ffn_w_up[j], ffn_w_down[j])
        else:
            j = l // 2
            f = moe_swiglu(h, moe_router[j], moe_w_gate[j], moe_w_up[j], moe_w_down[j])
        h = layer_norm(DN_ALPHA * h + f, ln2_g[l], ln2_b[l])
    return h[:, N_META:]
```

```python
import numpy as np
import threading
from contextlib import ExitStack
import concourse.bass as bass
import concourse.mybir as mybir
from concourse.bass_utils import run_bass_kernel_spmd

F32 = mybir.dt.float32
BF16 = mybir.dt.bfloat16
AF = mybir.ActivationFunctionType
ALU = mybir.AluOpType
AX = mybir.AxisListType

D = 1024
KC = 8
T = 2064
PRE = 16
NH = 8
DEPTH = 2
D_FF = 2816
N_EXP = 8
D_FFE = 3584
CW = 31
HALO = CW - 1
OFF_Q, OFF_F, OFF_I, OFF_OG, OFF_CV, OFF_CG, OFF_GA, OFF_GB = [i * 1024 for i in range(8)]
ALPHA = float((2 * DEPTH) ** 0.25)
LN_EPS = 1e-5
RMS_EPS = 1e-6
XW = 1024 + KC * HALO
CH = 128
NPE = 22


def blocks(n):
    out = [(0, PRE)]
    t = PRE
    while t < T:
        out.append((t, n))
        t += n
    return out


TILES = blocks(128)


class Buf:
    __slots__ = ("name", "w", "r")

    def __init__(self, name="b"):
        self.name = name
        self.w = None
        self.r = {}


class Sched:
    def __init__(self, nc, n_dma_sems=40):
        self.nc = nc
        self.eng = {"pe": nc.tensor, "act": nc.scalar, "dve": nc.vector, "pool": nc.gpsimd, "sp": nc.sync}
        self.sem = {k: nc.semaphore("s_" + k).__enter__() for k in self.eng}
        self.cnt = {k: 0 for k in self.eng}
        self.waited = {k: {} for k in self.eng}
        self.dma_sems = [nc.semaphore(f"s_dma{i}").__enter__() for i in range(n_dma_sems)]
        self.dma_cnt = [0] * n_dma_sems
        self.dma_rr = 0
        self.nwaits = 0
        self.ninstr = 0
        self.switch = None

    def _wait(self, e, dep, war=False):
        kind, key, val = dep
        if kind == "eng":
            if key == e and (war or e in ("pe", "sp")):
                return
            assert val <= self.cnt[key], (e, dep, self.cnt[key])
            sem = self.sem[key]
        else:
            sem = self.dma_sems[key]
        wk = (kind, key)
        if self.waited[e].get(wk, 0) >= val:
            return
        self.eng[e].wait_ge(sem, val)
        self.waited[e][wk] = val
        self.nwaits += 1

    def _deps(self, e, reads, writes):
        for b in reads:
            if b.w is not None:
                self._wait(e, b.w)
        for b in writes:
            if b.w is not None:
                self._wait(e, b.w)
            for d in b.r.values():
                self._wait(e, d, war=True)

    def _mark(self, tok, reads, writes):
        for b in reads:
            b.r[(tok[0], tok[1])] = tok
        for b in writes:
            b.w = tok
            b.r = {}

    def op(self, e, fn, reads=(), writes=(), signal=True):
        self._deps(e, reads, writes)
        ins = fn()
        self.ninstr += 1
        if signal or e != "pe":
            self.cnt[e] += 1
            ins.then_inc(self.sem[e], 1)
            tok = ("eng", e, self.cnt[e])
        else:
            tok = ("eng", e, self.cnt[e] + 1)
        self._mark(tok, reads, writes)
        if self.switch is not None:
            self.switch()
        return ins

    def dma(self, q, out, in_, reads=(), writes=(), **kw):
        self._deps(q, reads, writes)
        si = self.dma_rr
        self.dma_rr = (self.dma_rr + 1) % len(self.dma_sems)
        ins = self.eng[q].dma_start(out=out, in_=in_, **kw)
        self.dma_cnt[si] += 16
        ins.then_inc(self.dma_sems[si], 16)
        tok = ("dma", si, self.dma_cnt[si])
        self._mark(tok, reads, writes)
        self.ninstr += 1
        if self.switch is not None:
            self.switch()
        return tok

    def collective(self, fn, reads=(), writes=()):
        self._deps("pool", reads, writes)
        si = self.dma_rr
        self.dma_rr = (self.dma_rr + 1) % len(self.dma_sems)
        ins = fn()
        self.dma_cnt[si] += 1
        ins.then_inc(self.dma_sems[si])
        tok = ("dma", si, self.dma_cnt[si])
        self._mark(tok, reads, writes)
        return tok

    def barrier(self):
        if getattr(self, "on_barrier", None):
            self.on_barrier()
        for e in self.eng:
            for e2 in self.eng:
                if e2 != e and self.cnt[e2] > 0:
                    self._wait(e, ("eng", e2, self.cnt[e2]))
            for si, c in enumerate(self.dma_cnt):
                if c > 0:
                    self._wait(e, ("dma", si, c))


class Interleaver:
    def __init__(self, sched):
        self.S = sched

    def run(self, fns):
        if len(fns) == 1:
            fns[0]()
            return
        n = len(fns)
        self.ev = [threading.Event() for _ in range(n)]
        self.alive = [True] * n
        self.tid = {}
        errs = []

        def worker(i):
            self.ev[i].wait()
            self.ev[i].clear()
            try:
                fns[i]()
            except BaseException as e:
                errs.append(e)
            self.alive[i] = False
            j = self._next(i)
            if j is not None:
                self.ev[j].set()

        ths = [threading.Thread(target=worker, args=(i,)) for i in range(n)]
        for i, t in enumerate(ths):
            t.start()
            self.tid[t.ident] = i
        self.S.switch = self._switch
        self.ev[0].set()
        for t in ths:
            t.join()
        self.S.switch = None
        if errs:
            raise errs[0]

    def _next(self, i):
        n = len(self.alive)
        for d in range(1, n):
            j = (i + d) % n
            if self.alive[j]:
                return j
        return None

    def _switch(self):
        i = self.tid.get(threading.get_ident())
        if i is None:
            return
        j = self._next(i)
        if j is None:
            return
        self.ev[j].set()
        self.ev[i].wait()
        self.ev[i].clear()


class Ring:
    def __init__(self, tiles):
        self.items = [(t, Buf()) for t in tiles]
        self.i = 0

    def next(self):
        it = self.items[self.i]
        self.i = (self.i + 1) % len(self.items)
        return it


class Prog:
    def __init__(self, stop_after=None, debug=False):
        self.stop_after = stop_after
        self.debug = debug
        nc = bass.Bass("TRN2", target_bir_lowering=False)
        self.nc = nc
        self.S = Sched(nc)
        self.IL = Interleaver(self.S)
        self.wq = []
        self.S.on_barrier = lambda: self.wq.clear()
        self.es = ExitStack()
        self.build()

    def din(self, name, shape, dt=F32):
        return self.nc.dram_tensor(name, list(shape), dt, kind="ExternalInput").ap()

    def dscr(self, name, shape, dt=F32, cc=False):
        if self.debug and not cc:
            return self.nc.dram_tensor(name, list(shape), dt, kind="ExternalOutput").ap()
        return self.nc.dram_tensor(name, list(shape), dt).ap()

    def sb(self, es, name, shape, dt):
        self._uid = getattr(self, "_uid", 0) + 1
        return es.enter_context(self.nc.sbuf_tensor(f"sb{self._uid}_{name}", list(shape), dt))

    def ring(self, es, name, shape, dt, n):
        return Ring([self.sb(es, f"{name}{i}", shape, dt) for i in range(n)])

    def ps(self):
        return self.psr.next()

    def load_w(self, es, name, src, ncols, eng="pool", piece=512):
        t = self.sb(es, name, [128, KC, ncols], BF16)
        bufs = []
        for c0 in range(0, ncols, piece):
            b = Buf(name)
            prev = []
            self.S.dma(eng, t[:, :, c0:c0 + piece],
                       src[:, c0:c0 + piece].rearrange("(kc p) n -> p kc n", p=128), reads=prev, writes=[b])
            self.wq.append(b)
            bufs.append(b)
        return t, bufs, piece

    def build(self):
        nc, S = self.nc, self.S
        es = self.es
        I = {}
        I["xin"] = self.din("xin", [T, D])
        I["flag"] = self.din("flag", [128, 1])
        I["ln_in_g"] = self.din("ln_in_g", [1, D])
        I["ln_in_b"] = self.din("ln_in_b", [1, D])
        I["w_in"] = self.din("w_in", [DEPTH, D, 8 * D])
        I["lower_bounds"] = self.din("lower_bounds", [DEPTH, 128, KC])
        I["hg_norm_g"] = self.din("hg_norm_g", [DEPTH, 128, KC])
        I["w_hg_out"] = self.din("w_hg_out", [DEPTH, D, D])
        I["conv_w"] = self.din("conv_w", [DEPTH, 128, KC, CW])
        I["conv_b"] = self.din("conv_b", [DEPTH, 128, KC])
        I["conv_ln_g"] = self.din("conv_ln_g", [DEPTH, 128, KC])
        I["conv_ln_b"] = self.din("conv_ln_b", [DEPTH, 128, KC])
        I["w_conv_out"] = self.din("w_conv_out", [DEPTH, D, D])
        I["w_out"] = self.din("w_out", [DEPTH, D, D])
        I["ln1_g"] = self.din("ln1_g", [DEPTH, D])
        I["ln1_b"] = self.din("ln1_b", [DEPTH, D])
        I["ln2_g"] = self.din("ln2_g", [DEPTH, D])
        I["ln2_b"] = self.din("ln2_b", [DEPTH, D])
        full = self.stop_after is None
        if full or self.stop_after >= 5:
            I["ffn_w_gate"] = self.din("ffn_w_gate", [1, D, D_FF])
            I["ffn_w_up"] = self.din("ffn_w_up", [1, D, D_FF])
            I["ffn_w_down"] = self.din("ffn_w_down", [1, D_FF, D])
        if full or self.stop_after >= 10:
            I["moe_router"] = self.din("moe_router", [1, 128, KC, N_EXP])
            I["moe_w_gate"] = self.din("moe_w_gate", [1, N_EXP, D, D_FFE])
            I["moe_w_up"] = self.din("moe_w_up", [1, N_EXP, D, D_FFE])
            I["moe_w_down"] = self.din("moe_w_down", [1, N_EXP, D_FFE, D])
        self.I = I
        self.y = nc.dram_tensor("y", [T, D], F32, kind="ExternalOutput").ap()
        self.h_scr = self.dscr("h_scr", [T, D])
        self.on_scr = self.dscr("on_scr", [KC, 128, T], BF16)
        self.yc_scr = self.dscr("yc_scr", [KC, 128, T], BF16)
        self.cc_in = self.dscr("cc_in", [128, XW], cc=True)
        self.cc_out = self.dscr("cc_out", [256, XW], cc=True)
        self.b_hscr = Buf("h_scr")
        self.b_on = Buf("on_scr")
        self.b_yc = Buf("yc_scr")
        self.b_y = Buf("y")

        self.hT = self.sb(es, "hT", [128, KC, T], BF16)
        self.b_hT = [Buf(f"hT{i}") for i in range(len(TILES))]
        self.ident = self.sb(es, "ident", [128, 128], BF16)
        self.identf = self.sb(es, "identf", [128, 128], F32)
        self.onesf = self.sb(es, "onesf", [128, 128], F32)
        self.maskA = self.sb(es, "maskA", [128, 128], F32)
        self.maskA4 = self.sb(es, "maskA4", [128, 4, 128], F32)
        self.neghalf = self.sb(es, "neghalf", [128, 1], F32)
        self.flag = self.sb(es, "flag", [128, 1], F32)
        self.lbt = self.sb(es, "lbt", [128, DEPTH, KC], F32)
        self.lnoml = self.sb(es, "lnoml", [128, DEPTH, KC], F32)
        self.Sst = self.sb(es, "Sst", [128, D], F32)
        self.halo0 = self.sb(es, "halo0", [128, KC, HALO], BF16)
        self.gates = self.sb(es, "gates", [128, len(TILES), N_EXP], F32)
        self.b_const = Buf("const")
        self.b_S = Buf("S")
        self.b_Sh = [Buf(f"S{h}") for h in range(NH)]
        self.b_halo0 = Buf("halo0")
        self.b_gates = Buf("gates")
        self.psr = Ring([es.enter_context(nc.psum_tensor(f"ps{i}", [128, 512], F32)) for i in range(4)])
        self.pshold = Ring([es.enter_context(nc.psum_tensor(f"psh{i}", [128, 512], F32)) for i in range(2)])
        self.pstr = Ring([es.enter_context(nc.psum_tensor(f"pst{i}", [128, 1024], BF16)) for i in range(2)])

        self.psr4 = self.psr
        self.psr6 = Ring([])
        self.psr6.items = self.psr.items + self.pshold.items
        self.psA = Ring([])
        self.psA.items = self.psr.items[0:2]
        self.psB = Ring([])
        self.psB.items = self.psr.items[2:4]
        self.setup_consts()
        self.wes = ExitStack()
        hw = self.load_hgrn_weights(self.wes, 0)
        self.phase_ln_in()
        if self.stop_after == 0:
            return self.finish()
        for l in range(DEPTH):
            base = 5 * l
            if l > 0:
                self.wes = ExitStack()
                hw = self.load_hgrn_weights(self.wes, l)
            self.hw = hw
            self.phase_hgrn(l, state_only=True)
            self.exchange()
            if self.stop_after == base + 1:
                return self.finish()
            self.phase_hgrn(l, state_only=False)
            self.wes.close()
            self.wes = None
            if self.stop_after == base + 2:
                return self.finish()
            self.phase_conv(l)
            if self.stop_after == base + 3:
                return self.finish()
            self.phase_mix(l)
            if self.stop_after == base + 4:
                return self.finish()
            if l % 2 == 0:
                self.phase_ffn(l, [(I["ffn_w_gate"][0], I["ffn_w_up"][0], I["ffn_w_down"][0])], D_FF, moe=False)
            else:
                ex = [(I["moe_w_gate"][0, e], I["moe_w_up"][0, e], I["moe_w_down"][0, e]) for e in range(N_EXP)]
                self.phase_ffn(l, ex, D_FFE, moe=True)
            if self.stop_after == base + 5:
                return self.finish()
        self.finish()

    def finish(self):
        S = self.S
        S.barrier()
        if getattr(self, "wes", None) is not None:
            self.wes.close()
            self.wes = None
        self.es.close()

    def load_hgrn_weights(self, es, l):
        W = self.I["w_in"][l]
        wF = self.load_w(es, "wF", W[:, OFF_F:OFF_F + D], D)
        wI = self.load_w(es, "wI", W[:, OFF_I:OFF_I + D], D)
        wQ = self.load_w(es, "wQ", W[:, OFF_Q:OFF_Q + D], D)
        return dict(F=wF, I=wI, Q=wQ)

    def setup_consts(self):
        nc, S, I = self.nc, self.S, self.I
        bc = self.b_const
        S.op("pool", lambda: nc.gpsimd.memset(self.ident[:], 0.0), writes=[bc])
        S.op("pool", lambda: nc.gpsimd.affine_select(out=self.ident[:], in_=self.ident[:], pattern=[[-1, 128]],
                                                     compare_op=ALU.not_equal, fill=1.0, base=0, channel_multiplier=1),
             reads=[bc], writes=[bc])
        S.op("pool", lambda: nc.gpsimd.memset(self.identf[:], 0.0), writes=[bc])
        S.op("pool", lambda: nc.gpsimd.affine_select(out=self.identf[:], in_=self.identf[:], pattern=[[-1, 128]],
                                                     compare_op=ALU.not_equal, fill=1.0, base=0, channel_multiplier=1),
             reads=[bc], writes=[bc])
        S.op("dve", lambda: nc.vector.memset(self.onesf[:], 1.0), writes=[bc])
        S.op("dve", lambda: nc.vector.memset(self.neghalf[:], -0.5), writes=[bc])
        S.op("pool", lambda: nc.gpsimd.memset(self.maskA[:], 1.0), writes=[bc])
        S.op("pool", lambda: nc.gpsimd.affine_select(out=self.maskA[:], in_=self.maskA[:], pattern=[[1, 128]],
                                                     compare_op=ALU.is_ge, fill=0.0, base=0, channel_multiplier=-1),
             reads=[bc], writes=[bc])
        if CH == 64:
            S.op("pool", lambda: nc.gpsimd.memset(self.maskA[0:64, 64:128], 0.0), writes=[bc])
        for q in range(4):
            S.op("pool", lambda q=q: nc.gpsimd.tensor_copy(out=self.maskA4[:, q, :], in_=self.maskA[:, :]), reads=[bc], writes=[bc])
        S.dma("sp", self.flag[:], I["flag"][:, :], writes=[bc])
        S.op("dve", lambda: nc.vector.memset(self.Sst[:], 0.0), writes=[self.b_S])
        S.op("dve", lambda: nc.vector.memset(self.halo0[:], 0.0), writes=[self.b_halo0])
        S.op("dve", lambda: nc.vector.memset(self.gates[:], 0.0), writes=[self.b_gates])
        with ExitStack() as es:
            a = self.sb(es, "lb_a", [128, DEPTH, KC], F32)
            e = self.sb(es, "lb_e", [128, DEPTH, KC], F32)
            m = self.sb(es, "lb_m", [128, KC], F32)
            s = self.sb(es, "lb_s", [128, KC], F32)
            cum = self.sb(es, "lb_c", [128, KC], F32)
            b = Buf("lbtmp")
            for l in range(DEPTH):
                S.dma("sp", a[:, l, :], I["lower_bounds"][l], writes=[b])
            S.op("dve", lambda: nc.vector.tensor_copy(out=m[:], in_=a[:, 0, :]), reads=[b], writes=[b])
            for l in range(1, DEPTH):
                S.op("dve", lambda l=l: nc.vector.tensor_tensor(out=m[:], in0=m[:], in1=a[:, l, :], op=ALU.max), reads=[b], writes=[b])
            for l in range(DEPTH):
                S.op("dve", lambda l=l: nc.vector.tensor_tensor(out=a[:, l, :], in0=a[:, l, :], in1=m[:], op=ALU.subtract), reads=[b], writes=[b])
            S.op("act", lambda: nc.scalar.activation(out=e[:], in_=a[:], func=AF.Exp), reads=[b], writes=[b])
            S.op("dve", lambda: nc.vector.tensor_copy(out=s[:], in_=e[:, 0, :]), reads=[b], writes=[b])
            for l in range(1, DEPTH):
                S.op("dve", lambda l=l: nc.vector.tensor_tensor(out=s[:], in0=s[:], in1=e[:, l, :], op=ALU.add), reads=[b], writes=[b])
            S.op("dve", lambda: nc.vector.reciprocal(out=s[:], in_=s[:]), reads=[b], writes=[b])
            for l in range(DEPTH):
                S.op("dve", lambda l=l: nc.vector.tensor_tensor(out=e[:, l, :], in0=e[:, l, :], in1=s[:], op=ALU.mult), reads=[b], writes=[b])
            S.op("dve", lambda: nc.vector.memset(cum[:], 0.0), writes=[b])
            for l in range(DEPTH):
                S.op("dve", lambda l=l: nc.vector.tensor_tensor(out=cum[:], in0=cum[:], in1=e[:, l, :], op=ALU.add), reads=[b], writes=[b])
                S.op("dve", lambda l=l: nc.vector.tensor_tensor(out=self.lbt[:, l, :], in0=cum[:], in1=e[:, 0, :], op=ALU.subtract), reads=[b], writes=[bc])
            S.op("act", lambda: nc.scalar.activation(out=self.lnoml[:], in_=self.lbt[:], func=AF.Ln, scale=-1.0, bias=1.0),
                 reads=[bc], writes=[bc])
            S.barrier()

    def ln_tile(self, es_bufs, src, n, g_t, b_t, ti, out_h, out_scale_h=None):
        raise NotImplementedError

    def layer_norm_tile(self, sc, x_ap, xbuf, n, g_t, b_t, gb_buf, out_ap, out_buf, eps=LN_EPS):
        nc, S = self.nc, self.S
        st, mv, rstd, b = sc["stats"], sc["mv"], sc["rstd"], sc["b"]
        for hh in range(2):
            S.op("dve", lambda hh=hh: nc.vector.bn_stats(out=st[0:n, hh, :], in_=x_ap[:, hh * 512:(hh + 1) * 512]),
                 reads=[xbuf], writes=[b])
        S.op("dve", lambda: nc.vector.bn_aggr(out=mv[0:n, :], in_=st[0:n, :, :]), reads=[b], writes=[b])
        S.op("dve", lambda: nc.vector.tensor_scalar(out=rstd[0:n, :], in0=mv[0:n, 1:2], scalar1=eps, scalar2=None, op0=ALU.add),
             reads=[b], writes=[b])
        S.op("pool", lambda: nc.gpsimd.tensor_tensor(out=rstd[0:n, :], in0=rstd[0:n, :], in1=self.neghalf[0:n, 0:1], op=ALU.pow),
             reads=[b, self.b_const], writes=[b])
        S.op("dve", lambda: nc.vector.tensor_scalar(out=out_ap, in0=x_ap, scalar1=mv[0:n, 0:1], scalar2=rstd[0:n, 0:1],
                                                    op0=ALU.subtract, op1=ALU.mult),
             reads=[xbuf, b], writes=[out_buf])
        S.op("dve", lambda: nc.vector.tensor_tensor(out=out_ap, in0=out_ap, in1=g_t[0:n, :], op=ALU.mult),
             reads=[gb_buf], writes=[out_buf])
        S.op("dve", lambda: nc.vector.tensor_tensor(out=out_ap, in0=out_ap, in1=b_t[0:n, :], op=ALU.add),
             reads=[gb_buf], writes=[out_buf])

    def to_hT(self, h_ap, hbuf, n, ti, hb_ap, hb_buf):
        nc, S = self.nc, self.S
        t0, _ = TILES[ti]
        S.op("act", lambda: nc.scalar.copy(out=hb_ap, in_=h_ap), reads=[hbuf], writes=[hb_buf])
        pt, pb = self.pstr.next()
        for kc in range(KC):
            S.op("pe", lambda kc=kc: nc.tensor.transpose(pt[:, kc * 128:kc * 128 + n], hb_ap[:, kc * 128:(kc + 1) * 128], self.ident[0:n, 0:n]),
                 reads=[hb_buf, self.b_const], writes=[pb], signal=(kc == KC - 1))
        S.op("dve", lambda: nc.vector.tensor_copy(out=self.hT[:, :, t0:t0 + n],
                                                  in_=pt[:].rearrange("p (k t) -> p k t", t=128)[:, :, 0:n]),
             reads=[pb], writes=[self.b_hT[ti]])

    def load_bcast(self, es, name, src_row):
        t = self.sb(es, name, [128, D], F32)
        return t

    def ln_scratch(self, es, tag):
        return {"stats": self.sb(es, tag + "_st", [128, 2, 6], F32), "mv": self.sb(es, tag + "_mv", [128, 2], F32),
                "rstd": self.sb(es, tag + "_rs", [128, 1], F32), "b": Buf(tag)}

    def phase_ln_in(self):
        nc, S, I = self.nc, self.S, self.I
        with ExitStack() as es:
            g_t = self.sb(es, "lni_g", [128, D], F32)
            b_t = self.sb(es, "lni_b", [128, D], F32)
            gb = Buf("lni_gb")
            xr = self.ring(es, "lni_x", [128, D], F32, 4)
            hr = self.ring(es, "lni_h", [128, D], F32, 2)
            ar = self.ring(es, "lni_a", [128, D], F32, 2)
            hbr = self.ring(es, "lni_hb", [128, D], BF16, 2)
            sc = [self.ln_scratch(es, f"lni{i}") for i in range(2)]
            loaded = {}

            def load(ti):
                t0, n = TILES[ti]
                xt, xb = xr.next()
                S.dma("sp", xt[0:n, :], I["xin"][t0:t0 + n, :], writes=[xb])
                loaded[ti] = (xt, xb)

            load(0)
            load(1)
            S.dma("sp", g_t[:], I["ln_in_g"][0:1, :].to_broadcast([128, D]), writes=[gb])
            S.dma("sp", b_t[:], I["ln_in_b"][0:1, :].to_broadcast([128, D]), writes=[gb])

            def tile_body(ti):
                t0, n = TILES[ti]
                xt, xb = loaded.pop(ti)
                ht, hb = hr.next()
                self.layer_norm_tile(sc[ti % 2], xt[0:n, :], xb, n, g_t, b_t, gb, ht[0:n, :], hb)
                if ti == 0:
                    S.op("dve", lambda: nc.vector.tensor_scalar(out=ht[0:n, :], in0=ht[0:n, :], scalar1=self.flag[0:n, 0:1],
                                                                scalar2=None, op0=ALU.mult), reads=[self.b_const], writes=[hb])
                at, ab = ar.next()
                S.op("act", lambda: nc.scalar.mul(out=at[0:n, :], in_=ht[0:n, :], mul=ALPHA), reads=[hb], writes=[ab])
                S.dma("pool", self.h_scr[t0:t0 + n, :], at[0:n, :], reads=[ab], writes=[self.b_hscr])
                hbt, hbb = hbr.next()
                self.to_hT(ht[0:n, :], hb, n, ti, hbt[0:n, :], hbb)

            for p0 in range(0, len(TILES), 2):
                pair = [q for q in (p0, p0 + 1) if q < len(TILES)]
                for q in pair:
                    if q + 2 < len(TILES):
                        load(q + 2)
                self.IL.run([lambda q=q: tile_body(q) for q in pair])
            S.barrier()

    def phase_hgrn(self, l, state_only):
        nc, S, I = self.nc, self.S, self.I
        self.psr = self.psr4
        W = I["w_in"][l]
        BL = blocks(512)
        with ExitStack() as es:
            wF, wFb, pc = self.hw["F"]
            wI, wIb, _ = self.hw["I"]
            wQ, wQb, _ = self.hw["Q"]
            if state_only:
                wCV, wCVb, _ = self.load_w(es, "wCV", W[:, OFF_CV:OFF_CV + D], D)
                wCG, wCGb, _ = self.load_w(es, "wCG", W[:, OFF_CG:OFF_CG + D], D)
            tr = self.ring(es, "hg_t", [128, 512], F32, 14)
            self.mask_sc = self.sb(es, "mask_sc", [128, 512], F32)
            S.op("dve", lambda: nc.vector.memset(self.mask_sc[:], 1.0), writes=[self.b_const])
            S.op("dve", lambda: nc.vector.memset(self.mask_sc[:].rearrange("p (c t) -> p c t", t=CH)[:, :, 0:1], 0.0),
                 writes=[self.b_const])
            ktT = self.sb(es, "ktT", [128, NH, 512], BF16)
            b_kt = [Buf(f"kt{h}") for h in range(NH)]
            if not state_only:
                qtT = self.sb(es, "qtT", [128, NH, 512], BF16)
                b_qt = [Buf(f"qt{h}") for h in range(NH)]
                on_sb = self.ring(es, "on_sb", [128, NH, 128], BF16, 2)
                att_r = self.ring(es, "att", [128, NH, 128], BF16, 2)
                cl_r = self.ring(es, "attcl", [128, 4, 128], F32, 2)
                sq_r = self.ring(es, "sq", [128, NH, 128], F32, 2)
                rs_r = self.ring(es, "rs", [128, NH, 128], F32, 2)
                St_r = self.ring(es, "St", [128, D], BF16, 3)
            dec = self.sb(es, "dec", [128, NH, 8], F32)
            edl = self.sb(es, "edl", [128, NH, 8], F32)
            ebr = self.sb(es, "ebr", [128, NH, 8], F32)
            b_sc = Buf("chunk_scalars")
            v_r = self.ring(es, "v_sb", [128, D], BF16, 3)
            ktm_r = self.ring(es, "ktm", [128, D], BF16, 3)
            lb = self.lbt[:, l, :]
            lno = self.lnoml[:, l, :]
            bc = self.b_const

            for bi, (t0, n) in enumerate(BL):
                C = 16 if n == 16 else CH
                nch = n // C
                mid, last = C // 2 - 1, C - 1
                tiles = [(t0 + o, min(128, n - o)) for o in range(0, n, 128)]
                tis = [TILES.index(tt) for tt in tiles]
                hbufs = [self.b_hT[ti] for ti in tis]
                def head_body(h):
                    zp, zb = self.ps()
                    for kc in range(KC):
                        S.op("pe", lambda kc=kc: nc.tensor.matmul(zp[:, 0:n], lhsT=wF[:, kc, h * 128:(h + 1) * 128], rhs=self.hT[:, kc, t0:t0 + n],
                                                                  start=(kc == 0), stop=(kc == KC - 1)),
                             reads=hbufs + [wFb[(h * 128) // pc]], writes=[zb], signal=(kc == KC - 1))
                    E, Eb = tr.next()
                    L1, L1b = tr.next()
                    L2, L2b = tr.next()
                    lsn, lsnb = tr.next()
                    S.op("act", lambda: nc.scalar.activation(out=E[:, 0:n], in_=zp[:, 0:n], func=AF.Exp, scale=-1.0), reads=[zb], writes=[Eb])
                    S.op("act", lambda: nc.scalar.activation(out=L1[:, 0:n], in_=E[:, 0:n], func=AF.Ln, bias=1.0), reads=[Eb], writes=[L1b])
                    S.op("act", lambda: nc.scalar.activation(out=L2[:, 0:n], in_=E[:, 0:n], func=AF.Ln, scale=lb[:, h:h + 1], bias=1.0),
                         reads=[Eb, bc], writes=[L2b])
                    S.op("dve", lambda: nc.vector.scalar_tensor_tensor(out=lsn[:, 0:n], in0=zp[:, 0:n], scalar=-1.0, in1=L1[:, 0:n],
                                                                       op0=ALU.mult, op1=ALU.subtract), reads=[zb, L1b], writes=[lsnb])
                    S.op("dve", lambda: nc.vector.tensor_tensor(out=L2[:, 0:n], in0=L2[:, 0:n], in1=L1[:, 0:n], op=ALU.subtract),
                         reads=[L1b], writes=[L2b])
                    bt, bb = tr.next()
                    S.op("dve", lambda: nc.vector.tensor_tensor_scan(out=bt[:, 0:n], data0=self.mask_sc[:, 0:n], data1=L2[:, 0:n], initial=0.0,
                                                                     op0=ALU.mult, op1=ALU.add), reads=[L2b, bc], writes=[bb])
                    b3 = bt[:, 0:n].rearrange("p (c t) -> p c t", t=C)
                    df, dfb = tr.next()
                    S.op("dve", lambda: nc.vector.tensor_tensor(out=df[:, 0:n].rearrange("p (c t) -> p c t", t=C), in0=b3,
                                                                in1=b3[:, :, mid:mid + 1].to_broadcast([128, nch, C]), op=ALU.subtract),
                         reads=[bb], writes=[dfb])
                    d3 = df[:, 0:n].rearrange("p (c t) -> p c t", t=C)
                    S.op("act", lambda: nc.scalar.activation(out=dec[:, h, 0:nch], in_=b3[:, :, last], func=AF.Exp), reads=[bb], writes=[b_sc])
                    S.op("act", lambda: nc.scalar.activation(out=ebr[:, h, 0:nch], in_=b3[:, :, mid], func=AF.Exp), reads=[bb], writes=[b_sc])
                    S.op("act", lambda: nc.scalar.activation(out=edl[:, h, 0:nch], in_=d3[:, :, last], func=AF.Exp), reads=[dfb], writes=[b_sc])
                    S.op("dve", lambda: nc.vector.tensor_tensor(out=lsn[:, 0:n], in0=lsn[:, 0:n], in1=df[:, 0:n], op=ALU.subtract),
                         reads=[dfb], writes=[lsnb])
                    S.op("act", lambda: nc.scalar.activation(out=ktT[:, h, 0:n], in_=lsn[:, 0:n], func=AF.Exp, bias=lno[:, h:h + 1]),
                         reads=[lsnb, bc], writes=[b_kt[h]])
                    if not state_only:
                        qp, qb = self.ps()
                        for kc in range(KC):
                            S.op("pe", lambda kc=kc: nc.tensor.matmul(qp[:, 0:n], lhsT=wQ[:, kc, h * 128:(h + 1) * 128], rhs=self.hT[:, kc, t0:t0 + n],
                                                                      start=(kc == 0), stop=(kc == KC - 1)),
                                 reads=hbufs + [wQb[(h * 128) // pc]], writes=[qb], signal=(kc == KC - 1))
                        Eq, Eqb = tr.next()
                        S.op("act", lambda: nc.scalar.activation(out=Eq[:, 0:n], in_=qp[:, 0:n], func=AF.Exp, scale=-1.0), reads=[qb], writes=[Eqb])
                        S.op("act", lambda: nc.scalar.activation(out=Eq[:, 0:n], in_=Eq[:, 0:n], func=AF.Ln, bias=1.0), reads=[Eqb], writes=[Eqb])
                        S.op("dve", lambda: nc.vector.tensor_tensor(out=Eq[:, 0:n], in0=df[:, 0:n], in1=Eq[:, 0:n], op=ALU.subtract),
                             reads=[dfb], writes=[Eqb])
                        S.op("act", lambda: nc.scalar.activation(out=Eq[:, 0:n], in_=Eq[:, 0:n], func=AF.Exp), reads=[Eqb], writes=[Eqb])
                        S.op("dve", lambda: nc.vector.tensor_tensor(out=qtT[:, h, 0:n], in0=qp[:, 0:n], in1=Eq[:, 0:n], op=ALU.mult),
                             reads=[qb, Eqb], writes=[b_qt[h]])
                for h0 in range(0, NH, 2):
                    self.IL.run([lambda h=h0: head_body(h), lambda h=h0 + 1: head_body(h)])
                psA, psB = self.psA, self.psB

                def stage1(tt0, tn, ti):
                    o0 = tt0 - t0
                    r = {}
                    vt, vb = v_r.next()
                    for hh in range(2):
                        vp, vpb = psA.next()
                        for kc in range(KC):
                            S.op("pe", lambda kc=kc: nc.tensor.matmul(vp[0:tn, :], lhsT=self.hT[:, kc, tt0:tt0 + tn], rhs=wI[:, kc, hh * 512:(hh + 1) * 512],
                                                                      start=(kc == 0), stop=(kc == KC - 1)),
                                 reads=[self.b_hT[ti], wIb[(hh * 512) // pc]], writes=[vpb], signal=(kc == KC - 1))
                        S.op("act", lambda: nc.scalar.copy(out=vt[0:tn, hh * 512:(hh + 1) * 512], in_=vp[0:tn, :]), reads=[vpb], writes=[vb])
                    pt, ptb = self.pstr.next()
                    for h in range(NH):
                        S.op("pe", lambda h=h: nc.tensor.transpose(pt[0:tn, h * 128:(h + 1) * 128], ktT[:, h, o0:o0 + tn], self.ident[:, :]),
                             reads=[b_kt[h], bc], writes=[ptb], signal=(h == NH - 1))
                    kt, kb = ktm_r.next()
                    S.op("dve", lambda: nc.vector.tensor_copy(out=kt[0:tn, :], in_=pt[0:tn, :]), reads=[ptb], writes=[kb])
                    r.update(vt=vt, vb=vb, kt=kt, kb=kb)
                    if not state_only:
                        at, atb = att_r.next()
                        for half in range(2):
                            ap_, apb = psA.next()
                            for hq in range(4):
                                h = half * 4 + hq
                                S.op("pe", lambda h=h, hq=hq: nc.tensor.matmul(ap_[0:tn, hq * 128:hq * 128 + tn], lhsT=ktT[:, h, o0:o0 + tn],
                                                                               rhs=qtT[:, h, o0:o0 + tn], start=True, stop=True),
                                     reads=[b_kt[h], b_qt[h]], writes=[apb], signal=(hq == 3))
                            cl, clb = cl_r.next()
                            S.op("dve", lambda half=half: nc.vector.tensor_scalar(
                                out=cl[0:tn, :, 0:tn], in0=ap_[0:tn, :].rearrange("p (h t) -> p h t", t=128)[:, :, 0:tn],
                                scalar1=1e30, scalar2=-1e30, op0=ALU.min, op1=ALU.max), reads=[apb], writes=[clb])
                            S.op("dve", lambda half=half: nc.vector.tensor_tensor(
                                out=at[0:tn, half * 4:half * 4 + 4, 0:tn], in0=cl[0:tn, :, 0:tn],
                                in1=self.maskA4[0:tn, :, 0:tn], op=ALU.mult),
                                reads=[clb, bc], writes=[atb])
                        r.update(at=at, atb=atb)
                    return r

                def stage2(tt0, tn, ti, r):
                    o0 = tt0 - t0
                    vt, vb, kt, kb = r["vt"], r["vb"], r["kt"], r["kb"]
                    gch = o0 // C
                    c0, cn = 0, tn
                    if not state_only:
                        at, atb = r["at"], r["atb"]
                        ont, onb = on_sb.next()
                        sqt, sqb = sq_r.next()
                        opl = []
                        Stt, Stb = St_r.next()
                        for h in range(NH):
                            S.op("act", lambda h=h: nc.scalar.activation(out=Stt[:, h * 128:(h + 1) * 128], in_=self.Sst[:, h * 128:(h + 1) * 128],
                                                                         func=AF.Identity, scale=ebr[:, h, gch:gch + 1]),
                                 reads=[self.b_S, self.b_Sh[h], b_sc], writes=[Stb])
                        for half in range(2):
                            op_, opb = self.pshold.next()
                            opl.append((op_, opb))
                            for hq in range(4):
                                h = half * 4 + hq
                                S.op("pe", lambda h=h, hq=hq, op_=op_: nc.tensor.matmul(op_[:, hq * 128:hq * 128 + tn], lhsT=vt[0:tn, h * 128:(h + 1) * 128],
                                                                                     rhs=at[0:tn, h, 0:tn], start=(hq == 0), stop=False),
                                     reads=[vb, atb], writes=[opb], signal=False)
                        for half in range(2):
                            op_, opb = opl[half]
                            for hq in range(4):
                                h = half * 4 + hq
                                S.op("pe", lambda h=h, hq=hq, op_=op_: nc.tensor.matmul(op_[:, hq * 128:hq * 128 + tn], lhsT=Stt[:, h * 128:(h + 1) * 128],
                                                                                     rhs=qtT[:, h, o0:o0 + tn], start=False, stop=True),
                                     reads=[Stb, b_qt[h]], writes=[opb], signal=(hq == 3))
                    for half in range(2):
                        mp, mpb = psB.next()
                        for hq in range(4):
                            h = half * 4 + hq
                            S.op("pe", lambda h=h, hq=hq: nc.tensor.matmul(mp[:, hq * 128:(hq + 1) * 128], lhsT=kt[c0:c0 + cn, h * 128:(h + 1) * 128],
                                                                           rhs=vt[c0:c0 + cn, h * 128:(h + 1) * 128], start=True, stop=True),
                                 reads=[kb, vb], writes=[mpb], signal=(hq == 3))
                        for hq in range(4):
                            h = half * 4 + hq
                            S.op("dve", lambda h=h: nc.vector.tensor_scalar(out=self.Sst[:, h * 128:(h + 1) * 128], in0=self.Sst[:, h * 128:(h + 1) * 128],
                                                                            scalar1=dec[:, h, gch:gch + 1], scalar2=None, op0=ALU.mult),
                                 reads=[b_sc, self.b_S], writes=[self.b_Sh[h]])
                            S.op("dve", lambda h=h, hq=hq: nc.vector.scalar_tensor_tensor(out=self.Sst[:, h * 128:(h + 1) * 128], in0=mp[:, hq * 128:(hq + 1) * 128],
                                                                                          scalar=edl[:, h, gch:gch + 1], in1=self.Sst[:, h * 128:(h + 1) * 128],
                                                                                          op0=ALU.mult, op1=ALU.add),
                                 reads=[mpb, b_sc], writes=[self.b_Sh[h]])
                    if not state_only:
                        for half in range(2):
                            op_, opb = opl[half]
                            S.op("act", lambda half=half, op_=op_: nc.scalar.activation(
                                out=sqt[:, half * 4:half * 4 + 4, 0:tn], in_=op_[:, :].rearrange("p (h t) -> p h t", t=128)[:, :, 0:tn], func=AF.Square),
                                reads=[opb], writes=[sqb])
                        rst, rsb = rs_r.next()
                        for half in range(2):
                            sp_, spb = psB.next()
                            if tn == 128:
                                S.op("pe", lambda half=half, sp_=sp_: nc.tensor.matmul(sp_[:, :], lhsT=self.onesf[:, :],
                                                                                       rhs=sqt[:, half * 4:half * 4 + 4, :].rearrange("p h t -> p (h t)"), start=True, stop=True),
                                     reads=[sqb, bc], writes=[spb])
                            else:
                                for hq in range(4):
                                    S.op("pe", lambda half=half, sp_=sp_, hq=hq: nc.tensor.matmul(sp_[:, hq * 128:hq * 128 + tn], lhsT=self.onesf[:, :],
                                                                                                rhs=sqt[:, half * 4 + hq, 0:tn], start=True, stop=True),
                                         reads=[sqb, bc], writes=[spb], signal=(hq == 3))
                            S.op("act", lambda half=half, sp_=sp_: nc.scalar.activation(
                                out=rst[:, half * 4:half * 4 + 4, 0:tn], in_=sp_[:, :].rearrange("p (h t) -> p h t", t=128)[:, :, 0:tn],
                                func=AF.Ln, scale=1.0 / 128.0, bias=RMS_EPS), reads=[spb], writes=[rsb])
                            S.op("act", lambda half=half: nc.scalar.activation(out=rst[:, half * 4:half * 4 + 4, 0:tn], in_=rst[:, half * 4:half * 4 + 4, 0:tn],
                                                                               func=AF.Exp, scale=-0.5), reads=[rsb], writes=[rsb])
                            op_, opb = opl[half]
                            S.op("dve", lambda half=half, op_=op_: nc.vector.tensor_tensor(
                                out=ont[:, half * 4:half * 4 + 4, 0:tn], in0=op_[:, :].rearrange("p (h t) -> p h t", t=128)[:, :, 0:tn],
                                in1=rst[:, half * 4:half * 4 + 4, 0:tn], op=ALU.mult), reads=[opb, rsb], writes=[onb])
                        S.dma("pool", self.on_scr[:, :, tt0:tt0 + tn].rearrange("k p t -> p k t"), ont[:, :, 0:tn], reads=[onb], writes=[self.b_on])

                s1 = {}

                def run_s1(k):
                    s1[k] = stage1(tiles[k][0], tiles[k][1], tis[k])

                run_s1(0)
                for k in range(len(tiles)):
                    fns = [lambda k=k: stage2(tiles[k][0], tiles[k][1], tis[k], s1.pop(k))]
                    if k + 1 < len(tiles):
                        fns.append(lambda k=k: run_s1(k + 1))
                    self.IL.run(fns)
            if state_only:
                xt = self.sb(es, "xch", [128, XW], F32)
                xb = Buf("xch")
                tl = T - 32
                ti = len(TILES) - 1
                for ck in range(KC):
                    cvp, cvb = self.ps()
                    for kc in range(KC):
                        S.op("pe", lambda kc=kc: nc.tensor.matmul(cvp[:, 0:32], lhsT=wCV[:, kc, ck * 128:(ck + 1) * 128], rhs=self.hT[:, kc, tl:T],
                                                                  start=(kc == 0), stop=(kc == KC - 1)),
                             reads=[self.b_hT[ti], wCVb[(ck * 128) // pc]], writes=[cvb], signal=(kc == KC - 1))
                    cgp, cgb = self.ps()
                    for kc in range(KC):
                        S.op("pe", lambda kc=kc: nc.tensor.matmul(cgp[:, 0:32], lhsT=wCG[:, kc, ck * 128:(ck + 1) * 128], rhs=self.hT[:, kc, tl:T],
                                                                  start=(kc == 0), stop=(kc == KC - 1)),
                             reads=[self.b_hT[ti], wCGb[(ck * 128) // pc]], writes=[cgb], signal=(kc == KC - 1))
                    e1, e1b = tr.next()
                    S.op("act", lambda: nc.scalar.activation(out=e1[:, 0:32], in_=cgp[:, 0:32], func=AF.Exp, scale=-1.0), reads=[cgb], writes=[e1b])
                    S.op("act", lambda: nc.scalar.activation(out=e1[:, 0:32], in_=e1[:, 0:32], func=AF.Ln, bias=1.0), reads=[e1b], writes=[e1b])
                    S.op("act", lambda: nc.scalar.activation(out=e1[:, 0:32], in_=e1[:, 0:32], func=AF.Exp, scale=-1.0), reads=[e1b], writes=[e1b])
                    S.op("dve", lambda: nc.vector.tensor_tensor(out=xt[:, 1024 + ck * HALO:1024 + (ck + 1) * HALO], in0=cvp[:, 2:32], in1=e1[:, 2:32], op=ALU.mult),
                         reads=[cvb, e1b], writes=[xb])
                S.op("dve", lambda: nc.vector.tensor_copy(out=xt[:, 0:1024], in_=self.Sst[:, :]), reads=[self.b_S] + self.b_Sh, writes=[xb])
                self.b_ccin = Buf("cc_in")
                S.dma("pool", self.cc_in[:, :], xt[:, :], reads=[xb], writes=[self.b_ccin])
            S.barrier()

    def exchange(self):
        nc, S = self.nc, self.S
        b_ccout = Buf("cc_out")
        S.collective(lambda: nc.gpsimd.collective_compute("AllGather", ALU.bypass, replica_groups=[[0, 1], [2, 3], [4, 5], [6, 7]],
                                                          ins=[self.cc_in[:, :]], outs=[self.cc_out[:, :]]),
                     reads=[self.b_ccin], writes=[b_ccout])
        with ExitStack() as es:
            xt = self.sb(es, "xin_t", [128, XW], F32)
            xb = Buf("xin_t")
            S.dma("sp", xt[:, :], self.cc_out[0:128, :], reads=[b_ccout], writes=[xb])
            S.op("dve", lambda: nc.vector.tensor_scalar(out=self.Sst[:, :], in0=xt[:, 0:1024], scalar1=self.flag[:, 0:1], scalar2=None, op0=ALU.mult),
                 reads=[xb, self.b_const], writes=[self.b_S] + self.b_Sh)
            S.op("dve", lambda: nc.vector.tensor_scalar(out=self.halo0[:, :, :], in0=xt[:, 1024:XW].rearrange("p (k t) -> p k t", t=HALO),
                                                        scalar1=self.flag[:, 0:1], scalar2=2.0, op0=ALU.mult, op1=ALU.mult),
                 reads=[xb, self.b_const], writes=[self.b_halo0])
            S.barrier()

    def phase_conv(self, l):
        nc, S, I = self.nc, self.S, self.I
        self.psr = self.psr6
        W = I["w_in"][l]
        BL = blocks(512)
        bc = self.b_const
        with ExitStack() as es:
            wCV, wCVb, pc = self.load_w(es, "wCV", W[:, OFF_CV:OFF_CV + D], D)
            wCG, wCGb, _ = self.load_w(es, "wCG", W[:, OFF_CG:OFF_CG + D], D)
            cw = self.sb(es, "cw", [128, KC, CW], F32)
            par = self.sb(es, "cpar", [128, 3, KC], F32)
            b_par = Buf("cpar")
            S.dma("sp", cw[:], I["conv_w"][l], writes=[b_par])
            S.dma("sp", par[:, 0, :], I["conv_b"][l], writes=[b_par])
            S.dma("sp", par[:, 1, :], I["conv_ln_g"][l], writes=[b_par])
            S.dma("sp", par[:, 2, :], I["conv_ln_b"][l], writes=[b_par])
            S.op("dve", lambda: nc.vector.tensor_scalar(out=cw[:], in0=cw[:], scalar1=0.5, scalar2=None, op0=ALU.mult), reads=[b_par], writes=[b_par])
            diag = self.sb(es, "diag", [128, KC, NPE, 128], BF16)
            acc_r = self.ring(es, "cacc", [128, 512], F32, 4)
            b_diag = Buf("diag")
            for ck in range(KC):
                S.op("pool", lambda ck=ck: nc.gpsimd.tensor_tensor(
                    out=diag[:, ck, :, :], in0=self.ident[:, :].rearrange("p (o n) -> p o n", o=1).to_broadcast([128, NPE, 128]),
                    in1=cw[:, ck, 0:NPE].rearrange("p (j o) -> p j o", o=1).to_broadcast([128, NPE, 128]), op=ALU.mult),
                    reads=[b_par, bc], writes=[b_diag])
            uT = self.sb(es, "uT", [128, KC, HALO + 512], BF16)
            b_u = Buf("uT")
            b_uk = [Buf(f"uT{i}") for i in range(KC)]
            utmp = self.sb(es, "utmp", [128, KC, HALO], BF16)
            S.op("dve", lambda: nc.vector.tensor_copy(out=uT[:, :, 0:HALO], in_=self.halo0[:, :, :]), reads=[self.b_halo0], writes=[b_u] + b_uk)
            y_sb = self.sb(es, "cy", [128, KC, 512], F32)
            ysq = self.sb(es, "cysq", [128, KC, 512], F32)
            b_y, b_ysq = Buf("cy"), Buf("cysq")
            b_yk = [Buf(f"cyk{i}") for i in range(KC)]
            yc_r = self.ring(es, "ycT", [128, KC, 512], BF16, 1)
            tr = self.ring(es, "cv_t", [128, 512], F32, 5)
            for bi, (t0, n) in enumerate(BL):
                tis = [TILES.index((t0 + o, min(128, n - o))) for o in range(0, n, 128)]
                hbufs = [self.b_hT[ti] for ti in tis]
                def proj_body(ck):
                    cvp, cvb = self.ps()
                    for kc in range(KC):
                        S.op("pe", lambda kc=kc: nc.tensor.matmul(cvp[:, 0:n], lhsT=wCV[:, kc, ck * 128:(ck + 1) * 128], rhs=self.hT[:, kc, t0:t0 + n],
                                                                  start=(kc == 0), stop=(kc == KC - 1)),
                             reads=hbufs + [wCVb[(ck * 128) // pc]], writes=[cvb], signal=(kc == KC - 1))
                    cgp, cgb = self.ps()
                    for kc in range(KC):
                        S.op("pe", lambda kc=kc: nc.tensor.matmul(cgp[:, 0:n], lhsT=wCG[:, kc, ck * 128:(ck + 1) * 128], rhs=self.hT[:, kc, t0:t0 + n],
                                                                  start=(kc == 0), stop=(kc == KC - 1)),
                             reads=hbufs + [wCGb[(ck * 128) // pc]], writes=[cgb], signal=(kc == KC - 1))
                    th, thb = tr.next()
                    S.op("act", lambda: nc.scalar.activation(out=th[:, 0:n], in_=cgp[:, 0:n], func=AF.Tanh, scale=0.5), reads=[cgb], writes=[thb])
                    S.op("dve", lambda: nc.vector.scalar_tensor_tensor(out=uT[:, ck, HALO:HALO + n], in0=th[:, 0:n], scalar=1.0, in1=cvp[:, 0:n],
                                                                       op0=ALU.add, op1=ALU.mult), reads=[thb, cvb, b_u], writes=[b_uk[ck]])
                for c0 in range(0, KC, 2):
                    self.IL.run([lambda c=c0: proj_body(c), lambda c=c0 + 1: proj_body(c)])

                def conv_body(ck):
                    yp, ypb = self.ps()
                    for j in range(NPE):
                        S.op("pe", lambda j=j: nc.tensor.matmul(yp[:, 0:n], lhsT=diag[:, ck, j, :], rhs=uT[:, ck, j:j + n], start=(j == 0), stop=(j == NPE - 1)),
                             reads=[b_u, b_uk[ck], b_diag], writes=[ypb], signal=(j == NPE - 1))
                    ac, acb = acc_r.next()
                    S.op("dve", lambda: nc.vector.tensor_scalar(out=ac[:, 0:n], in0=uT[:, ck, NPE:NPE + n], scalar1=cw[:, ck, NPE:NPE + 1], scalar2=None, op0=ALU.mult),
                         reads=[b_u, b_uk[ck], b_par], writes=[acb])
                    for j in range(NPE + 1, CW):
                        S.op("dve", lambda j=j: nc.vector.scalar_tensor_tensor(out=ac[:, 0:n], in0=uT[:, ck, j:j + n], scalar=cw[:, ck, j:j + 1], in1=ac[:, 0:n],
                                                                               op0=ALU.mult, op1=ALU.add), reads=[b_u, b_uk[ck], b_par], writes=[acb])
                    S.op("dve", lambda: nc.vector.scalar_tensor_tensor(out=y_sb[:, ck, 0:n], in0=yp[:, 0:n], scalar=par[:, 0, ck:ck + 1], in1=ac[:, 0:n],
                                                                       op0=ALU.add, op1=ALU.add), reads=[ypb, b_par, acb], writes=[b_y, b_yk[ck]])
                    S.op("act", lambda: nc.scalar.activation(out=ysq[:, ck, 0:n], in_=y_sb[:, ck, 0:n], func=AF.Square),
                         reads=[b_yk[ck]], writes=[b_ysq])
                for c0 in range(0, KC, 2):
                    self.IL.run([lambda c=c0: conv_body(c), lambda c=c0 + 1: conv_body(c)])
                S.op("pool", lambda: nc.gpsimd.tensor_copy(out=utmp[:, :, :], in_=uT[:, :, n:n + HALO]), reads=[b_u] + b_uk, writes=[b_u])
                S.op("pool", lambda: nc.gpsimd.tensor_copy(out=uT[:, :, 0:HALO], in_=utmp[:, :, :]), reads=[b_u], writes=[b_u] + b_uk)
                mp, mpb = self.ps()
                for ck in range(KC):
                    S.op("pe", lambda ck=ck: nc.tensor.matmul(mp[:, 0:n], lhsT=self.onesf[:, :], rhs=y_sb[:, ck, 0:n], start=(ck == 0), stop=(ck == KC - 1)),
                         reads=[b_y, bc], writes=[mpb], signal=(ck == KC - 1))
                qp, qpb = self.ps()
                for ck in range(KC):
                    S.op("pe", lambda ck=ck: nc.tensor.matmul(qp[:, 0:n], lhsT=self.onesf[:, :], rhs=ysq[:, ck, 0:n], start=(ck == 0), stop=(ck == KC - 1)),
                         reads=[b_ysq, bc], writes=[qpb], signal=(ck == KC - 1))
                mean, meb = tr.next()
                var, vab = tr.next()
                S.op("act", lambda: nc.scalar.mul(out=mean[:, 0:n], in_=mp[:, 0:n], mul=1.0 / D), reads=[mpb], writes=[meb])
                S.op("dve", lambda: nc.vector.tensor_tensor(out=var[:, 0:n], in0=mean[:, 0:n], in1=mean[:, 0:n], op=ALU.mult), reads=[meb], writes=[vab])
                S.op("dve", lambda: nc.vector.scalar_tensor_tensor(out=var[:, 0:n], in0=qp[:, 0:n], scalar=1.0 / D, in1=var[:, 0:n], op0=ALU.mult, op1=ALU.subtract),
                     reads=[qpb], writes=[vab])
                S.op("act", lambda: nc.scalar.activation(out=var[:, 0:n], in_=var[:, 0:n], func=AF.Ln, bias=LN_EPS), reads=[vab], writes=[vab])
                S.op("act", lambda: nc.scalar.activation(out=var[:, 0:n], in_=var[:, 0:n], func=AF.Exp, scale=-0.5), reads=[vab], writes=[vab])
                S.op("dve", lambda: nc.vector.scalar_tensor_tensor(out=mean[:, 0:n], in0=mean[:, 0:n], scalar=-1.0, in1=var[:, 0:n], op0=ALU.mult, op1=ALU.mult),
                     reads=[vab], writes=[meb])
                yct, ycb = yc_r.next()
                def norm_body(ck):
                    S.op("dve", lambda ck=ck: nc.vector.tensor_tensor(out=y_sb[:, ck, 0:n], in0=y_sb[:, ck, 0:n], in1=var[:, 0:n], op=ALU.mult), reads=[vab, b_y], writes=[b_yk[ck]])
                    S.op("pool", lambda ck=ck: nc.gpsimd.tensor_tensor(out=y_sb[:, ck, 0:n], in0=y_sb[:, ck, 0:n], in1=mean[:, 0:n], op=ALU.add), reads=[meb, b_yk[ck]], writes=[b_yk[ck]])
                    S.op("act", lambda ck=ck: nc.scalar.activation(out=yct[:, ck, 0:n], in_=y_sb[:, ck, 0:n], func=AF.Silu, scale=par[:, 1, ck:ck + 1], bias=par[:, 2, ck:ck + 1]),
                         reads=[b_yk[ck], b_par, ycb], writes=[ycb_k[ck]])
                ycb_k = [Buf(f"ycb{i}") for i in range(KC)]
                for c0 in range(0, KC, 4):
                    self.IL.run([lambda c=c0 + i: norm_body(c) for i in range(4)])
                S.dma("pool", self.yc_scr[:, :, t0:t0 + n].rearrange("k p t -> p k t"), yct[:, :, 0:n], reads=[ycb] + ycb_k, writes=[self.b_yc, ycb])
            S.barrier()

    def phase_mix(self, l):
        nc, S, I = self.nc, self.S, self.I
        self.psr = self.psr6
        W = I["w_in"][l]
        NB = 256
        BL = blocks(NB)
        bc = self.b_const
        moe = (l % 2 == 1)
        with ExitStack() as es:
            wOG, wOGb, pc = self.load_w(es, "wOG", W[:, OFF_OG:OFF_OG + D], D)
            wHG, wHGb, _ = self.load_w(es, "wHG", I["w_hg_out"][l], D)
            wGA, wGAb, _ = self.load_w(es, "wGA", W[:, OFF_GA:OFF_GA + D], D)
            wCO, wCOb, _ = self.load_w(es, "wCO", I["w_conv_out"][l], D)
            wGB, wGBb, _ = self.load_w(es, "wGB", W[:, OFF_GB:OFF_GB + D], D)
            wO, wOb, _ = self.load_w(es, "wO", I["w_out"][l], D)
            hg = self.sb(es, "hg_g", [128, KC], F32)
            b_par = Buf("mixpar")
            S.dma("sp", hg[:], I["hg_norm_g"][l], writes=[b_par])
            g_t = self.sb(es, "ln1_g", [128, D], F32)
            b_t = self.sb(es, "ln1_b", [128, D], F32)
            gb = Buf("ln1_gb")
            if moe:
                wr = self.sb(es, "wr", [128, KC, N_EXP], F32)
                S.dma("sp", wr[:], I["moe_router"][0], writes=[b_par])
                lg_all = self.sb(es, "lg_all", [128, len(TILES), N_EXP], F32)
                b_lg = Buf("lg_all")
                S.op("dve", lambda: nc.vector.memset(lg_all[:], 0.0), writes=[b_lg])
                h32T_r = self.ring(es, "h32T", [128, KC, 128], F32, 2)
            on_r = self.ring(es, "on_blk", [128, KC, NB], BF16, 2)
            yc_r = self.ring(es, "yc_blk", [128, KC, NB], BF16, 2)
            gt_r = self.ring(es, "gatedT", [128, KC, NB], BF16, 1)
            mx_r = self.ring(es, "mixedT", [128, KC, NB], BF16, 1)
            tr = self.ring(es, "mx_t", [128, NB], F32, 4)
            ho_r = self.ring(es, "h_old", [128, D], F32, 2)
            ah_r = self.ring(es, "ah_t", [128, D], F32, 2)
            hbr = self.ring(es, "mx_hb", [128, D], BF16, 2)
            sc = [self.ln_scratch(es, f"ln1s{i}") for i in range(2)]
            loaded = {}

            def load(bi):
                t0, n = BL[bi]
                ont, onb = on_r.next()
                yct, ycb = yc_r.next()
                S.dma("sp", ont[:, :, 0:n], self.on_scr[:, :, t0:t0 + n].rearrange("k p t -> p k t"), reads=[self.b_on], writes=[onb])
                S.dma("sp", yct[:, :, 0:n], self.yc_scr[:, :, t0:t0 + n].rearrange("k p t -> p k t"), reads=[self.b_yc], writes=[ycb])
                loaded[bi] = (ont, onb, yct, ycb)

            load(0)
            for bi, (t0, n) in enumerate(BL):
                if bi + 1 < len(BL):
                    load(bi + 1)
                if bi == 0:
                    S.dma("sp", g_t[:], I["ln1_g"][l:l + 1, :].to_broadcast([128, D]), writes=[gb])
                    S.dma("sp", b_t[:], I["ln1_b"][l:l + 1, :].to_broadcast([128, D]), writes=[gb])
                ont, onb, yct, ycb = loaded.pop(bi)
                tiles = [(t0 + o, min(128, n - o)) for o in range(0, n, 128)]
                tis = [TILES.index(tt) for tt in tiles]
                hbufs = [self.b_hT[ti] for ti in tis]
                hold = []
                for (tt0, tn) in tiles:
                    hot, hob = ho_r.next()
                    S.dma("sp", hot[0:tn, :], self.h_scr[tt0:tt0 + tn, :], reads=[self.b_hscr], writes=[hob])
                    hold.append((hot, hob))
                gtt, gtb = gt_r.next()
                gtk = [Buf(f"gt{i}") for i in range(KC)]
                S.op("dve", lambda: nc.vector.memset(gtt[:, 0, 0:1], 0.0), writes=[gtb] + gtk)

                def og_body(ck):
                    ogp, ogb = self.ps()
                    for kc in range(KC):
                        S.op("pe", lambda kc=kc: nc.tensor.matmul(ogp[:, 0:n], lhsT=wOG[:, kc, ck * 128:(ck + 1) * 128], rhs=self.hT[:, kc, t0:t0 + n],
                                                                  start=(kc == 0), stop=(kc == KC - 1)),
                             reads=hbufs + [wOGb[(ck * 128) // pc]], writes=[ogb], signal=(kc == KC - 1))
                    sg, sgb = tr.next()
                    S.op("act", lambda: nc.scalar.activation(out=sg[:, 0:n], in_=ogp[:, 0:n], func=AF.Silu), reads=[ogb], writes=[sgb])
                    S.op("dve", lambda: nc.vector.scalar_tensor_tensor(out=gtt[:, ck, 0:n], in0=ont[:, ck, 0:n], scalar=hg[:, ck:ck + 1], in1=sg[:, 0:n],
                                                                       op0=ALU.mult, op1=ALU.mult), reads=[onb, sgb, b_par, gtb], writes=[gtk[ck]])
                for c0 in range(0, KC, 2):
                    self.IL.run([lambda c=c0: og_body(c), lambda c=c0 + 1: og_body(c)])
                mxt, mxb = mx_r.next()
                mxk = [Buf(f"mx{i}") for i in range(KC)]
                S.op("dve", lambda: nc.vector.memset(mxt[:, 0, 0:1], 0.0), writes=[mxb] + mxk)

                def m_body(m):
                    yrp, yrb = self.ps()
                    for kc in range(KC):
                        S.op("pe", lambda kc=kc: nc.tensor.matmul(yrp[:, 0:n], lhsT=wHG[:, kc, m * 128:(m + 1) * 128], rhs=gtt[:, kc, 0:n],
                                                                  start=(kc == 0), stop=(kc == KC - 1)),
                             reads=gtk + [gtb, wHGb[(m * 128) // pc]], writes=[yrb], signal=(kc == KC - 1))
                    gap, gab = self.ps()
                    for kc in range(KC):
                        S.op("pe", lambda kc=kc: nc.tensor.matmul(gap[:, 0:n], lhsT=wGA[:, kc, m * 128:(m + 1) * 128], rhs=self.hT[:, kc, t0:t0 + n],
                                                                  start=(kc == 0), stop=(kc == KC - 1)),
                             reads=hbufs + [wGAb[(m * 128) // pc]], writes=[gab], signal=(kc == KC - 1))
                    ta, tab = tr.next()
                    S.op("act", lambda: nc.scalar.activation(out=ta[:, 0:n], in_=gap[:, 0:n], func=AF.Tanh, scale=0.5), reads=[gab], writes=[tab])
                    S.op("dve", lambda: nc.vector.scalar_tensor_tensor(out=ta[:, 0:n], in0=ta[:, 0:n], scalar=1.0, in1=yrp[:, 0:n], op0=ALU.add, op1=ALU.mult),
                         reads=[yrb], writes=[tab])
                    ycp, ycpb = self.ps()
                    for kc in range(KC):
                        S.op("pe", lambda kc=kc: nc.tensor.matmul(ycp[:, 0:n], lhsT=wCO[:, kc, m * 128:(m + 1) * 128], rhs=yct[:, kc, 0:n],
                                                                  start=(kc == 0), stop=(kc == KC - 1)),
                             reads=[ycb, wCOb[(m * 128) // pc]], writes=[ycpb], signal=(kc == KC - 1))
                    gbp, gbb = self.ps()
                    for kc in range(KC):
                        S.op("pe", lambda kc=kc: nc.tensor.matmul(gbp[:, 0:n], lhsT=wGB[:, kc, m * 128:(m + 1) * 128], rhs=self.hT[:, kc, t0:t0 + n],
                                                                  start=(kc == 0), stop=(kc == KC - 1)),
                             reads=hbufs + [wGBb[(m * 128) // pc]], writes=[gbb], signal=(kc == KC - 1))
                    tb_, tbb = tr.next()
                    S.op("act", lambda: nc.scalar.activation(out=tb_[:, 0:n], in_=gbp[:, 0:n], func=AF.Tanh, scale=0.5), reads=[gbb], writes=[tbb])
                    S.op("dve", lambda: nc.vector.scalar_tensor_tensor(out=tb_[:, 0:n], in0=tb_[:, 0:n], scalar=1.0, in1=ycp[:, 0:n], op0=ALU.add, op1=ALU.mult),
                         reads=[ycpb], writes=[tbb])
                    S.op("dve", lambda: nc.vector.tensor_tensor(out=mxt[:, m, 0:n], in0=ta[:, 0:n], in1=tb_[:, 0:n], op=ALU.add), reads=[tab, tbb, mxb], writes=[mxk[m]])
                for m0 in range(0, KC, 2):
                    self.IL.run([lambda m=m0: m_body(m), lambda m=m0 + 1: m_body(m)])
                def ln1_body(tt0, tn, ti, hot, hob):
                    o0 = tt0 - t0
                    rt, rb = hot, hob
                    for hh in range(2):
                        pp, ppb = self.ps()
                        for kc in range(KC):
                            S.op("pe", lambda kc=kc: nc.tensor.matmul(pp[0:tn, :], lhsT=mxt[:, kc, o0:o0 + tn], rhs=wO[:, kc, hh * 512:(hh + 1) * 512],
                                                                      start=(kc == 0), stop=(kc == KC - 1)),
                                 reads=mxk + [mxb, wOb[(hh * 512) // pc]], writes=[ppb], signal=(kc == KC - 1))
                        S.op("dve", lambda hh=hh, pp=pp: nc.vector.scalar_tensor_tensor(out=rt[0:tn, hh * 512:(hh + 1) * 512], in0=pp[0:tn, :], scalar=0.5,
                                                                                       in1=hot[0:tn, hh * 512:(hh + 1) * 512], op0=ALU.mult, op1=ALU.add),
                             reads=[ppb, hob], writes=[rb])
                    hnt, hnb = hot, hob
                    self.layer_norm_tile(sc[ti % 2], rt[0:tn, :], rb, tn, g_t, b_t, gb, hnt[0:tn, :], hnb)
                    if ti == 0:
                        S.op("dve", lambda: nc.vector.tensor_scalar(out=hnt[0:tn, :], in0=hnt[0:tn, :], scalar1=self.flag[0:tn, 0:1],
                                                                    scalar2=None, op0=ALU.mult), reads=[bc], writes=[hnb])
                    aht, ahb = ah_r.next()
                    S.op("act", lambda: nc.scalar.mul(out=aht[0:tn, :], in_=hnt[0:tn, :], mul=ALPHA), reads=[hnb], writes=[ahb])
                    S.dma("pool", self.h_scr[tt0:tt0 + tn, :], aht[0:tn, :], reads=[ahb], writes=[self.b_hscr])
                    hbt, hbb = hbr.next()
                    self.to_hT(hnt[0:tn, :], hnb, tn, ti, hbt[0:tn, :], hbb)
                    if moe:
                        h32, h32b = h32T_r.next()
                        for half in range(2):
                            tp, tpb = self.ps()
                            for q in range(4):
                                kc = half * 4 + q
                                S.op("pe", lambda kc=kc, q=q: nc.tensor.transpose(tp[:, q * 128:q * 128 + tn], hnt[0:tn, kc * 128:(kc + 1) * 128], self.identf[0:tn, 0:tn]),
                                     reads=[hnb, bc], writes=[tpb], signal=(q == 3))
                            S.op("act", lambda half=half, tp=tp: nc.scalar.copy(out=h32[:, half * 4:half * 4 + 4, 0:tn],
                                                                               in_=tp[:, :].rearrange("p (k t) -> p k t", t=128)[:, :, 0:tn]),
                                 reads=[tpb], writes=[h32b])
                        lp, lpb = self.ps()
                        for kc in range(KC):
                            S.op("pe", lambda kc=kc: nc.tensor.matmul(lp[0:tn, 0:N_EXP], lhsT=h32[:, kc, 0:tn], rhs=wr[:, kc, :], start=(kc == 0), stop=(kc == KC - 1)),
                                 reads=[h32b, b_par], writes=[lpb], signal=(kc == KC - 1))
                        S.op("dve", lambda: nc.vector.tensor_copy(out=lg_all[0:tn, ti, :], in_=lp[0:tn, 0:N_EXP]), reads=[lpb], writes=[b_lg])
                self.IL.run([lambda a=a: ln1_body(a[0][0], a[0][1], a[1], a[2][0], a[2][1]) for a in zip(tiles, tis, hold)])
            if moe:
                NT = len(TILES)
                mx8 = self.sb(es, "mx8", [128, NT, 8], F32)
                msk = self.sb(es, "msk", [128, NT, 8], F32)
                den = self.sb(es, "den", [128, NT, 1], F32)
                b_g = Buf("gtmp")
                for ti in range(NT):
                    S.op("dve", lambda ti=ti: nc.vector.max(out=mx8[:, ti, :], in_=lg_all[:, ti, :]), reads=[b_lg], writes=[b_g], signal=(ti == NT - 1))
                S.op("dve", lambda: nc.vector.tensor_tensor(out=msk[:], in0=lg_all[:], in1=mx8[:, :, 1:2].to_broadcast([128, NT, 8]), op=ALU.is_ge),
                     reads=[b_lg, b_g], writes=[b_g])
                S.op("dve", lambda: nc.vector.tensor_tensor(out=lg_all[:], in0=lg_all[:], in1=mx8[:, :, 0:1].to_broadcast([128, NT, 8]), op=ALU.subtract),
                     reads=[b_g], writes=[b_lg])
                S.op("act", lambda: nc.scalar.activation(out=lg_all[:], in_=lg_all[:], func=AF.Exp), reads=[b_lg], writes=[b_lg])
                S.op("dve", lambda: nc.vector.tensor_tensor(out=lg_all[:], in0=lg_all[:], in1=msk[:], op=ALU.mult), reads=[b_g], writes=[b_lg])
                S.op("dve", lambda: nc.vector.tensor_reduce(out=den[:].rearrange("p n o -> p (n o)"), in_=lg_all[:], axis=AX.X, op=ALU.add), reads=[b_lg], writes=[b_g])
                S.op("dve", lambda: nc.vector.reciprocal(out=den[:], in_=den[:]), reads=[b_g], writes=[b_g])
                S.op("dve", lambda: nc.vector.tensor_tensor(out=self.gates[:], in0=lg_all[:], in1=den[:, :, 0:1].to_broadcast([128, NT, 8]), op=ALU.mult),
                     reads=[b_lg, b_g], writes=[self.b_gates])
            S.barrier()

    def phase_ffn(self, l, experts, dff, moe):
        nc, S, I = self.nc, self.S, self.I
        self.psr = self.psr6
        BL = blocks(512)
        bc = self.b_const
        G = 512
        groups = [(g0, min(G, dff - g0)) for g0 in range(0, dff, G)]
        last_layer = (l == DEPTH - 1)
        with ExitStack() as es:
            acc = self.sb(es, "acc", [128, len(TILES), D], F32)
            b_acc = [Buf(f"acc{i}") for i in range(len(TILES))]
            for ti, (t0, n) in enumerate(TILES):
                S.dma("sp", acc[0:n, ti, :], self.h_scr[t0:t0 + n, :], reads=[self.b_hscr], writes=[b_acc[ti]])
            g_t = self.sb(es, "ln2_g", [128, D], F32)
            b_t = self.sb(es, "ln2_b", [128, D], F32)
            gb = Buf("ln2_gb")
            S.dma("sp", g_t[:], I["ln2_g"][l:l + 1, :].to_broadcast([128, D]), writes=[gb])
            S.dma("sp", b_t[:], I["ln2_b"][l:l + 1, :].to_broadcast([128, D]), writes=[gb])
            NWB = 2
            wg_r = self.ring(es, "wg", [128, KC, G], BF16, NWB)
            wu_r = self.ring(es, "wu", [128, KC, G], BF16, NWB)
            wd_r = self.ring(es, "wd", [128, G // 128, D], BF16, NWB)
            aT_r = self.ring(es, "aT", [128, G // 128, 512], BF16, 2)
            tr = self.ring(es, "ff_t", [128, 512], F32, 4)
            work = [(e, g) for e in range(len(experts)) for g in groups]
            loaded = {}

            def load(wi):
                e, (g0, gn) = work[wi]
                wgs, wus, wds = experts[e]
                wgt, wgb = wg_r.next()
                wut, wub = wu_r.next()
                wdt, wdb = wd_r.next()
                S.dma("pool", wgt[:, :, 0:gn], wgs[:, g0:g0 + gn].rearrange("(kc p) n -> p kc n", p=128), writes=[wgb])
                S.dma("pool", wut[:, :, 0:gn], wus[:, g0:g0 + gn].rearrange("(kc p) n -> p kc n", p=128), writes=[wub])
                for hh in range(2):
                    S.dma("pool", wdt[:, 0:gn // 128, hh * 512:(hh + 1) * 512],
                          wds[g0:g0 + gn, hh * 512:(hh + 1) * 512].rearrange("(j p) n -> p j n", p=128), writes=[wdb])
                loaded[wi] = (wgt, wgb, wut, wub, wdt, wdb)

            hn_r = self.ring(es, "h2_new", [128, D], F32, 2)
            ah_r = self.ring(es, "h2_ah", [128, D], F32, 2)
            hbr = self.ring(es, "h2_hb", [128, D], BF16, 2)
            sc = [self.ln_scratch(es, f"ln2s{i}") for i in range(2)]

            def ln2_body(ti):
                t0, n = TILES[ti]
                hnt, hnb = hn_r.next()
                self.layer_norm_tile(sc[ti % 2], acc[0:n, ti, :], b_acc[ti], n, g_t, b_t, gb, hnt[0:n, :], hnb)
                if last_layer:
                    S.dma("pool", self.y[t0:t0 + n, :], hnt[0:n, :], reads=[hnb], writes=[self.b_y])
                else:
                    if ti == 0:
                        S.op("dve", lambda: nc.vector.tensor_scalar(out=hnt[0:n, :], in0=hnt[0:n, :], scalar1=self.flag[0:n, 0:1],
                                                                    scalar2=None, op0=ALU.mult), reads=[bc], writes=[hnb])
                    aht, ahb = ah_r.next()
                    S.op("act", lambda: nc.scalar.mul(out=aht[0:n, :], in_=hnt[0:n, :], mul=ALPHA), reads=[hnb], writes=[ahb])
                    S.dma("pool", self.h_scr[t0:t0 + n, :], aht[0:n, :], reads=[ahb], writes=[self.b_hscr])
                    hbt, hbb = hbr.next()
                    self.to_hT(hnt[0:n, :], hnb, n, ti, hbt[0:n, :], hbb)

            load(0)
            for wi, (e, (g0, gn)) in enumerate(work):
                if wi + 1 < len(work):
                    load(wi + 1)
                wgt, wgb, wut, wub, wdt, wdb = loaded.pop(wi)
                last_item = (wi == len(work) - 1)
                nj = gn // 128
                for bi, (t0, n) in enumerate(BL):
                    tiles = [(t0 + o, min(128, n - o)) for o in range(0, n, 128)]
                    tis = [TILES.index(tt) for tt in tiles]
                    hbufs = [self.b_hT[ti] for ti in tis]
                    aT, aTb = aT_r.next()
                    for j in range(nj):
                        gp, gpb = self.ps()
                        for kc in range(KC):
                            S.op("pe", lambda kc=kc: nc.tensor.matmul(gp[:, 0:n], lhsT=wgt[:, kc, j * 128:(j + 1) * 128], rhs=self.hT[:, kc, t0:t0 + n],
                                                                      start=(kc == 0), stop=(kc == KC - 1)),
                                 reads=hbufs + [wgb], writes=[gpb], signal=(kc == KC - 1))
                        up, upb = self.ps()
                        for kc in range(KC):
                            S.op("pe", lambda kc=kc: nc.tensor.matmul(up[:, 0:n], lhsT=wut[:, kc, j * 128:(j + 1) * 128], rhs=self.hT[:, kc, t0:t0 + n],
                                                                      start=(kc == 0), stop=(kc == KC - 1)),
                                 reads=hbufs + [wub], writes=[upb], signal=(kc == KC - 1))
                        sg, sgb = tr.next()
                        S.op("act", lambda: nc.scalar.activation(out=sg[:, 0:n], in_=gp[:, 0:n], func=AF.Silu), reads=[gpb], writes=[sgb])
                        S.op("dve", lambda j=j: nc.vector.tensor_tensor(out=aT[:, j, 0:n], in0=up[:, 0:n], in1=sg[:, 0:n], op=ALU.mult),
                             reads=[upb, sgb], writes=[aTb])
                    for (tt0, tn), ti in zip(tiles, tis):
                        o0 = tt0 - t0
                        for hh in range(2):
                            dp, dpb = self.ps()
                            for j in range(nj):
                                S.op("pe", lambda j=j: nc.tensor.matmul(dp[0:tn, :], lhsT=aT[:, j, o0:o0 + tn], rhs=wdt[:, j, hh * 512:(hh + 1) * 512],
                                                                        start=(j == 0), stop=(j == nj - 1)),
                                     reads=[aTb, wdb], writes=[dpb], signal=(j == nj - 1))
                            if moe:
                                S.op("dve", lambda hh=hh, dp=dp: nc.vector.scalar_tensor_tensor(
                                    out=acc[0:tn, ti, hh * 512:(hh + 1) * 512], in0=dp[0:tn, :], scalar=self.gates[0:tn, ti, e:e + 1],
                                    in1=acc[0:tn, ti, hh * 512:(hh + 1) * 512], op0=ALU.mult, op1=ALU.add),
                                    reads=[dpb, self.b_gates], writes=[b_acc[ti]])
                            else:
                                S.op("dve", lambda hh=hh, dp=dp: nc.vector.tensor_tensor(
                                    out=acc[0:tn, ti, hh * 512:(hh + 1) * 512], in0=dp[0:tn, :], in1=acc[0:tn, ti, hh * 512:(hh + 1) * 512], op=ALU.add),
                                    reads=[dpb], writes=[b_acc[ti]])
                        if last_item:
                            ln2_body(ti)
            S.barrier()


def pm(a):
    a = np.asarray(a, dtype=np.float32)
    return np.ascontiguousarray(a.reshape(a.shape[:-1] + (KC, 128)).swapaxes(-1, -2))


def make_in_maps(inputs, stop_after=None):
    x = np.asarray(inputs["x"], dtype=np.float32)
    meta = np.asarray(inputs["meta_tokens"], dtype=np.float32)
    B = x.shape[0]
    shared = {
        "ln_in_g": np.asarray(inputs["ln_in_g"], np.float32).reshape(1, D),
        "ln_in_b": np.asarray(inputs["ln_in_b"], np.float32).reshape(1, D),
        "w_in": np.ascontiguousarray(inputs["w_in"], dtype=np.float32),
        "lower_bounds": pm(inputs["lower_bounds"]),
        "hg_norm_g": pm(inputs["hg_norm_g"]),
        "w_hg_out": np.ascontiguousarray(inputs["w_hg_out"], dtype=np.float32),
        "conv_w": np.ascontiguousarray(np.asarray(inputs["conv_w"], np.float32).reshape(DEPTH, CW, KC, 128).transpose(0, 3, 2, 1)),
        "conv_b": pm(inputs["conv_b"]),
        "conv_ln_g": pm(inputs["conv_ln_g"]),
        "conv_ln_b": pm(inputs["conv_ln_b"]),
        "w_conv_out": np.ascontiguousarray(inputs["w_conv_out"], dtype=np.float32),
        "w_out": np.ascontiguousarray(inputs["w_out"], dtype=np.float32),
        "ln1_g": np.ascontiguousarray(inputs["ln1_g"], dtype=np.float32),
        "ln1_b": np.ascontiguousarray(inputs["ln1_b"], dtype=np.float32),
        "ln2_g": np.ascontiguousarray(inputs["ln2_g"], dtype=np.float32),
        "ln2_b": np.ascontiguousarray(inputs["ln2_b"], dtype=np.float32),
    }
    full = stop_after is None
    if full or stop_after >= 5:
        shared["ffn_w_gate"] = np.ascontiguousarray(inputs["ffn_w_gate"], dtype=np.float32)
        shared["ffn_w_up"] = np.ascontiguousarray(inputs["ffn_w_up"], dtype=np.float32)
        shared["ffn_w_down"] = np.ascontiguousarray(inputs["ffn_w_down"], dtype=np.float32)
    if full or stop_after >= 10:
        shared["moe_router"] = np.ascontiguousarray(
            np.asarray(inputs["moe_router"], np.float32).reshape(1, KC, 128, N_EXP).transpose(0, 2, 1, 3))
        shared["moe_w_gate"] = np.ascontiguousarray(inputs["moe_w_gate"], dtype=np.float32)
        shared["moe_w_up"] = np.ascontiguousarray(inputs["moe_w_up"], dtype=np.float32)
        shared["moe_w_down"] = np.ascontiguousarray(inputs["moe_w_down"], dtype=np.float32)
    maps = []
    NA = T - 32
    for c in range(8):
        b, half = c // 2, c % 2
        if half == 0:
            xin = np.concatenate([np.zeros((16, D), np.float32), meta, x[b, :NA]], axis=0)
        else:
            xin = x[b, NA:]
        m = dict(shared)
        m["xin"] = np.ascontiguousarray(xin)
        m["flag"] = np.full((128, 1), float(half), np.float32)
        maps.append(m)
    return maps


_PROG_CACHE = {}


def kernel(**inputs):
    x = np.asarray(inputs["x"])
    B, SEQ, _ = x.shape
    if None not in _PROG_CACHE:
        _PROG_CACHE[None] = Prog()
    prog = _PROG_CACHE[None]
    maps = make_in_maps(inputs)
    res = run_bass_kernel_spmd(prog.nc, maps, core_ids=list(range(8)))
    NA = T - 32
    out = np.empty((B, SEQ, D), np.float32)
    for c in range(8):
        b, half = c // 2, c % 2
        y = res.results[c]["y"]
        if half == 0:
            out[b, :NA] = y[32:]
        else:
            out[b, NA:] = y
    return out
```

```python
import numpy as np
import threading
from contextlib import ExitStack
import concourse.bass as bass
import concourse.mybir as mybir
from concourse.bass_utils import run_bass_kernel_spmd

F32 = mybir.dt.float32
BF16 = mybir.dt.bfloat16
AF = mybir.ActivationFunctionType
ALU = mybir.AluOpType
AX = mybir.AxisListType

D = 1024
KC = 8
T = 2064
PRE = 16
NH = 8
DEPTH = 2
D_FF = 2816
N_EXP = 8
D_FFE = 3584
CW = 31
HALO = CW - 1
OFF_Q, OFF_F, OFF_I, OFF_OG, OFF_CV, OFF_CG, OFF_GA, OFF_GB = [i * 1024 for i in range(8)]
ALPHA = float((2 * DEPTH) ** 0.25)
LN_EPS = 1e-5
RMS_EPS = 1e-6
XW = 1024 + KC * HALO
CH = 128


def blocks(n):
    out = [(0, PRE)]
    t = PRE
    while t < T:
        out.append((t, n))
        t += n
    return out


TILES = blocks(128)


class Buf:
    __slots__ = ("name", "w", "r")

    def __init__(self, name="b"):
        self.name = name
        self.w = None
        self.r = {}


class Sched:
    def __init__(self, nc, n_dma_sems=40):
        self.nc = nc
        self.eng = {"pe": nc.tensor, "act": nc.scalar, "dve": nc.vector, "pool": nc.gpsimd, "sp": nc.sync}
        self.sem = {k: nc.semaphore("s_" + k).__enter__() for k in self.eng}
        self.cnt = {k: 0 for k in self.eng}
        self.waited = {k: {} for k in self.eng}
        self.dma_sems = [nc.semaphore(f"s_dma{i}").__enter__() for i in range(n_dma_sems)]
        self.dma_cnt = [0] * n_dma_sems
        self.dma_rr = 0
        self.nwaits = 0
        self.ninstr = 0
        self.switch = None

    def _wait(self, e, dep, war=False):
        kind, key, val = dep
        if kind == "eng":
            if key == e and (war or e in ("pe", "sp")):
                return
            assert val <= self.cnt[key], (e, dep, self.cnt[key])
            sem = self.sem[key]
        else:
            sem = self.dma_sems[key]
        wk = (kind, key)
        if self.waited[e].get(wk, 0) >= val:
            return
        self.eng[e].wait_ge(sem, val)
        self.waited[e][wk] = val
        self.nwaits += 1

    def _deps(self, e, reads, writes):
        for b in reads:
            if b.w is not None:
                self._wait(e, b.w)
        for b in writes:
            if b.w is not None:
                self._wait(e, b.w)
            for d in b.r.values():
                self._wait(e, d, war=True)

    def _mark(self, tok, reads, writes):
        for b in reads:
            b.r[(tok[0], tok[1])] = tok
        for b in writes:
            b.w = tok
            b.r = {}

    def op(self, e, fn, reads=(), writes=(), signal=True):
        self._deps(e, reads, writes)
        ins = fn()
        self.ninstr += 1
        if signal or e != "pe":
            self.cnt[e] += 1
            ins.then_inc(self.sem[e], 1)
            tok = ("eng", e, self.cnt[e])
        else:
            tok = ("eng", e, self.cnt[e] + 1)
        self._mark(tok, reads, writes)
        if self.switch is not None:
            self.switch()
        return ins

    def dma(self, q, out, in_, reads=(), writes=(), **kw):
        self._deps(q, reads, writes)
        si = self.dma_rr
        self.dma_rr = (self.dma_rr + 1) % len(self.dma_sems)
        ins = self.eng[q].dma_start(out=out, in_=in_, **kw)
        self.dma_cnt[si] += 16
        ins.then_inc(self.dma_sems[si], 16)
        tok = ("dma", si, self.dma_cnt[si])
        self._mark(tok, reads, writes)
        self.ninstr += 1
        if self.switch is not None:
            self.switch()
        return tok

    def collective(self, fn, reads=(), writes=()):
        self._deps("pool", reads, writes)
        si = self.dma_rr
        self.dma_rr = (self.dma_rr + 1) % len(self.dma_sems)
        ins = fn()
        self.dma_cnt[si] += 1
        ins.then_inc(self.dma_sems[si])
        tok = ("dma", si, self.dma_cnt[si])
        self._mark(tok, reads, writes)
        return tok

    def barrier(self):
        if getattr(self, "on_barrier", None):
            self.on_barrier()
        for e in self.eng:
            for e2 in self.eng:
                if e2 != e and self.cnt[e2] > 0:
                    self._wait(e, ("eng", e2, self.cnt[e2]))
            for si, c in enumerate(self.dma_cnt):
                if c > 0:
                    self._wait(e, ("dma", si, c))


class Interleaver:
    def __init__(self, sched):
        self.S = sched

    def run(self, fns):
        if len(fns) == 1:
            fns[0]()
            return
        n = len(fns)
        self.ev = [threading.Event() for _ in range(n)]
        self.alive = [True] * n
        self.tid = {}
        errs = []

        def worker(i):
            self.ev[i].wait()
            self.ev[i].clear()
            try:
                fns[i]()
            except BaseException as e:
                errs.append(e)
            self.alive[i] = False
            j = self._next(i)
            if j is not None:
                self.ev[j].set()

        ths = [threading.Thread(target=worker, args=(i,)) for i in range(n)]
        for i, t in enumerate(ths):
            t.start()
            self.tid[t.ident] = i
        self.S.switch = self._switch
        self.ev[0].set()
        for t in ths:
            t.join()
        self.S.switch = None
        if errs:
            raise errs[0]

    def _next(self, i):
        n = len(self.alive)
        for d in range(1, n):
            j = (i + d) % n
            if self.alive[j]:
                return j
        return None

    def _switch(self):
        i = self.tid.get(threading.get_ident())
        if i is None:
            return
        j = self._next(i)
        if j is None:
            return
        self.ev[j].set()
        self.ev[i].wait()
        self.ev[i].clear()


class Ring:
    def __init__(self, tiles):
        self.items = [(t, Buf()) for t in tiles]
        self.i = 0

    def next(self):
        it = self.items[self.i]
        self.i = (self.i + 1) % len(self.items)
        return it


class Prog:
    def __init__(self, stop_after=None, debug=False):
        self.stop_after = stop_after
        self.debug = debug
        nc = bass.Bass("TRN2", target_bir_lowering=False)
        self.nc = nc
        self.S = Sched(nc)
        self.IL = Interleaver(self.S)
        self.wq = []
        self.S.on_barrier = lambda: self.wq.clear()
        self.es = ExitStack()
        self.build()

    def din(self, name, shape, dt=F32):
        return self.nc.dram_tensor(name, list(shape), dt, kind="ExternalInput").ap()

    def dscr(self, name, shape, dt=F32, cc=False):
        if self.debug and not cc:
            return self.nc.dram_tensor(name, list(shape), dt, kind="ExternalOutput").ap()
        return self.nc.dram_tensor(name, list(shape), dt).ap()

    def sb(self, es, name, shape, dt):
        self._uid = getattr(self, "_uid", 0) + 1
        return es.enter_context(self.nc.sbuf_tensor(f"sb{self._uid}_{name}", list(shape), dt))

    def ring(self, es, name, shape, dt, n):
        return Ring([self.sb(es, f"{name}{i}", shape, dt) for i in range(n)])

    def ps(self):
        return self.psr.next()

    def load_w(self, es, name, src, ncols, eng="pool", piece=512):
        t = self.sb(es, name, [128, KC, ncols], BF16)
        bufs = []
        for c0 in range(0, ncols, piece):
            b = Buf(name)
            prev = []
            self.S.dma(eng, t[:, :, c0:c0 + piece],
                       src[:, c0:c0 + piece].rearrange("(kc p) n -> p kc n", p=128), reads=prev, writes=[b])
            self.wq.append(b)
            bufs.append(b)
        return t, bufs, piece

    def build(self):
        nc, S = self.nc, self.S
        es = self.es
        I = {}
        I["xin"] = self.din("xin", [T, D])
        I["flag"] = self.din("flag", [128, 1])
        I["ln_in_g"] = self.din("ln_in_g", [1, D])
        I["ln_in_b"] = self.din("ln_in_b", [1, D])
        I["w_in"] = self.din("w_in", [DEPTH, D, 8 * D])
        I["lower_bounds"] = self.din("lower_bounds", [DEPTH, 128, KC])
        I["hg_norm_g"] = self.din("hg_norm_g", [DEPTH, 128, KC])
        I["w_hg_out"] = self.din("w_hg_out", [DEPTH, D, D])
        I["conv_w"] = self.din("conv_w", [DEPTH, 128, KC, CW])
        I["conv_b"] = self.din("conv_b", [DEPTH, 128, KC])
        I["conv_ln_g"] = self.din("conv_ln_g", [DEPTH, 128, KC])
        I["conv_ln_b"] = self.din("conv_ln_b", [DEPTH, 128, KC])
        I["w_conv_out"] = self.din("w_conv_out", [DEPTH, D, D])
        I["w_out"] = self.din("w_out", [DEPTH, D, D])
        I["ln1_g"] = self.din("ln1_g", [DEPTH, D])
        I["ln1_b"] = self.din("ln1_b", [DEPTH, D])
        I["ln2_g"] = self.din("ln2_g", [DEPTH, D])
        I["ln2_b"] = self.din("ln2_b", [DEPTH, D])
        full = self.stop_after is None
        if full or self.stop_after >= 5:
            I["ffn_w_gate"] = self.din("ffn_w_gate", [1, D, D_FF])
            I["ffn_w_up"] = self.din("ffn_w_up", [1, D, D_FF])
            I["ffn_w_down"] = self.din("ffn_w_down", [1, D_FF, D])
        if full or self.stop_after >= 10:
            I["moe_router"] = self.din("moe_router", [1, 128, KC, N_EXP])
            I["moe_w_gate"] = self.din("moe_w_gate", [1, N_EXP, D, D_FFE])
            I["moe_w_up"] = self.din("moe_w_up", [1, N_EXP, D, D_FFE])
            I["moe_w_down"] = self.din("moe_w_down", [1, N_EXP, D_FFE, D])
        self.I = I
        self.y = nc.dram_tensor("y", [T, D], F32, kind="ExternalOutput").ap()
        self.h_scr = self.dscr("h_scr", [T, D])
        self.on_scr = self.dscr("on_scr", [KC, 128, T], BF16)
        self.yc_scr = self.dscr("yc_scr", [KC, 128, T], BF16)
        self.cc_in = self.dscr("cc_in", [128, XW], cc=True)
        self.cc_out = self.dscr("cc_out", [256, XW], cc=True)
        self.b_hscr = Buf("h_scr")
        self.b_hs = [Buf(f"h_scr{i}") for i in range(len(TILES))]
        self.b_on = Buf("on_scr")
        self.b_yc = Buf("yc_scr")
        self.b_y = Buf("y")

        self.hT = self.sb(es, "hT", [128, KC, T], BF16)
        self.b_hT = [Buf(f"hT{i}") for i in range(len(TILES))]
        self.ident = self.sb(es, "ident", [128, 128], BF16)
        self.identf = self.sb(es, "identf", [128, 128], F32)
        self.onesf = self.sb(es, "onesf", [128, 128], F32)
        self.onesb = self.sb(es, "onesb", [128, 128], BF16)
        self.maskA = self.sb(es, "maskA", [128, 128], F32)
        self.maskA4 = self.sb(es, "maskA4", [128, 4, 128], F32)
        self.neghalf = self.sb(es, "neghalf", [128, 1], F32)
        self.flag = self.sb(es, "flag", [128, 1], F32)
        self.lbt = self.sb(es, "lbt", [128, DEPTH, KC], F32)
        self.lnoml = self.sb(es, "lnoml", [128, DEPTH, KC], F32)
        self.Sst = self.sb(es, "Sst", [128, D], F32)
        self.halo0 = self.sb(es, "halo0", [128, KC, HALO], BF16)
        self.gates = self.sb(es, "gates", [128, len(TILES), N_EXP], F32)
        self.b_const = Buf("const")
        self.b_S = Buf("S")
        self.b_Sh = [Buf(f"S{h}") for h in range(NH)]
        self.b_halo0 = Buf("halo0")
        self.b_gates = Buf("gates")
        self.psr = Ring([es.enter_context(nc.psum_tensor(f"ps{i}", [128, 512], F32)) for i in range(4)])
        self.pshold = Ring([es.enter_context(nc.psum_tensor(f"psh{i}", [128, 512], F32)) for i in range(2)])
        self.pstr = Ring([es.enter_context(nc.psum_tensor(f"pst{i}", [128, 1024], BF16)) for i in range(2)])

        self.psr4 = self.psr
        self.psr6 = Ring([])
        self.psr6.items = self.psr.items + self.pshold.items
        self.psA = Ring([])
        self.psA.items = self.psr.items[0:2]
        self.psB = Ring([])
        self.psB.items = self.psr.items[2:4]
        self.setup_consts()
        self.wes = ExitStack()
        hw = self.load_hgrn_weights(self.wes, 0)
        self.phase_ln_in()
        if self.stop_after == 0:
            return self.finish()
        for l in range(DEPTH):
            base = 5 * l
            if l > 0:
                self.wes = ExitStack()
                hw = self.load_hgrn_weights(self.wes, l)
            self.hw = hw
            self.phase_hgrn(l, state_only=True)
            self.exchange()
            if self.stop_after == base + 1:
                return self.finish()
            self.phase_hgrn(l, state_only=False)
            self.wes.close()
            self.wes = None
            if self.stop_after == base + 2:
                return self.finish()
            self.phase_conv(l)
            if self.stop_after == base + 3:
                return self.finish()
            self.phase_mix(l)
            if self.stop_after == base + 4:
                return self.finish()
            if l % 2 == 0:
                self.phase_ffn(l, [(I["ffn_w_gate"][0], I["ffn_w_up"][0], I["ffn_w_down"][0])], D_FF, moe=False)
            else:
                ex = [(I["moe_w_gate"][0, e], I["moe_w_up"][0, e], I["moe_w_down"][0, e]) for e in range(N_EXP)]
                self.phase_ffn(l, ex, D_FFE, moe=True)
            if self.stop_after == base + 5:
                return self.finish()
        self.finish()

    def finish(self):
        S = self.S
        S.barrier()
        if getattr(self, "wes", None) is not None:
            self.wes.close()
            self.wes = None
        self.es.close()

    def load_hgrn_weights(self, es, l):
        W = self.I["w_in"][l]
        wF = self.load_w(es, "wF", W[:, OFF_F:OFF_F + D], D)
        wI = self.load_w(es, "wI", W[:, OFF_I:OFF_I + D], D)
        wQ = self.load_w(es, "wQ", W[:, OFF_Q:OFF_Q + D], D)
        return dict(F=wF, I=wI, Q=wQ)

    def setup_consts(self):
        nc, S, I = self.nc, self.S, self.I
        bc = self.b_const
        S.op("pool", lambda: nc.gpsimd.memset(self.ident[:], 0.0), writes=[bc])
        S.op("pool", lambda: nc.gpsimd.affine_select(out=self.ident[:], in_=self.ident[:], pattern=[[-1, 128]],
                                                     compare_op=ALU.not_equal, fill=1.0, base=0, channel_multiplier=1),
             reads=[bc], writes=[bc])
        S.op("pool", lambda: nc.gpsimd.memset(self.identf[:], 0.0), writes=[bc])
        S.op("pool", lambda: nc.gpsimd.affine_select(out=self.identf[:], in_=self.identf[:], pattern=[[-1, 128]],
                                                     compare_op=ALU.not_equal, fill=1.0, base=0, channel_multiplier=1),
             reads=[bc], writes=[bc])
        S.op("dve", lambda: nc.vector.memset(self.onesf[:], 1.0), writes=[bc])
        S.op("dve", lambda: nc.vector.memset(self.onesb[:], 1.0), writes=[bc])
        S.op("dve", lambda: nc.vector.memset(self.neghalf[:], -0.5), writes=[bc])
        S.op("pool", lambda: nc.gpsimd.memset(self.maskA[:], 1.0), writes=[bc])
        S.op("pool", lambda: nc.gpsimd.affine_select(out=self.maskA[:], in_=self.maskA[:], pattern=[[1, 128]],
                                                     compare_op=ALU.is_ge, fill=0.0, base=0, channel_multiplier=-1),
             reads=[bc], writes=[bc])
        if CH == 64:
            S.op("pool", lambda: nc.gpsimd.memset(self.maskA[0:64, 64:128], 0.0), writes=[bc])
        for q in range(4):
            S.op("pool", lambda q=q: nc.gpsimd.tensor_copy(out=self.maskA4[:, q, :], in_=self.maskA[:, :]), reads=[bc], writes=[bc])
        S.dma("sp", self.flag[:], I["flag"][:, :], writes=[bc])
        S.op("dve", lambda: nc.vector.memset(self.Sst[:], 0.0), writes=[self.b_S])
        S.op("dve", lambda: nc.vector.memset(self.halo0[:], 0.0), writes=[self.b_halo0])
        S.op("dve", lambda: nc.vector.memset(self.gates[:], 0.0), writes=[self.b_gates])
        with ExitStack() as es:
            a = self.sb(es, "lb_a", [128, DEPTH, KC], F32)
            e = self.sb(es, "lb_e", [128, DEPTH, KC], F32)
            m = self.sb(es, "lb_m", [128, KC], F32)
            s = self.sb(es, "lb_s", [128, KC], F32)
            cum = self.sb(es, "lb_c", [128, KC], F32)
            b = Buf("lbtmp")
            for l in range(DEPTH):
                S.dma("sp", a[:, l, :], I["lower_bounds"][l], writes=[b])
            S.op("dve", lambda: nc.vector.tensor_copy(out=m[:], in_=a[:, 0, :]), reads=[b], writes=[b])
            for l in range(1, DEPTH):
                S.op("dve", lambda l=l: nc.vector.tensor_tensor(out=m[:], in0=m[:], in1=a[:, l, :], op=ALU.max), reads=[b], writes=[b])
            for l in range(DEPTH):
                S.op("dve", lambda l=l: nc.vector.tensor_tensor(out=a[:, l, :], in0=a[:, l, :], in1=m[:], op=ALU.subtract), reads=[b], writes=[b])
            S.op("act", lambda: nc.scalar.activation(out=e[:], in_=a[:], func=AF.Exp), reads=[b], writes=[b])
            S.op("dve", lambda: nc.vector.tensor_copy(out=s[:], in_=e[:, 0, :]), reads=[b], writes=[b])
            for l in range(1, DEPTH):
                S.op("dve", lambda l=l: nc.vector.tensor_tensor(out=s[:], in0=s[:], in1=e[:, l, :], op=ALU.add), reads=[b], writes=[b])
            S.op("dve", lambda: nc.vector.reciprocal(out=s[:], in_=s[:]), reads=[b], writes=[b])
            for l in range(DEPTH):
                S.op("dve", lambda l=l: nc.vector.tensor_tensor(out=e[:, l, :], in0=e[:, l, :], in1=s[:], op=ALU.mult), reads=[b], writes=[b])
            S.op("dve", lambda: nc.vector.memset(cum[:], 0.0), writes=[b])
            for l in range(DEPTH):
                S.op("dve", lambda l=l: nc.vector.tensor_tensor(out=cum[:], in0=cum[:], in1=e[:, l, :], op=ALU.add), reads=[b], writes=[b])
                S.op("dve", lambda l=l: nc.vector.tensor_tensor(out=self.lbt[:, l, :], in0=cum[:], in1=e[:, 0, :], op=ALU.subtract), reads=[b], writes=[bc])
            S.op("act", lambda: nc.scalar.activation(out=self.lnoml[:], in_=self.lbt[:], func=AF.Ln, scale=-1.0, bias=1.0),
                 reads=[bc], writes=[bc])
            S.barrier()

    def ln_tile(self, es_bufs, src, n, g_t, b_t, ti, out_h, out_scale_h=None):
        raise NotImplementedError

    def layer_norm_tile(self, sc, x_ap, xbuf, n, g_t, b_t, gb_buf, out_ap, out_buf, eps=LN_EPS):
        nc, S = self.nc, self.S
        st, mv, rstd, b = sc["stats"], sc["mv"], sc["rstd"], sc["b"]
        for hh in range(2):
            S.op("dve", lambda hh=hh: nc.vector.bn_stats(out=st[0:n, hh, :], in_=x_ap[:, hh * 512:(hh + 1) * 512]),
                 reads=[xbuf], writes=[b])
        S.op("dve", lambda: nc.vector.bn_aggr(out=mv[0:n, :], in_=st[0:n, :, :]), reads=[b], writes=[b])
        S.op("dve", lambda: nc.vector.tensor_scalar(out=rstd[0:n, :], in0=mv[0:n, 1:2], scalar1=eps, scalar2=None, op0=ALU.add),
             reads=[b], writes=[b])
        S.op("pool", lambda: nc.gpsimd.tensor_tensor(out=rstd[0:n, :], in0=rstd[0:n, :], in1=self.neghalf[0:n, 0:1], op=ALU.pow),
             reads=[b, self.b_const], writes=[b])
        S.op("dve", lambda: nc.vector.tensor_scalar(out=out_ap, in0=x_ap, scalar1=mv[0:n, 0:1], scalar2=rstd[0:n, 0:1],
                                                    op0=ALU.subtract, op1=ALU.mult),
             reads=[xbuf, b], writes=[out_buf])
        S.op("dve", lambda: nc.vector.tensor_tensor(out=out_ap, in0=out_ap, in1=g_t[0:n, :], op=ALU.mult),
             reads=[gb_buf], writes=[out_buf])
        S.op("dve", lambda: nc.vector.tensor_tensor(out=out_ap, in0=out_ap, in1=b_t[0:n, :], op=ALU.add),
             reads=[gb_buf], writes=[out_buf])

    def to_hT(self, h_ap, hbuf, n, ti, hb_ap, hb_buf):
        nc, S = self.nc, self.S
        t0, _ = TILES[ti]
        S.op("act", lambda: nc.scalar.copy(out=hb_ap, in_=h_ap), reads=[hbuf], writes=[hb_buf])
        pt, pb = self.pstr.next()
        for kc in range(KC):
            S.op("pe", lambda kc=kc: nc.tensor.transpose(pt[:, kc * 128:kc * 128 + n], hb_ap[:, kc * 128:(kc + 1) * 128], self.ident[0:n, 0:n]),
                 reads=[hb_buf, self.b_const], writes=[pb], signal=(kc == KC - 1))
        S.op("dve", lambda: nc.vector.tensor_copy(out=self.hT[:, :, t0:t0 + n],
                                                  in_=pt[:].rearrange("p (k t) -> p k t", t=128)[:, :, 0:n]),
             reads=[pb], writes=[self.b_hT[ti]])

    def load_bcast(self, es, name, src_row):
        t = self.sb(es, name, [128, D], F32)
        return t

    def ln_scratch(self, es, tag):
        return {"stats": self.sb(es, tag + "_st", [128, 2, 6], F32), "mv": self.sb(es, tag + "_mv", [128, 2], F32),
                "rstd": self.sb(es, tag + "_rs", [128, 1], F32), "b": Buf(tag)}

    def phase_ln_in(self):
        nc, S, I = self.nc, self.S, self.I
        with ExitStack() as es:
            g_t = self.sb(es, "lni_g", [128, D], F32)
            b_t = self.sb(es, "lni_b", [128, D], F32)
            gb = Buf("lni_gb")
            xr = self.ring(es, "lni_x", [128, D], F32, 4)
            hr = self.ring(es, "lni_h", [128, D], F32, 2)
            ar = self.ring(es, "lni_a", [128, D], F32, 2)
            hbr = self.ring(es, "lni_hb", [128, D], BF16, 2)
            sc = [self.ln_scratch(es, f"lni{i}") for i in range(2)]
            loaded = {}

            def load(ti):
                t0, n = TILES[ti]
                xt, xb = xr.next()
                S.dma("sp", xt[0:n, :], I["xin"][t0:t0 + n, :], writes=[xb])
                loaded[ti] = (xt, xb)

            load(0)
            load(1)
            S.dma("sp", g_t[:], I["ln_in_g"][0:1, :].to_broadcast([128, D]), writes=[gb])
            S.dma("sp", b_t[:], I["ln_in_b"][0:1, :].to_broadcast([128, D]), writes=[gb])

            def tile_body(ti):
                t0, n = TILES[ti]
                xt, xb = loaded.pop(ti)
                ht, hb = hr.next()
                self.layer_norm_tile(sc[ti % 2], xt[0:n, :], xb, n, g_t, b_t, gb, ht[0:n, :], hb)
                if ti == 0:
                    S.op("dve", lambda: nc.vector.tensor_scalar(out=ht[0:n, :], in0=ht[0:n, :], scalar1=self.flag[0:n, 0:1],
                                                                scalar2=None, op0=ALU.mult), reads=[self.b_const], writes=[hb])
                at, ab = ar.next()
                S.op("act", lambda: nc.scalar.mul(out=at[0:n, :], in_=ht[0:n, :], mul=ALPHA), reads=[hb], writes=[ab])
                S.dma("pool", self.h_scr[t0:t0 + n, :], at[0:n, :], reads=[ab], writes=[self.b_hs[ti]])
                hbt, hbb = hbr.next()
                self.to_hT(ht[0:n, :], hb, n, ti, hbt[0:n, :], hbb)

            for p0 in range(0, len(TILES), 2):
                pair = [q for q in (p0, p0 + 1) if q < len(TILES)]
                for q in pair:
                    if q + 2 < len(TILES):
                        load(q + 2)
                self.IL.run([lambda q=q: tile_body(q) for q in pair])
            S.barrier()

    def phase_hgrn(self, l, state_only):
        nc, S, I = self.nc, self.S, self.I
        self.psr = self.psr4
        W = I["w_in"][l]
        BL = blocks(512)
        with ExitStack() as es:
            wF, wFb, pc = self.hw["F"]
            wI, wIb, _ = self.hw["I"]
            wQ, wQb, _ = self.hw["Q"]
            if state_only:
                wCV, wCVb, _ = self.load_w(es, "wCV", W[:, OFF_CV:OFF_CV + D], D)
                wCG, wCGb, _ = self.load_w(es, "wCG", W[:, OFF_CG:OFF_CG + D], D)
            tr = self.ring(es, "hg_t", [128, 512], F32, 14)
            self.mask_sc = self.sb(es, "mask_sc", [128, 512], F32)
            S.op("dve", lambda: nc.vector.memset(self.mask_sc[:], 1.0), writes=[self.b_const])
            S.op("dve", lambda: nc.vector.memset(self.mask_sc[:].rearrange("p (c t) -> p c t", t=CH)[:, :, 0:1], 0.0),
                 writes=[self.b_const])
            ktT = self.sb(es, "ktT", [128, NH, 512], BF16)
            b_kt = [Buf(f"kt{h}") for h in range(NH)]
            if not state_only:
                qtT = self.sb(es, "qtT", [128, NH, 512], BF16)
                b_qt = [Buf(f"qt{h}") for h in range(NH)]
                on_sb = self.ring(es, "on_sb", [128, NH, 128], BF16, 2)
                att_r = self.ring(es, "att", [128, NH, 128], BF16, 2)
                cl_r = self.ring(es, "attcl", [128, 4, 128], F32, 2)
                sq_r = self.ring(es, "sq", [128, NH, 128], BF16, 2)
                rs_r = self.ring(es, "rs", [128, NH, 128], F32, 2)
                St_r = self.ring(es, "St", [128, D], BF16, 3)
            dec = self.sb(es, "dec", [128, NH, 8], F32)
            edl = self.sb(es, "edl", [128, NH, 8], F32)
            ebr = self.sb(es, "ebr", [128, NH, 8], F32)
            b_sc = Buf("chunk_scalars")
            v_r = self.ring(es, "v_sb", [128, D], BF16, 3)
            ktm_r = self.ring(es, "ktm", [128, D], BF16, 3)
            lb = self.lbt[:, l, :]
            lno = self.lnoml[:, l, :]
            bc = self.b_const

            for bi, (t0, n) in enumerate(BL):
                C = 16 if n == 16 else CH
                nch = n // C
                mid, last = C // 2 - 1, C - 1
                tiles = [(t0 + o, min(128, n - o)) for o in range(0, n, 128)]
                tis = [TILES.index(tt) for tt in tiles]
                hbufs = [self.b_hT[ti] for ti in tis]
                def head_body(h):
                    zp, zb = self.ps()
                    for kc in range(KC):
                        S.op("pe", lambda kc=kc: nc.tensor.matmul(zp[:, 0:n], lhsT=wF[:, kc, h * 128:(h + 1) * 128], rhs=self.hT[:, kc, t0:t0 + n],
                                                                  start=(kc == 0), stop=(kc == KC - 1)),
                             reads=hbufs + [wFb[(h * 128) // pc]], writes=[zb], signal=(kc == KC - 1))
                    E, Eb = tr.next()
                    L1, L1b = tr.next()
                    L2, L2b = tr.next()
                    lsn, lsnb = tr.next()
                    S.op("act", lambda: nc.scalar.activation(out=E[:, 0:n], in_=zp[:, 0:n], func=AF.Exp, scale=-1.0), reads=[zb], writes=[Eb])
                    S.op("act", lambda: nc.scalar.activation(out=L1[:, 0:n], in_=E[:, 0:n], func=AF.Ln, bias=1.0), reads=[Eb], writes=[L1b])
                    S.op("act", lambda: nc.scalar.activation(out=L2[:, 0:n], in_=E[:, 0:n], func=AF.Ln, scale=lb[:, h:h + 1], bias=1.0),
                         reads=[Eb, bc], writes=[L2b])
                    S.op("dve", lambda: nc.vector.scalar_tensor_tensor(out=lsn[:, 0:n], in0=zp[:, 0:n], scalar=-1.0, in1=L1[:, 0:n],
                                                                       op0=ALU.mult, op1=ALU.subtract), reads=[zb, L1b], writes=[lsnb])
                    S.op("dve", lambda: nc.vector.tensor_tensor(out=L2[:, 0:n], in0=L2[:, 0:n], in1=L1[:, 0:n], op=ALU.subtract),
                         reads=[L1b], writes=[L2b])
                    bt, bb = tr.next()
                    S.op("dve", lambda: nc.vector.tensor_tensor_scan(out=bt[:, 0:n], data0=self.mask_sc[:, 0:n], data1=L2[:, 0:n], initial=0.0,
                                                                     op0=ALU.mult, op1=ALU.add), reads=[L2b, bc], writes=[bb])
                    b3 = bt[:, 0:n].rearrange("p (c t) -> p c t", t=C)
                    df, dfb = tr.next()
                    S.op("dve", lambda: nc.vector.tensor_tensor(out=df[:, 0:n].rearrange("p (c t) -> p c t", t=C), in0=b3,
                                                                in1=b3[:, :, mid:mid + 1].to_broadcast([128, nch, C]), op=ALU.subtract),
                         reads=[bb], writes=[dfb])
                    d3 = df[:, 0:n].rearrange("p (c t) -> p c t", t=C)
                    S.op("act", lambda: nc.scalar.activation(out=dec[:, h, 0:nch], in_=b3[:, :, last], func=AF.Exp), reads=[bb], writes=[b_sc])
                    S.op("act", lambda: nc.scalar.activation(out=ebr[:, h, 0:nch], in_=b3[:, :, mid], func=AF.Exp), reads=[bb], writes=[b_sc])
                    S.op("act", lambda: nc.scalar.activation(out=edl[:, h, 0:nch], in_=d3[:, :, last], func=AF.Exp), reads=[dfb], writes=[b_sc])
                    S.op("dve", lambda: nc.vector.tensor_tensor(out=lsn[:, 0:n], in0=lsn[:, 0:n], in1=df[:, 0:n], op=ALU.subtract),
                         reads=[dfb], writes=[lsnb])
                    S.op("act", lambda: nc.scalar.activation(out=ktT[:, h, 0:n], in_=lsn[:, 0:n], func=AF.Exp, bias=lno[:, h:h + 1]),
                         reads=[lsnb, bc], writes=[b_kt[h]])
                    if not state_only:
                        qp, qb = self.ps()
                        for kc in range(KC):
                            S.op("pe", lambda kc=kc: nc.tensor.matmul(qp[:, 0:n], lhsT=wQ[:, kc, h * 128:(h + 1) * 128], rhs=self.hT[:, kc, t0:t0 + n],
                                                                      start=(kc == 0), stop=(kc == KC - 1)),
                                 reads=hbufs + [wQb[(h * 128) // pc]], writes=[qb], signal=(kc == KC - 1))
                        Eq, Eqb = tr.next()
                        S.op("act", lambda: nc.scalar.activation(out=Eq[:, 0:n], in_=qp[:, 0:n], func=AF.Exp, scale=-1.0), reads=[qb], writes=[Eqb])
                        S.op("act", lambda: nc.scalar.activation(out=Eq[:, 0:n], in_=Eq[:, 0:n], func=AF.Ln, bias=1.0), reads=[Eqb], writes=[Eqb])
                        S.op("dve", lambda: nc.vector.tensor_tensor(out=Eq[:, 0:n], in0=df[:, 0:n], in1=Eq[:, 0:n], op=ALU.subtract),
                             reads=[dfb], writes=[Eqb])
                        S.op("act", lambda: nc.scalar.activation(out=Eq[:, 0:n], in_=Eq[:, 0:n], func=AF.Exp), reads=[Eqb], writes=[Eqb])
                        S.op("dve", lambda: nc.vector.tensor_tensor(out=qtT[:, h, 0:n], in0=qp[:, 0:n], in1=Eq[:, 0:n], op=ALU.mult),
                             reads=[qb, Eqb], writes=[b_qt[h]])
                for h0 in range(0, NH, 2):
                    self.IL.run([lambda h=h0: head_body(h), lambda h=h0 + 1: head_body(h)])
                psA, psB = self.psA, self.psB

                def stage1(tt0, tn, ti):
                    o0 = tt0 - t0
                    r = {}
                    vt, vb = v_r.next()
                    for hh in range(2):
                        vp, vpb = psA.next()
                        for kc in range(KC):
                            S.op("pe", lambda kc=kc: nc.tensor.matmul(vp[0:tn, :], lhsT=self.hT[:, kc, tt0:tt0 + tn], rhs=wI[:, kc, hh * 512:(hh + 1) * 512],
                                                                      start=(kc == 0), stop=(kc == KC - 1)),
                                 reads=[self.b_hT[ti], wIb[(hh * 512) // pc]], writes=[vpb], signal=(kc == KC - 1))
                        S.op("act", lambda: nc.scalar.copy(out=vt[0:tn, hh * 512:(hh + 1) * 512], in_=vp[0:tn, :]), reads=[vpb], writes=[vb])
                    pt, ptb = self.pstr.next()
                    for h in range(NH):
                        S.op("pe", lambda h=h: nc.tensor.transpose(pt[0:tn, h * 128:(h + 1) * 128], ktT[:, h, o0:o0 + tn], self.ident[:, :]),
                             reads=[b_kt[h], bc], writes=[ptb], signal=(h == NH - 1))
                    kt, kb = ktm_r.next()
                    S.op("dve", lambda: nc.vector.tensor_copy(out=kt[0:tn, :], in_=pt[0:tn, :]), reads=[ptb], writes=[kb])
                    r.update(vt=vt, vb=vb, kt=kt, kb=kb)
                    if not state_only:
                        at, atb = att_r.next()
                        for half in range(2):
                            ap_, apb = psA.next()
                            for hq in range(4):
                                h = half * 4 + hq
                                S.op("pe", lambda h=h, hq=hq: nc.tensor.matmul(ap_[0:tn, hq * 128:hq * 128 + tn], lhsT=ktT[:, h, o0:o0 + tn],
                                                                               rhs=qtT[:, h, o0:o0 + tn], start=True, stop=True),
                                     reads=[b_kt[h], b_qt[h]], writes=[apb], signal=(hq == 3))
                            cl, clb = cl_r.next()
                            S.op("dve", lambda half=half: nc.vector.tensor_scalar(
                                out=cl[0:tn, :, 0:tn], in0=ap_[0:tn, :].rearrange("p (h t) -> p h t", t=128)[:, :, 0:tn],
                                scalar1=1e30, scalar2=-1e30, op0=ALU.min, op1=ALU.max), reads=[apb], writes=[clb])
                            S.op("dve", lambda half=half: nc.vector.tensor_tensor(
                                out=at[0:tn, half * 4:half * 4 + 4, 0:tn], in0=cl[0:tn, :, 0:tn],
                                in1=self.maskA4[0:tn, :, 0:tn], op=ALU.mult),
                                reads=[clb, bc], writes=[atb])
                        r.update(at=at, atb=atb)
                    return r

                def stage2(tt0, tn, ti, r):
                    o0 = tt0 - t0
                    vt, vb, kt, kb = r["vt"], r["vb"], r["kt"], r["kb"]
                    gch = o0 // C
                    c0, cn = 0, tn
                    if not state_only:
                        at, atb = r["at"], r["atb"]
                        ont, onb = on_sb.next()
                        sqt, sqb = sq_r.next()
                        opl = []
                        Stt, Stb = St_r.next()
                        for h in range(NH):
                            S.op("act", lambda h=h: nc.scalar.activation(out=Stt[:, h * 128:(h + 1) * 128], in_=self.Sst[:, h * 128:(h + 1) * 128],
                                                                         func=AF.Identity, scale=ebr[:, h, gch:gch + 1]),
                                 reads=[self.b_S, self.b_Sh[h], b_sc], writes=[Stb])
                        for half in range(2):
                            op_, opb = self.pshold.next()
                            opl.append((op_, opb))
                            for hq in range(4):
                                h = half * 4 + hq
                                S.op("pe", lambda h=h, hq=hq, op_=op_: nc.tensor.matmul(op_[:, hq * 128:hq * 128 + tn], lhsT=vt[0:tn, h * 128:(h + 1) * 128],
                                                                                     rhs=at[0:tn, h, 0:tn], start=(hq == 0), stop=False),
                                     reads=[vb, atb], writes=[opb], signal=False)
                        for half in range(2):
                            op_, opb = opl[half]
                            for hq in range(4):
                                h = half * 4 + hq
                                S.op("pe", lambda h=h, hq=hq, op_=op_: nc.tensor.matmul(op_[:, hq * 128:hq * 128 + tn], lhsT=Stt[:, h * 128:(h + 1) * 128],
                                                                                     rhs=qtT[:, h, o0:o0 + tn], start=False, stop=True),
                                     reads=[Stb, b_qt[h]], writes=[opb], signal=(hq == 3))
                    for half in range(2):
                        mp, mpb = psB.next()
                        for hq in range(4):
                            h = half * 4 + hq
                            S.op("pe", lambda h=h, hq=hq: nc.tensor.matmul(mp[:, hq * 128:(hq + 1) * 128], lhsT=kt[c0:c0 + cn, h * 128:(h + 1) * 128],
                                                                           rhs=vt[c0:c0 + cn, h * 128:(h + 1) * 128], start=True, stop=True),
                                 reads=[kb, vb], writes=[mpb], signal=(hq == 3))
                        for hq in range(4):
                            h = half * 4 + hq
                            S.op("dve", lambda h=h: nc.vector.tensor_scalar(out=self.Sst[:, h * 128:(h + 1) * 128], in0=self.Sst[:, h * 128:(h + 1) * 128],
                                                                            scalar1=dec[:, h, gch:gch + 1], scalar2=None, op0=ALU.mult),
                                 reads=[b_sc, self.b_S], writes=[self.b_Sh[h]])
                            S.op("dve", lambda h=h, hq=hq: nc.vector.scalar_tensor_tensor(out=self.Sst[:, h * 128:(h + 1) * 128], in0=mp[:, hq * 128:(hq + 1) * 128],
                                                                                          scalar=edl[:, h, gch:gch + 1], in1=self.Sst[:, h * 128:(h + 1) * 128],
                                                                                          op0=ALU.mult, op1=ALU.add),
                                 reads=[mpb, b_sc], writes=[self.b_Sh[h]])
                    if not state_only:
                        for half in range(2):
                            op_, opb = opl[half]
                            S.op("act", lambda half=half, op_=op_: nc.scalar.activation(
                                out=sqt[:, half * 4:half * 4 + 4, 0:tn], in_=op_[:, :].rearrange("p (h t) -> p h t", t=128)[:, :, 0:tn], func=AF.Square),
                                reads=[opb], writes=[sqb])
                        rst, rsb = rs_r.next()
                        for half in range(2):
                            sp_, spb = psB.next()
                            if tn == 128:
                                S.op("pe", lambda half=half, sp_=sp_: nc.tensor.matmul(sp_[:, :], lhsT=self.onesb[:, :],
                                                                                       rhs=sqt[:, half * 4:half * 4 + 4, :].rearrange("p h t -> p (h t)"), start=True, stop=True),
                                     reads=[sqb, bc], writes=[spb])
                            else:
                                for hq in range(4):
                                    S.op("pe", lambda half=half, sp_=sp_, hq=hq: nc.tensor.matmul(sp_[:, hq * 128:hq * 128 + tn], lhsT=self.onesb[:, :],
                                                                                                rhs=sqt[:, half * 4 + hq, 0:tn], start=True, stop=True),
                                         reads=[sqb, bc], writes=[spb], signal=(hq == 3))
                            S.op("act", lambda half=half, sp_=sp_: nc.scalar.activation(
                                out=rst[:, half * 4:half * 4 + 4, 0:tn], in_=sp_[:, :].rearrange("p (h t) -> p h t", t=128)[:, :, 0:tn],
                                func=AF.Ln, scale=1.0 / 128.0, bias=RMS_EPS), reads=[spb], writes=[rsb])
                            S.op("act", lambda half=half: nc.scalar.activation(out=rst[:, half * 4:half * 4 + 4, 0:tn], in_=rst[:, half * 4:half * 4 + 4, 0:tn],
                                                                               func=AF.Exp, scale=-0.5), reads=[rsb], writes=[rsb])
                            op_, opb = opl[half]
                            S.op("dve", lambda half=half, op_=op_: nc.vector.tensor_tensor(
                                out=ont[:, half * 4:half * 4 + 4, 0:tn], in0=op_[:, :].rearrange("p (h t) -> p h t", t=128)[:, :, 0:tn],
                                in1=rst[:, half * 4:half * 4 + 4, 0:tn], op=ALU.mult), reads=[opb, rsb], writes=[onb])
                        S.dma("pool", self.on_scr[:, :, tt0:tt0 + tn].rearrange("k p t -> p k t"), ont[:, :, 0:tn], reads=[onb], writes=[self.b_on])

                s1 = {}

                def run_s1(k):
                    s1[k] = stage1(tiles[k][0], tiles[k][1], tis[k])

                run_s1(0)
                for k in range(len(tiles)):
                    fns = [lambda k=k: stage2(tiles[k][0], tiles[k][1], tis[k], s1.pop(k))]
                    if k + 1 < len(tiles):
                        fns.append(lambda k=k: run_s1(k + 1))
                    self.IL.run(fns)
            if state_only:
                xt = self.sb(es, "xch", [128, XW], F32)
                xb = Buf("xch")
                tl = T - 32
                ti = len(TILES) - 1
                for ck in range(KC):
                    cvp, cvb = self.ps()
                    for kc in range(KC):
                        S.op("pe", lambda kc=kc: nc.tensor.matmul(cvp[:, 0:32], lhsT=wCV[:, kc, ck * 128:(ck + 1) * 128], rhs=self.hT[:, kc, tl:T],
                                                                  start=(kc == 0), stop=(kc == KC - 1)),
                             reads=[self.b_hT[ti], wCVb[(ck * 128) // pc]], writes=[cvb], signal=(kc == KC - 1))
                    cgp, cgb = self.ps()
                    for kc in range(KC):
                        S.op("pe", lambda kc=kc: nc.tensor.matmul(cgp[:, 0:32], lhsT=wCG[:, kc, ck * 128:(ck + 1) * 128], rhs=self.hT[:, kc, tl:T],
                                                                  start=(kc == 0), stop=(kc == KC - 1)),
                             reads=[self.b_hT[ti], wCGb[(ck * 128) // pc]], writes=[cgb], signal=(kc == KC - 1))
                    e1, e1b = tr.next()
                    S.op("act", lambda: nc.scalar.activation(out=e1[:, 0:32], in_=cgp[:, 0:32], func=AF.Exp, scale=-1.0), reads=[cgb], writes=[e1b])
                    S.op("act", lambda: nc.scalar.activation(out=e1[:, 0:32], in_=e1[:, 0:32], func=AF.Ln, bias=1.0), reads=[e1b], writes=[e1b])
                    S.op("act", lambda: nc.scalar.activation(out=e1[:, 0:32], in_=e1[:, 0:32], func=AF.Exp, scale=-1.0), reads=[e1b], writes=[e1b])
                    S.op("dve", lambda: nc.vector.tensor_tensor(out=xt[:, 1024 + ck * HALO:1024 + (ck + 1) * HALO], in0=cvp[:, 2:32], in1=e1[:, 2:32], op=ALU.mult),
                         reads=[cvb, e1b], writes=[xb])
                S.op("dve", lambda: nc.vector.tensor_copy(out=xt[:, 0:1024], in_=self.Sst[:, :]), reads=[self.b_S] + self.b_Sh, writes=[xb])
                self.b_ccin = Buf("cc_in")
                S.dma("pool", self.cc_in[:, :], xt[:, :], reads=[xb], writes=[self.b_ccin])
            S.barrier()

    def exchange(self):
        nc, S = self.nc, self.S
        b_ccout = Buf("cc_out")
        S.collective(lambda: nc.gpsimd.collective_compute("AllGather", ALU.bypass, replica_groups=[[0, 1], [2, 3], [4, 5], [6, 7]],
                                                          ins=[self.cc_in[:, :]], outs=[self.cc_out[:, :]]),
                     reads=[self.b_ccin], writes=[b_ccout])
        with ExitStack() as es:
            xt = self.sb(es, "xin_t", [128, XW], F32)
            xb = Buf("xin_t")
            S.dma("sp", xt[:, :], self.cc_out[0:128, :], reads=[b_ccout], writes=[xb])
            S.op("dve", lambda: nc.vector.tensor_scalar(out=self.Sst[:, :], in0=xt[:, 0:1024], scalar1=self.flag[:, 0:1], scalar2=None, op0=ALU.mult),
                 reads=[xb, self.b_const], writes=[self.b_S] + self.b_Sh)
            S.op("dve", lambda: nc.vector.tensor_scalar(out=self.halo0[:, :, :], in0=xt[:, 1024:XW].rearrange("p (k t) -> p k t", t=HALO),
                                                        scalar1=self.flag[:, 0:1], scalar2=2.0, op0=ALU.mult, op1=ALU.mult),
                 reads=[xb, self.b_const], writes=[self.b_halo0])
            S.barrier()

    def phase_conv(self, l):
        nc, S, I = self.nc, self.S, self.I
        self.psr = self.psr6
        W = I["w_in"][l]
        BL = blocks(512)
        bc = self.b_const
        with ExitStack() as es:
            wCV, wCVb, pc = self.load_w(es, "wCV", W[:, OFF_CV:OFF_CV + D], D)
            wCG, wCGb, _ = self.load_w(es, "wCG", W[:, OFF_CG:OFF_CG + D], D)
            cw = self.sb(es, "cw", [128, KC, CW], F32)
            par = self.sb(es, "cpar", [128, 3, KC], F32)
            b_par = Buf("cpar")
            S.dma("sp", cw[:], I["conv_w"][l], writes=[b_par])
            S.dma("sp", par[:, 0, :], I["conv_b"][l], writes=[b_par])
            S.dma("sp", par[:, 1, :], I["conv_ln_g"][l], writes=[b_par])
            S.dma("sp", par[:, 2, :], I["conv_ln_b"][l], writes=[b_par])
            S.op("dve", lambda: nc.vector.tensor_scalar(out=cw[:], in0=cw[:], scalar1=0.5, scalar2=None, op0=ALU.mult), reads=[b_par], writes=[b_par])
            diag = self.sb(es, "diag", [128, KC, CW, 128], BF16)
            b_diag = Buf("diag")
            for ck in range(KC):
                S.op("pool", lambda ck=ck: nc.gpsimd.tensor_tensor(
                    out=diag[:, ck, :, :], in0=self.ident[:, :].rearrange("p (o n) -> p o n", o=1).to_broadcast([128, CW, 128]),
                    in1=cw[:, ck, :].rearrange("p (j o) -> p j o", o=1).to_broadcast([128, CW, 128]), op=ALU.mult),
                    reads=[b_par, bc], writes=[b_diag])
            uT = self.sb(es, "uT", [128, KC, HALO + 512], BF16)
            b_u = Buf("uT")
            b_uk = [Buf(f"uT{i}") for i in range(KC)]
            utmp = self.sb(es, "utmp", [128, KC, HALO], BF16)
            S.op("dve", lambda: nc.vector.tensor_copy(out=uT[:, :, 0:HALO], in_=self.halo0[:, :, :]), reads=[self.b_halo0], writes=[b_u] + b_uk)
            y_sb = self.sb(es, "cy", [128, KC, 512], F32)
            ysq = self.sb(es, "cysq", [128, KC, 512], BF16)
            b_y, b_ysq = Buf("cy"), Buf("cysq")
            b_yk = [Buf(f"cyk{i}") for i in range(KC)]
            yc_r = self.ring(es, "ycT", [128, KC, 512], BF16, 1)
            tr = self.ring(es, "cv_t", [128, 512], F32, 5)
            for bi, (t0, n) in enumerate(BL):
                tis = [TILES.index((t0 + o, min(128, n - o))) for o in range(0, n, 128)]
                hbufs = [self.b_hT[ti] for ti in tis]
                def proj_body(ck):
                    cvp, cvb = self.ps()
                    for kc in range(KC):
                        S.op("pe", lambda kc=kc: nc.tensor.matmul(cvp[:, 0:n], lhsT=wCV[:, kc, ck * 128:(ck + 1) * 128], rhs=self.hT[:, kc, t0:t0 + n],
                                                                  start=(kc == 0), stop=(kc == KC - 1)),
                             reads=hbufs + [wCVb[(ck * 128) // pc]], writes=[cvb], signal=(kc == KC - 1))
                    cgp, cgb = self.ps()
                    for kc in range(KC):
                        S.op("pe", lambda kc=kc: nc.tensor.matmul(cgp[:, 0:n], lhsT=wCG[:, kc, ck * 128:(ck + 1) * 128], rhs=self.hT[:, kc, t0:t0 + n],
                                                                  start=(kc == 0), stop=(kc == KC - 1)),
                             reads=hbufs + [wCGb[(ck * 128) // pc]], writes=[cgb], signal=(kc == KC - 1))
                    th, thb = tr.next()
                    S.op("act", lambda: nc.scalar.activation(out=th[:, 0:n], in_=cgp[:, 0:n], func=AF.Tanh, scale=0.5), reads=[cgb], writes=[thb])
                    S.op("dve", lambda: nc.vector.scalar_tensor_tensor(out=uT[:, ck, HALO:HALO + n], in0=th[:, 0:n], scalar=1.0, in1=cvp[:, 0:n],
                                                                       op0=ALU.add, op1=ALU.mult), reads=[thb, cvb, b_u], writes=[b_uk[ck]])
                for c0 in range(0, KC, 2):
                    self.IL.run([lambda c=c0: proj_body(c), lambda c=c0 + 1: proj_body(c)])

                def conv_body(ck):
                    yp, ypb = self.ps()
                    for j in range(CW):
                        S.op("pe", lambda j=j: nc.tensor.matmul(yp[:, 0:n], lhsT=diag[:, ck, j, :], rhs=uT[:, ck, j:j + n], start=(j == 0), stop=(j == CW - 1)),
                             reads=[b_u, b_uk[ck], b_diag], writes=[ypb], signal=(j == CW - 1))
                    S.op("act", lambda: nc.scalar.activation(out=y_sb[:, ck, 0:n], in_=yp[:, 0:n], func=AF.Identity, bias=par[:, 0, ck:ck + 1]),
                         reads=[ypb, b_par], writes=[b_y, b_yk[ck]])
                    S.op("act", lambda: nc.scalar.activation(out=ysq[:, ck, 0:n], in_=yp[:, 0:n], func=AF.Square, bias=par[:, 0, ck:ck + 1]),
                         reads=[ypb, b_par], writes=[b_ysq])
                for c0 in range(0, KC, 2):
                    self.IL.run([lambda c=c0: conv_body(c), lambda c=c0 + 1: conv_body(c)])
                S.op("pool", lambda: nc.gpsimd.tensor_copy(out=utmp[:, :, :], in_=uT[:, :, n:n + HALO]), reads=[b_u] + b_uk, writes=[b_u])
                S.op("pool", lambda: nc.gpsimd.tensor_copy(out=uT[:, :, 0:HALO], in_=utmp[:, :, :]), reads=[b_u], writes=[b_u] + b_uk)
                mp, mpb = self.ps()
                for ck in range(KC):
                    S.op("pe", lambda ck=ck: nc.tensor.matmul(mp[:, 0:n], lhsT=self.onesf[:, :], rhs=y_sb[:, ck, 0:n], start=(ck == 0), stop=(ck == KC - 1)),
                         reads=[b_y, bc], writes=[mpb], signal=(ck == KC - 1))
                qp, qpb = self.ps()
                for ck in range(KC):
                    S.op("pe", lambda ck=ck: nc.tensor.matmul(qp[:, 0:n], lhsT=self.onesb[:, :], rhs=ysq[:, ck, 0:n], start=(ck == 0), stop=(ck == KC - 1)),
                         reads=[b_ysq, bc], writes=[qpb], signal=(ck == KC - 1))
                mean, meb = tr.next()
                var, vab = tr.next()
                S.op("act", lambda: nc.scalar.mul(out=mean[:, 0:n], in_=mp[:, 0:n], mul=1.0 / D), reads=[mpb], writes=[meb])
                S.op("dve", lambda: nc.vector.tensor_tensor(out=var[:, 0:n], in0=mean[:, 0:n], in1=mean[:, 0:n], op=ALU.mult), reads=[meb], writes=[vab])
                S.op("dve", lambda: nc.vector.scalar_tensor_tensor(out=var[:, 0:n], in0=qp[:, 0:n], scalar=1.0 / D, in1=var[:, 0:n], op0=ALU.mult, op1=ALU.subtract),
                     reads=[qpb], writes=[vab])
                S.op("act", lambda: nc.scalar.activation(out=var[:, 0:n], in_=var[:, 0:n], func=AF.Ln, bias=LN_EPS), reads=[vab], writes=[vab])
                S.op("act", lambda: nc.scalar.activation(out=var[:, 0:n], in_=var[:, 0:n], func=AF.Exp, scale=-0.5), reads=[vab], writes=[vab])
                S.op("dve", lambda: nc.vector.scalar_tensor_tensor(out=mean[:, 0:n], in0=mean[:, 0:n], scalar=-1.0, in1=var[:, 0:n], op0=ALU.mult, op1=ALU.mult),
                     reads=[vab], writes=[meb])
                yct, ycb = yc_r.next()
                def norm_body(ck):
                    S.op("dve", lambda ck=ck: nc.vector.tensor_tensor(out=y_sb[:, ck, 0:n], in0=y_sb[:, ck, 0:n], in1=var[:, 0:n], op=ALU.mult), reads=[vab, b_y], writes=[b_yk[ck]])
                    S.op("pool", lambda ck=ck: nc.gpsimd.tensor_tensor(out=y_sb[:, ck, 0:n], in0=y_sb[:, ck, 0:n], in1=mean[:, 0:n], op=ALU.add), reads=[meb, b_yk[ck]], writes=[b_yk[ck]])
                    S.op("act", lambda ck=ck: nc.scalar.activation(out=yct[:, ck, 0:n], in_=y_sb[:, ck, 0:n], func=AF.Silu, scale=par[:, 1, ck:ck + 1], bias=par[:, 2, ck:ck + 1]),
                         reads=[b_yk[ck], b_par, ycb], writes=[ycb_k[ck]])
                ycb_k = [Buf(f"ycb{i}") for i in range(KC)]
                for c0 in range(0, KC, 4):
                    self.IL.run([lambda c=c0 + i: norm_body(c) for i in range(4)])
                S.dma("pool", self.yc_scr[:, :, t0:t0 + n].rearrange("k p t -> p k t"), yct[:, :, 0:n], reads=[ycb] + ycb_k, writes=[self.b_yc, ycb])
            S.barrier()

    def phase_mix(self, l):
        nc, S, I = self.nc, self.S, self.I
        self.psr = self.psr6
        W = I["w_in"][l]
        NB = 256
        BL = blocks(NB)
        bc = self.b_const
        moe = (l % 2 == 1)
        with ExitStack() as es:
            wOG, wOGb, pc = self.load_w(es, "wOG", W[:, OFF_OG:OFF_OG + D], D)
            wHG, wHGb, _ = self.load_w(es, "wHG", I["w_hg_out"][l], D)
            wGA, wGAb, _ = self.load_w(es, "wGA", W[:, OFF_GA:OFF_GA + D], D)
            wCO, wCOb, _ = self.load_w(es, "wCO", I["w_conv_out"][l], D)
            wGB, wGBb, _ = self.load_w(es, "wGB", W[:, OFF_GB:OFF_GB + D], D)
            wO, wOb, _ = self.load_w(es, "wO", I["w_out"][l], D)
            hg = self.sb(es, "hg_g", [128, KC], F32)
            b_par = Buf("mixpar")
            S.dma("sp", hg[:], I["hg_norm_g"][l], writes=[b_par])
            g_t = self.sb(es, "ln1_g", [128, D], F32)
            b_t = self.sb(es, "ln1_b", [128, D], F32)
            gb = Buf("ln1_gb")
            if moe:
                wr = self.sb(es, "wr", [128, KC, N_EXP], F32)
                S.dma("sp", wr[:], I["moe_router"][0], writes=[b_par])
                lg_all = self.sb(es, "lg_all", [128, len(TILES), N_EXP], F32)
                b_lg = Buf("lg_all")
                S.op("dve", lambda: nc.vector.memset(lg_all[:], 0.0), writes=[b_lg])
                h32T_r = self.ring(es, "h32T", [128, KC, 128], F32, 2)
            on_r = self.ring(es, "on_blk", [128, KC, NB], BF16, 2)
            yc_r = self.ring(es, "yc_blk", [128, KC, NB], BF16, 2)
            gt_r = self.ring(es, "gatedT", [128, KC, NB], BF16, 1)
            mx_r = self.ring(es, "mixedT", [128, KC, NB], BF16, 1)
            tr = self.ring(es, "mx_t", [128, NB], F32, 4)
            ho_r = self.ring(es, "h_old", [128, D], F32, 2)
            ah_r = self.ring(es, "ah_t", [128, D], F32, 2)
            hbr = self.ring(es, "mx_hb", [128, D], BF16, 2)
            sc = [self.ln_scratch(es, f"ln1s{i}") for i in range(2)]
            loaded = {}

            def load(bi):
                t0, n = BL[bi]
                ont, onb = on_r.next()
                yct, ycb = yc_r.next()
                S.dma("sp", ont[:, :, 0:n], self.on_scr[:, :, t0:t0 + n].rearrange("k p t -> p k t"), reads=[self.b_on], writes=[onb])
                S.dma("sp", yct[:, :, 0:n], self.yc_scr[:, :, t0:t0 + n].rearrange("k p t -> p k t"), reads=[self.b_yc], writes=[ycb])
                loaded[bi] = (ont, onb, yct, ycb)

            load(0)
            for bi, (t0, n) in enumerate(BL):
                if bi + 1 < len(BL):
                    load(bi + 1)
                if bi == 0:
                    S.dma("sp", g_t[:], I["ln1_g"][l:l + 1, :].to_broadcast([128, D]), writes=[gb])
                    S.dma("sp", b_t[:], I["ln1_b"][l:l + 1, :].to_broadcast([128, D]), writes=[gb])
                ont, onb, yct, ycb = loaded.pop(bi)
                tiles = [(t0 + o, min(128, n - o)) for o in range(0, n, 128)]
                tis = [TILES.index(tt) for tt in tiles]
                hbufs = [self.b_hT[ti] for ti in tis]
                hold = []
                for (tt0, tn), ti_ in zip(tiles, tis):
                    hot, hob = ho_r.next()
                    S.dma("sp", hot[0:tn, :], self.h_scr[tt0:tt0 + tn, :], reads=[self.b_hs[ti_]], writes=[hob])
                    hold.append((hot, hob))
                gtt, gtb = gt_r.next()
                gtk = [Buf(f"gt{i}") for i in range(KC)]
                S.op("dve", lambda: nc.vector.memset(gtt[:, 0, 0:1], 0.0), writes=[gtb] + gtk)

                def og_body(ck):
                    ogp, ogb = self.ps()
                    for kc in range(KC):
                        S.op("pe", lambda kc=kc: nc.tensor.matmul(ogp[:, 0:n], lhsT=wOG[:, kc, ck * 128:(ck + 1) * 128], rhs=self.hT[:, kc, t0:t0 + n],
                                                                  start=(kc == 0), stop=(kc == KC - 1)),
                             reads=hbufs + [wOGb[(ck * 128) // pc]], writes=[ogb], signal=(kc == KC - 1))
                    sg, sgb = tr.next()
                    S.op("act", lambda: nc.scalar.activation(out=sg[:, 0:n], in_=ogp[:, 0:n], func=AF.Silu), reads=[ogb], writes=[sgb])
                    S.op("dve", lambda: nc.vector.scalar_tensor_tensor(out=gtt[:, ck, 0:n], in0=ont[:, ck, 0:n], scalar=hg[:, ck:ck + 1], in1=sg[:, 0:n],
                                                                       op0=ALU.mult, op1=ALU.mult), reads=[onb, sgb, b_par, gtb], writes=[gtk[ck]])
                for c0 in range(0, KC, 2):
                    self.IL.run([lambda c=c0: og_body(c), lambda c=c0 + 1: og_body(c)])
                mxt, mxb = mx_r.next()
                mxk = [Buf(f"mx{i}") for i in range(KC)]
                S.op("dve", lambda: nc.vector.memset(mxt[:, 0, 0:1], 0.0), writes=[mxb] + mxk)

                def m_body(m):
                    yrp, yrb = self.ps()
                    for kc in range(KC):
                        S.op("pe", lambda kc=kc: nc.tensor.matmul(yrp[:, 0:n], lhsT=wHG[:, kc, m * 128:(m + 1) * 128], rhs=gtt[:, kc, 0:n],
                                                                  start=(kc == 0), stop=(kc == KC - 1)),
                             reads=gtk + [gtb, wHGb[(m * 128) // pc]], writes=[yrb], signal=(kc == KC - 1))
                    gap, gab = self.ps()
                    for kc in range(KC):
                        S.op("pe", lambda kc=kc: nc.tensor.matmul(gap[:, 0:n], lhsT=wGA[:, kc, m * 128:(m + 1) * 128], rhs=self.hT[:, kc, t0:t0 + n],
                                                                  start=(kc == 0), stop=(kc == KC - 1)),
                             reads=hbufs + [wGAb[(m * 128) // pc]], writes=[gab], signal=(kc == KC - 1))
                    ta, tab = tr.next()
                    S.op("act", lambda: nc.scalar.activation(out=ta[:, 0:n], in_=gap[:, 0:n], func=AF.Tanh, scale=0.5), reads=[gab], writes=[tab])
                    S.op("dve", lambda: nc.vector.scalar_tensor_tensor(out=ta[:, 0:n], in0=ta[:, 0:n], scalar=1.0, in1=yrp[:, 0:n], op0=ALU.add, op1=ALU.mult),
                         reads=[yrb], writes=[tab])
                    ycp, ycpb = self.ps()
                    for kc in range(KC):
                        S.op("pe", lambda kc=kc: nc.tensor.matmul(ycp[:, 0:n], lhsT=wCO[:, kc, m * 128:(m + 1) * 128], rhs=yct[:, kc, 0:n],
                                                                  start=(kc == 0), stop=(kc == KC - 1)),
                             reads=[ycb, wCOb[(m * 128) // pc]], writes=[ycpb], signal=(kc == KC - 1))
                    gbp, gbb = self.ps()
                    for kc in range(KC):
                        S.op("pe", lambda kc=kc: nc.tensor.matmul(gbp[:, 0:n], lhsT=wGB[:, kc, m * 128:(m + 1) * 128], rhs=self.hT[:, kc, t0:t0 + n],
                                                                  start=(kc == 0), stop=(kc == KC - 1)),
                             reads=hbufs + [wGBb[(m * 128) // pc]], writes=[gbb], signal=(kc == KC - 1))
                    tb_, tbb = tr.next()
                    S.op("act", lambda: nc.scalar.activation(out=tb_[:, 0:n], in_=gbp[:, 0:n], func=AF.Tanh, scale=0.5), reads=[gbb], writes=[tbb])
                    S.op("dve", lambda: nc.vector.scalar_tensor_tensor(out=tb_[:, 0:n], in0=tb_[:, 0:n], scalar=1.0, in1=ycp[:, 0:n], op0=ALU.add, op1=ALU.mult),
                         reads=[ycpb], writes=[tbb])
                    S.op("dve", lambda: nc.vector.tensor_tensor(out=mxt[:, m, 0:n], in0=ta[:, 0:n], in1=tb_[:, 0:n], op=ALU.add), reads=[tab, tbb, mxb], writes=[mxk[m]])
                for m0 in range(0, KC, 2):
                    self.IL.run([lambda m=m0: m_body(m), lambda m=m0 + 1: m_body(m)])
                def ln1_body(tt0, tn, ti, hot, hob):
                    o0 = tt0 - t0
                    rt, rb = hot, hob
                    for hh in range(2):
                        pp, ppb = self.ps()
                        for kc in range(KC):
                            S.op("pe", lambda kc=kc: nc.tensor.matmul(pp[0:tn, :], lhsT=mxt[:, kc, o0:o0 + tn], rhs=wO[:, kc, hh * 512:(hh + 1) * 512],
                                                                      start=(kc == 0), stop=(kc == KC - 1)),
                                 reads=mxk + [mxb, wOb[(hh * 512) // pc]], writes=[ppb], signal=(kc == KC - 1))
                        S.op("dve", lambda hh=hh, pp=pp: nc.vector.scalar_tensor_tensor(out=rt[0:tn, hh * 512:(hh + 1) * 512], in0=pp[0:tn, :], scalar=0.5,
                                                                                       in1=hot[0:tn, hh * 512:(hh + 1) * 512], op0=ALU.mult, op1=ALU.add),
                             reads=[ppb, hob], writes=[rb])
                    hnt, hnb = hot, hob
                    self.layer_norm_tile(sc[ti % 2], rt[0:tn, :], rb, tn, g_t, b_t, gb, hnt[0:tn, :], hnb)
                    if ti == 0:
                        S.op("dve", lambda: nc.vector.tensor_scalar(out=hnt[0:tn, :], in0=hnt[0:tn, :], scalar1=self.flag[0:tn, 0:1],
                                                                    scalar2=None, op0=ALU.mult), reads=[bc], writes=[hnb])
                    aht, ahb = ah_r.next()
                    S.op("act", lambda: nc.scalar.mul(out=aht[0:tn, :], in_=hnt[0:tn, :], mul=ALPHA), reads=[hnb], writes=[ahb])
                    S.dma("pool", self.h_scr[tt0:tt0 + tn, :], aht[0:tn, :], reads=[ahb], writes=[self.b_hs[ti]])
                    hbt, hbb = hbr.next()
                    self.to_hT(hnt[0:tn, :], hnb, tn, ti, hbt[0:tn, :], hbb)
                    if moe:
                        h32, h32b = h32T_r.next()
                        for half in range(2):
                            tp, tpb = self.ps()
                            for q in range(4):
                                kc = half * 4 + q
                                S.op("pe", lambda kc=kc, q=q: nc.tensor.transpose(tp[:, q * 128:q * 128 + tn], hnt[0:tn, kc * 128:(kc + 1) * 128], self.identf[0:tn, 0:tn]),
                                     reads=[hnb, bc], writes=[tpb], signal=(q == 3))
                            S.op("act", lambda half=half, tp=tp: nc.scalar.copy(out=h32[:, half * 4:half * 4 + 4, 0:tn],
                                                                               in_=tp[:, :].rearrange("p (k t) -> p k t", t=128)[:, :, 0:tn]),
                                 reads=[tpb], writes=[h32b])
                        lp, lpb = self.ps()
                        for kc in range(KC):
                            S.op("pe", lambda kc=kc: nc.tensor.matmul(lp[0:tn, 0:N_EXP], lhsT=h32[:, kc, 0:tn], rhs=wr[:, kc, :], start=(kc == 0), stop=(kc == KC - 1)),
                                 reads=[h32b, b_par], writes=[lpb], signal=(kc == KC - 1))
                        S.op("dve", lambda: nc.vector.tensor_copy(out=lg_all[0:tn, ti, :], in_=lp[0:tn, 0:N_EXP]), reads=[lpb], writes=[b_lg])
                self.IL.run([lambda a=a: ln1_body(a[0][0], a[0][1], a[1], a[2][0], a[2][1]) for a in zip(tiles, tis, hold)])
            if moe:
                NT = len(TILES)
                mx8 = self.sb(es, "mx8", [128, NT, 8], F32)
                msk = self.sb(es, "msk", [128, NT, 8], F32)
                den = self.sb(es, "den", [128, NT, 1], F32)
                b_g = Buf("gtmp")
                for ti in range(NT):
                    S.op("dve", lambda ti=ti: nc.vector.max(out=mx8[:, ti, :], in_=lg_all[:, ti, :]), reads=[b_lg], writes=[b_g], signal=(ti == NT - 1))
                S.op("dve", lambda: nc.vector.tensor_tensor(out=msk[:], in0=lg_all[:], in1=mx8[:, :, 1:2].to_broadcast([128, NT, 8]), op=ALU.is_ge),
                     reads=[b_lg, b_g], writes=[b_g])
                S.op("dve", lambda: nc.vector.tensor_tensor(out=lg_all[:], in0=lg_all[:], in1=mx8[:, :, 0:1].to_broadcast([128, NT, 8]), op=ALU.subtract),
                     reads=[b_g], writes=[b_lg])
                S.op("act", lambda: nc.scalar.activation(out=lg_all[:], in_=lg_all[:], func=AF.Exp), reads=[b_lg], writes=[b_lg])
                S.op("dve", lambda: nc.vector.tensor_tensor(out=lg_all[:], in0=lg_all[:], in1=msk[:], op=ALU.mult), reads=[b_g], writes=[b_lg])
                S.op("dve", lambda: nc.vector.tensor_reduce(out=den[:].rearrange("p n o -> p (n o)"), in_=lg_all[:], axis=AX.X, op=ALU.add), reads=[b_lg], writes=[b_g])
                S.op("dve", lambda: nc.vector.reciprocal(out=den[:], in_=den[:]), reads=[b_g], writes=[b_g])
                S.op("dve", lambda: nc.vector.tensor_tensor(out=self.gates[:], in0=lg_all[:], in1=den[:, :, 0:1].to_broadcast([128, NT, 8]), op=ALU.mult),
                     reads=[b_lg, b_g], writes=[self.b_gates])
            S.barrier()

    def phase_ffn(self, l, experts, dff, moe):
        nc, S, I = self.nc, self.S, self.I
        self.psr = self.psr6
        BL = blocks(512)
        bc = self.b_const
        G = 512
        groups = [(g0, min(G, dff - g0)) for g0 in range(0, dff, G)]
        last_layer = (l == DEPTH - 1)
        with ExitStack() as es:
            acc = self.sb(es, "acc", [128, len(TILES), D], F32)
            b_acc = [Buf(f"acc{i}") for i in range(len(TILES))]
            for ti, (t0, n) in enumerate(TILES):
                S.dma("sp", acc[0:n, ti, :], self.h_scr[t0:t0 + n, :], reads=[self.b_hs[ti]], writes=[b_acc[ti]])
            g_t = self.sb(es, "ln2_g", [128, D], F32)
            b_t = self.sb(es, "ln2_b", [128, D], F32)
            gb = Buf("ln2_gb")
            S.dma("sp", g_t[:], I["ln2_g"][l:l + 1, :].to_broadcast([128, D]), writes=[gb])
            S.dma("sp", b_t[:], I["ln2_b"][l:l + 1, :].to_broadcast([128, D]), writes=[gb])
            NWB = 2
            wg_r = self.ring(es, "wg", [128, KC, G], BF16, NWB)
            wu_r = self.ring(es, "wu", [128, KC, G], BF16, NWB)
            wd_r = self.ring(es, "wd", [128, G // 128, D], BF16, NWB)
            aT_r = self.ring(es, "aT", [128, G // 128, 512], BF16, 2)
            tr = self.ring(es, "ff_t", [128, 512], F32, 4)
            work = [(e, g) for e in range(len(experts)) for g in groups]
            loaded = {}

            def load(wi):
                e, (g0, gn) = work[wi]
                wgs, wus, wds = experts[e]
                wgt, wgb = wg_r.next()
                wut, wub = wu_r.next()
                wdt, wdb = wd_r.next()
                S.dma("pool", wgt[:, :, 0:gn], wgs[:, g0:g0 + gn].rearrange("(kc p) n -> p kc n", p=128), writes=[wgb])
                S.dma("pool", wut[:, :, 0:gn], wus[:, g0:g0 + gn].rearrange("(kc p) n -> p kc n", p=128), writes=[wub])
                for hh in range(2):
                    S.dma("pool", wdt[:, 0:gn // 128, hh * 512:(hh + 1) * 512],
                          wds[g0:g0 + gn, hh * 512:(hh + 1) * 512].rearrange("(j p) n -> p j n", p=128), writes=[wdb])
                loaded[wi] = (wgt, wgb, wut, wub, wdt, wdb)

            hn_r = self.ring(es, "h2_new", [128, D], F32, 2)
            ah_r = self.ring(es, "h2_ah", [128, D], F32, 2)
            hbr = self.ring(es, "h2_hb", [128, D], BF16, 2)
            sc = [self.ln_scratch(es, f"ln2s{i}") for i in range(2)]

            def ln2_body(ti):
                t0, n = TILES[ti]
                hnt, hnb = hn_r.next()
                self.layer_norm_tile(sc[ti % 2], acc[0:n, ti, :], b_acc[ti], n, g_t, b_t, gb, hnt[0:n, :], hnb)
                if last_layer:
                    S.dma("pool", self.y[t0:t0 + n, :], hnt[0:n, :], reads=[hnb], writes=[self.b_y])
                else:
                    if ti == 0:
                        S.op("dve", lambda: nc.vector.tensor_scalar(out=hnt[0:n, :], in0=hnt[0:n, :], scalar1=self.flag[0:n, 0:1],
                                                                    scalar2=None, op0=ALU.mult), reads=[bc], writes=[hnb])
                    aht, ahb = ah_r.next()
                    S.op("act", lambda: nc.scalar.mul(out=aht[0:n, :], in_=hnt[0:n, :], mul=ALPHA), reads=[hnb], writes=[ahb])
                    S.dma("pool", self.h_scr[t0:t0 + n, :], aht[0:n, :], reads=[ahb], writes=[self.b_hs[ti]])
                    hbt, hbb = hbr.next()
                    self.to_hT(hnt[0:n, :], hnb, n, ti, hbt[0:n, :], hbb)

            load(0)
            for wi, (e, (g0, gn)) in enumerate(work):
                if wi + 1 < len(work):
                    load(wi + 1)
                wgt, wgb, wut, wub, wdt, wdb = loaded.pop(wi)
                last_item = (wi == len(work) - 1)
                nj = gn // 128
                pending = None
                for bi, (t0, n) in enumerate(BL):
                    tiles = [(t0 + o, min(128, n - o)) for o in range(0, n, 128)]
                    tis = [TILES.index(tt) for tt in tiles]
                    hbufs = [self.b_hT[ti] for ti in tis]
                    aT, aTb = aT_r.next()
                    for j in range(nj):
                        gp, gpb = self.ps()
                        for kc in range(KC):
                            S.op("pe", lambda kc=kc: nc.tensor.matmul(gp[:, 0:n], lhsT=wgt[:, kc, j * 128:(j + 1) * 128], rhs=self.hT[:, kc, t0:t0 + n],
                                                                      start=(kc == 0), stop=(kc == KC - 1)),
                                 reads=hbufs + [wgb], writes=[gpb], signal=(kc == KC - 1))
                        up, upb = self.ps()
                        for kc in range(KC):
                            S.op("pe", lambda kc=kc: nc.tensor.matmul(up[:, 0:n], lhsT=wut[:, kc, j * 128:(j + 1) * 128], rhs=self.hT[:, kc, t0:t0 + n],
                                                                      start=(kc == 0), stop=(kc == KC - 1)),
                                 reads=hbufs + [wub], writes=[upb], signal=(kc == KC - 1))
                        sg, sgb = tr.next()
                        S.op("act", lambda: nc.scalar.activation(out=sg[:, 0:n], in_=gp[:, 0:n], func=AF.Silu), reads=[gpb], writes=[sgb])
                        S.op("dve", lambda j=j: nc.vector.tensor_tensor(out=aT[:, j, 0:n], in0=up[:, 0:n], in1=sg[:, 0:n], op=ALU.mult),
                             reads=[upb, sgb], writes=[aTb])
                    cur = (aT, aTb, tiles, tis, t0)
                    todo = ([pending] if pending is not None else []) + ([cur] if bi == len(BL) - 1 else [])
                    pending = cur
                    for (aT, aTb, tiles, tis, t0) in todo:
                      for (tt0, tn), ti in zip(tiles, tis):
                        o0 = tt0 - t0
                        for hh in range(2):
                            dp, dpb = self.ps()
                            for j in range(nj):
                                S.op("pe", lambda j=j: nc.tensor.matmul(dp[0:tn, :], lhsT=aT[:, j, o0:o0 + tn], rhs=wdt[:, j, hh * 512:(hh + 1) * 512],
                                                                        start=(j == 0), stop=(j == nj - 1)),
                                     reads=[aTb, wdb], writes=[dpb], signal=(j == nj - 1))
                            if moe:
                                S.op("dve", lambda hh=hh, dp=dp: nc.vector.scalar_tensor_tensor(
                                    out=acc[0:tn, ti, hh * 512:(hh + 1) * 512], in0=dp[0:tn, :], scalar=self.gates[0:tn, ti, e:e + 1],
                                    in1=acc[0:tn, ti, hh * 512:(hh + 1) * 512], op0=ALU.mult, op1=ALU.add),
                                    reads=[dpb, self.b_gates], writes=[b_acc[ti]])
                            else:
                                S.op("dve", lambda hh=hh, dp=dp: nc.vector.tensor_tensor(
                                    out=acc[0:tn, ti, hh * 512:(hh + 1) * 512], in0=dp[0:tn, :], in1=acc[0:tn, ti, hh * 512:(hh + 1) * 512], op=ALU.add),
                                    reads=[dpb], writes=[b_acc[ti]])
                        if last_item:
                            ln2_body(ti)
            S.barrier()


def pm(a):
    a = np.asarray(a, dtype=np.float32)
    return np.ascontiguousarray(a.reshape(a.shape[:-1] + (KC, 128)).swapaxes(-1, -2))


def make_in_maps(inputs, stop_after=None):
    x = np.asarray(inputs["x"], dtype=np.float32)
    meta = np.asarray(inputs["meta_tokens"], dtype=np.float32)
    B = x.shape[0]
    shared = {
        "ln_in_g": np.asarray(inputs["ln_in_g"], np.float32).reshape(1, D),
        "ln_in_b": np.asarray(inputs["ln_in_b"], np.float32).reshape(1, D),
        "w_in": np.ascontiguousarray(inputs["w_in"], dtype=np.float32),
        "lower_bounds": pm(inputs["lower_bounds"]),
        "hg_norm_g": pm(inputs["hg_norm_g"]),
        "w_hg_out": np.ascontiguousarray(inputs["w_hg_out"], dtype=np.float32),
        "conv_w": np.ascontiguousarray(np.asarray(inputs["conv_w"], np.float32).reshape(DEPTH, CW, KC, 128).transpose(0, 3, 2, 1)),
        "conv_b": pm(inputs["conv_b"]),
        "conv_ln_g": pm(inputs["conv_ln_g"]),
        "conv_ln_b": pm(inputs["conv_ln_b"]),
        "w_conv_out": np.ascontiguousarray(inputs["w_conv_out"], dtype=np.float32),
        "w_out": np.ascontiguousarray(inputs["w_out"], dtype=np.float32),
        "ln1_g": np.ascontiguousarray(inputs["ln1_g"], dtype=np.float32),
        "ln1_b": np.ascontiguousarray(inputs["ln1_b"], dtype=np.float32),
        "ln2_g": np.ascontiguousarray(inputs["ln2_g"], dtype=np.float32),
        "ln2_b": np.ascontiguousarray(inputs["ln2_b"], dtype=np.float32),
    }
    full = stop_after is None
    if full or stop_after >= 5:
        shared["ffn_w_gate"] = np.ascontiguousarray(inputs["ffn_w_gate"], dtype=np.float32)
        shared["ffn_w_up"] = np.ascontiguousarray(inputs["ffn_w_up"], dtype=np.float32)
        shared["ffn_w_down"] = np.ascontiguousarray(inputs["ffn_w_down"], dtype=np.float32)
    if full or stop_after >= 10:
        shared["moe_router"] = np.ascontiguousarray(
            np.asarray(inputs["moe_router"], np.float32).reshape(1, KC, 128, N_EXP).transpose(0, 2, 1, 3))
        shared["moe_w_gate"] = np.ascontiguousarray(inputs["moe_w_gate"], dtype=np.float32)
        shared["moe_w_up"] = np.ascontiguousarray(inputs["moe_w_up"], dtype=np.float32)
        shared["moe_w_down"] = np.ascontiguousarray(inputs["moe_w_down"], dtype=np.float32)
    maps = []
    NA = T - 32
    for c in range(8):
        b, half = c // 2, c % 2
        if half == 0:
            xin = np.concatenate([np.zeros((16, D), np.float32), meta, x[b, :NA]], axis=0)
        else:
            xin = x[b, NA:]
        m = dict(shared)
        m["xin"] = np.ascontiguousarray(xin)
        m["flag"] = np.full((128, 1), float(half), np.float32)
        maps.append(m)
    return maps


_PROG_CACHE = {}


def kernel(**inputs):
    x = np.asarray(inputs["x"])
    B, SEQ, _ = x.shape
    if None not in _PROG_CACHE:
        _PROG_CACHE[None] = Prog()
    prog = _PROG_CACHE[None]
    maps = make_in_maps(inputs)
    res = run_bass_kernel_spmd(prog.nc, maps, core_ids=list(range(8)))
    NA = T - 32
    out = np.empty((B, SEQ, D), np.float32)
    for c in range(8):
        b, half = c // 2, c % 2
        y = res.results[c]["y"]
        if half == 0:
            out[b, :NA] = y[32:]
        else:
            out[b, NA:] = y
    return out
```

```python
import numpy as np
import threading
from contextlib import ExitStack
import concourse.bass as bass
import concourse.mybir as mybir
from concourse.bass_utils import run_bass_kernel_spmd

F32 = mybir.dt.float32
BF16 = mybir.dt.bfloat16
AF = mybir.ActivationFunctionType
ALU = mybir.AluOpType
AX = mybir.AxisListType

D = 1024
KC = 8
T = 2064
PRE = 16
NH = 8
DEPTH = 2
D_FF = 2816
N_EXP = 8
D_FFE = 3584
CW = 31
HALO = CW - 1
OFF_Q, OFF_F, OFF_I, OFF_OG, OFF_CV, OFF_CG, OFF_GA, OFF_GB = [i * 1024 for i in range(8)]
ALPHA = float((2 * DEPTH) ** 0.25)
LN_EPS = 1e-5
RMS_EPS = 1e-6
XW = 1024 + KC * HALO
CH = 128


def blocks(n):
    out = [(0, PRE)]
    t = PRE
    while t < T:
        out.append((t, n))
        t += n
    return out


TILES = blocks(128)


class Buf:
    __slots__ = ("name", "w", "r")

    def __init__(self, name="b"):
        self.name = name
        self.w = None
        self.r = {}


class Sched:
    def __init__(self, nc, n_dma_sems=40):
        self.nc = nc
        self.eng = {"pe": nc.tensor, "act": nc.scalar, "dve": nc.vector, "pool": nc.gpsimd, "sp": nc.sync}
        self.sem = {k: nc.semaphore("s_" + k).__enter__() for k in self.eng}
        self.cnt = {k: 0 for k in self.eng}
        self.waited = {k: {} for k in self.eng}
        self.dma_sems = [nc.semaphore(f"s_dma{i}").__enter__() for i in range(n_dma_sems)]
        self.dma_cnt = [0] * n_dma_sems
        self.dma_rr = 0
        self.nwaits = 0
        self.ninstr = 0
        self.switch = None

    def _wait(self, e, dep, war=False):
        kind, key, val = dep
        if kind == "eng":
            if key == e and (war or e in ("pe", "sp")):
                return
            assert val <= self.cnt[key], (e, dep, self.cnt[key])
            sem = self.sem[key]
        else:
            sem = self.dma_sems[key]
        wk = (kind, key)
        if self.waited[e].get(wk, 0) >= val:
            return
        self.eng[e].wait_ge(sem, val)
        self.waited[e][wk] = val
        self.nwaits += 1

    def _deps(self, e, reads, writes):
        for b in reads:
            if b.w is not None:
                self._wait(e, b.w)
        for b in writes:
            if b.w is not None:
                self._wait(e, b.w)
            for d in b.r.values():
                self._wait(e, d, war=True)

    def _mark(self, tok, reads, writes):
        for b in reads:
            b.r[(tok[0], tok[1])] = tok
        for b in writes:
            b.w = tok
            b.r = {}

    def op(self, e, fn, reads=(), writes=(), signal=True):
        self._deps(e, reads, writes)
        ins = fn()
        self.ninstr += 1
        if signal or e != "pe":
            self.cnt[e] += 1
            ins.then_inc(self.sem[e], 1)
            tok = ("eng", e, self.cnt[e])
        else:
            tok = ("eng", e, self.cnt[e] + 1)
        self._mark(tok, reads, writes)
        if self.switch is not None:
            self.switch()
        return ins

    def dma(self, q, out, in_, reads=(), writes=(), **kw):
        self._deps(q, reads, writes)
        si = self.dma_rr
        self.dma_rr = (self.dma_rr + 1) % len(self.dma_sems)
        ins = self.eng[q].dma_start(out=out, in_=in_, **kw)
        self.dma_cnt[si] += 16
        ins.then_inc(self.dma_sems[si], 16)
        tok = ("dma", si, self.dma_cnt[si])
        self._mark(tok, reads, writes)
        self.ninstr += 1
        if self.switch is not None:
            self.switch()
        return tok

    def collective(self, fn, reads=(), writes=()):
        self._deps("pool", reads, writes)
        si = self.dma_rr
        self.dma_rr = (self.dma_rr + 1) % len(self.dma_sems)
        ins = fn()
        self.dma_cnt[si] += 1
        ins.then_inc(self.dma_sems[si])
        tok = ("dma", si, self.dma_cnt[si])
        self._mark(tok, reads, writes)
        return tok

    def barrier(self):
        if getattr(self, "on_barrier", None):
            self.on_barrier()
        for e in self.eng:
            for e2 in self.eng:
                if e2 != e and self.cnt[e2] > 0:
                    self._wait(e, ("eng", e2, self.cnt[e2]))
            for si, c in enumerate(self.dma_cnt):
                if c > 0:
                    self._wait(e, ("dma", si, c))


class Interleaver:
    def __init__(self, sched):
        self.S = sched

    def run(self, fns):
        if len(fns) == 1:
            fns[0]()
            return
        n = len(fns)
        self.ev = [threading.Event() for _ in range(n)]
        self.alive = [True] * n
        self.tid = {}
        errs = []

        def worker(i):
            self.ev[i].wait()
            self.ev[i].clear()
            try:
                fns[i]()
            except BaseException as e:
                errs.append(e)
            self.alive[i] = False
            j = self._next(i)
            if j is not None:
                self.ev[j].set()

        ths = [threading.Thread(target=worker, args=(i,)) for i in range(n)]
        for i, t in enumerate(ths):
            t.start()
            self.tid[t.ident] = i
        self.S.switch = self._switch
        self.ev[0].set()
        for t in ths:
            t.join()
        self.S.switch = None
        if errs:
            raise errs[0]

    def _next(self, i):
        n = len(self.alive)
        for d in range(1, n):
            j = (i + d) % n
            if self.alive[j]:
                return j
        return None

    def _switch(self):
        i = self.tid.get(threading.get_ident())
        if i is None:
            return
        j = self._next(i)
        if j is None:
            return
        self.ev[j].set()
        self.ev[i].wait()
        self.ev[i].clear()


class Ring:
    def __init__(self, tiles):
        self.items = [(t, Buf()) for t in tiles]
        self.i = 0

    def next(self):
        it = self.items[self.i]
        self.i = (self.i + 1) % len(self.items)
        return it


class Prog:
    def __init__(self, stop_after=None, debug=False):
        self.stop_after = stop_after
        self.debug = debug
        nc = bass.Bass("TRN2", target_bir_lowering=False)
        self.nc = nc
        self.S = Sched(nc)
        self.IL = Interleaver(self.S)
        self.wq = []
        self.S.on_barrier = lambda: self.wq.clear()
        self.es = ExitStack()
        self.build()

    def din(self, name, shape, dt=F32):
        return self.nc.dram_tensor(name, list(shape), dt, kind="ExternalInput").ap()

    def dscr(self, name, shape, dt=F32, cc=False):
        if self.debug and not cc:
            return self.nc.dram_tensor(name, list(shape), dt, kind="ExternalOutput").ap()
        return self.nc.dram_tensor(name, list(shape), dt).ap()

    def sb(self, es, name, shape, dt):
        self._uid = getattr(self, "_uid", 0) + 1
        return es.enter_context(self.nc.sbuf_tensor(f"sb{self._uid}_{name}", list(shape), dt))

    def ring(self, es, name, shape, dt, n):
        return Ring([self.sb(es, f"{name}{i}", shape, dt) for i in range(n)])

    def ps(self):
        return self.psr.next()

    def load_w(self, es, name, src, ncols, eng="pool", piece=512):
        t = self.sb(es, name, [128, KC, ncols], BF16)
        bufs = []
        for c0 in range(0, ncols, piece):
            b = Buf(name)
            prev = []
            self.S.dma(eng, t[:, :, c0:c0 + piece],
                       src[:, c0:c0 + piece].rearrange("(kc p) n -> p kc n", p=128), reads=prev, writes=[b])
            self.wq.append(b)
            bufs.append(b)
        return t, bufs, piece

    def build(self):
        nc, S = self.nc, self.S
        es = self.es
        I = {}
        I["xin"] = self.din("xin", [T, D])
        I["flag"] = self.din("flag", [128, 1])
        I["ln_in_g"] = self.din("ln_in_g", [1, D])
        I["ln_in_b"] = self.din("ln_in_b", [1, D])
        I["w_in"] = self.din("w_in", [DEPTH, D, 8 * D])
        I["lower_bounds"] = self.din("lower_bounds", [DEPTH, 128, KC])
        I["hg_norm_g"] = self.din("hg_norm_g", [DEPTH, 128, KC])
        I["w_hg_out"] = self.din("w_hg_out", [DEPTH, D, D])
        I["conv_w"] = self.din("conv_w", [DEPTH, 128, KC, CW])
        I["conv_b"] = self.din("conv_b", [DEPTH, 128, KC])
        I["conv_ln_g"] = self.din("conv_ln_g", [DEPTH, 128, KC])
        I["conv_ln_b"] = self.din("conv_ln_b", [DEPTH, 128, KC])
        I["w_conv_out"] = self.din("w_conv_out", [DEPTH, D, D])
        I["w_out"] = self.din("w_out", [DEPTH, D, D])
        I["ln1_g"] = self.din("ln1_g", [DEPTH, D])
        I["ln1_b"] = self.din("ln1_b", [DEPTH, D])
        I["ln2_g"] = self.din("ln2_g", [DEPTH, D])
        I["ln2_b"] = self.din("ln2_b", [DEPTH, D])
        full = self.stop_after is None
        if full or self.stop_after >= 5:
            I["ffn_w_gate"] = self.din("ffn_w_gate", [1, D, D_FF])
            I["ffn_w_up"] = self.din("ffn_w_up", [1, D, D_FF])
            I["ffn_w_down"] = self.din("ffn_w_down", [1, D_FF, D])
        if full or self.stop_after >= 10:
            I["moe_router"] = self.din("moe_router", [1, 128, KC, N_EXP])
            I["moe_w_gate"] = self.din("moe_w_gate", [1, N_EXP, D, D_FFE])
            I["moe_w_up"] = self.din("moe_w_up", [1, N_EXP, D, D_FFE])
            I["moe_w_down"] = self.din("moe_w_down", [1, N_EXP, D_FFE, D])
        self.I = I
        self.y = nc.dram_tensor("y", [T, D], F32, kind="ExternalOutput").ap()
        self.h_scr = self.dscr("h_scr", [T, D])
        self.on_scr = self.dscr("on_scr", [KC, 128, T], BF16)
        self.yc_scr = self.dscr("yc_scr", [KC, 128, T], BF16)
        self.cc_in = self.dscr("cc_in", [128, XW], cc=True)
        self.cc_out = self.dscr("cc_out", [256, XW], cc=True)
        self.b_hscr = Buf("h_scr")
        self.b_hs = [Buf(f"h_scr{i}") for i in range(len(TILES))]
        self.b_on = Buf("on_scr")
        self.b_yc = Buf("yc_scr")
        self.b_y = Buf("y")

        self.hT = self.sb(es, "hT", [128, KC, T], BF16)
        self.b_hT = [Buf(f"hT{i}") for i in range(len(TILES))]
        self.ident = self.sb(es, "ident", [128, 128], BF16)
        self.identf = self.sb(es, "identf", [128, 128], F32)
        self.onesf = self.sb(es, "onesf", [128, 128], F32)
        self.onesb = self.sb(es, "onesb", [128, 128], BF16)
        self.maskA = self.sb(es, "maskA", [128, 128], F32)
        self.maskA4 = self.sb(es, "maskA4", [128, 4, 128], F32)
        self.neghalf = self.sb(es, "neghalf", [128, 1], F32)
        self.flag = self.sb(es, "flag", [128, 1], F32)
        self.lbt = self.sb(es, "lbt", [128, DEPTH, KC], F32)
        self.lnoml = self.sb(es, "lnoml", [128, DEPTH, KC], F32)
        self.Sst = self.sb(es, "Sst", [128, D], F32)
        self.halo0 = self.sb(es, "halo0", [128, KC, HALO], BF16)
        self.gates = self.sb(es, "gates", [128, len(TILES), N_EXP], F32)
        self.b_const = Buf("const")
        self.b_S = Buf("S")
        self.b_Sh = [Buf(f"S{h}") for h in range(NH)]
        self.b_halo0 = Buf("halo0")
        self.b_gates = Buf("gates")
        self.psr = Ring([es.enter_context(nc.psum_tensor(f"ps{i}", [128, 512], F32)) for i in range(4)])
        self.pshold = Ring([es.enter_context(nc.psum_tensor(f"psh{i}", [128, 512], F32)) for i in range(2)])
        self.pstr = Ring([es.enter_context(nc.psum_tensor(f"pst{i}", [128, 1024], BF16)) for i in range(2)])

        self.psr4 = self.psr
        self.psr6 = Ring([])
        self.psr6.items = self.psr.items + self.pshold.items
        self.psA = Ring([])
        self.psA.items = self.psr.items[0:2]
        self.psB = Ring([])
        self.psB.items = self.psr.items[2:4]
        self.setup_consts()
        self.wes = ExitStack()
        hw = self.load_hgrn_weights(self.wes, 0)
        self.phase_ln_in()
        if self.stop_after == 0:
            return self.finish()
        for l in range(DEPTH):
            base = 5 * l
            if l > 0:
                self.wes = ExitStack()
                hw = self.load_hgrn_weights(self.wes, l)
            self.hw = hw
            self.phase_hgrn(l, state_only=True)
            self.exchange()
            if self.stop_after == base + 1:
                return self.finish()
            self.phase_hgrn(l, state_only=False)
            self.wes.close()
            self.wes = None
            if self.stop_after == base + 2:
                return self.finish()
            self.phase_conv(l)
            if self.stop_after == base + 3:
                return self.finish()
            self.phase_mix(l)
            if self.stop_after == base + 4:
                return self.finish()
            if l % 2 == 0:
                self.phase_ffn(l, [(I["ffn_w_gate"][0], I["ffn_w_up"][0], I["ffn_w_down"][0])], D_FF, moe=False)
            else:
                ex = [(I["moe_w_gate"][0, e], I["moe_w_up"][0, e], I["moe_w_down"][0, e]) for e in range(N_EXP)]
                self.phase_ffn(l, ex, D_FFE, moe=True)
            if self.stop_after == base + 5:
                return self.finish()
        self.finish()

    def finish(self):
        S = self.S
        S.barrier()
        if getattr(self, "wes", None) is not None:
            self.wes.close()
            self.wes = None
        self.es.close()

    def load_hgrn_weights(self, es, l):
        W = self.I["w_in"][l]
        wF = self.load_w(es, "wF", W[:, OFF_F:OFF_F + D], D)
        wI = self.load_w(es, "wI", W[:, OFF_I:OFF_I + D], D)
        wQ = self.load_w(es, "wQ", W[:, OFF_Q:OFF_Q + D], D)
        return dict(F=wF, I=wI, Q=wQ)

    def setup_consts(self):
        nc, S, I = self.nc, self.S, self.I
        bc = self.b_const
        S.op("pool", lambda: nc.gpsimd.memset(self.ident[:], 0.0), writes=[bc])
        S.op("pool", lambda: nc.gpsimd.affine_select(out=self.ident[:], in_=self.ident[:], pattern=[[-1, 128]],
                                                     compare_op=ALU.not_equal, fill=1.0, base=0, channel_multiplier=1),
             reads=[bc], writes=[bc])
        S.op("pool", lambda: nc.gpsimd.memset(self.identf[:], 0.0), writes=[bc])
        S.op("pool", lambda: nc.gpsimd.affine_select(out=self.identf[:], in_=self.identf[:], pattern=[[-1, 128]],
                                                     compare_op=ALU.not_equal, fill=1.0, base=0, channel_multiplier=1),
             reads=[bc], writes=[bc])
        S.op("dve", lambda: nc.vector.memset(self.onesf[:], 1.0), writes=[bc])
        S.op("dve", lambda: nc.vector.memset(self.onesb[:], 1.0), writes=[bc])
        S.op("dve", lambda: nc.vector.memset(self.neghalf[:], -0.5), writes=[bc])
        S.op("pool", lambda: nc.gpsimd.memset(self.maskA[:], 1.0), writes=[bc])
        S.op("pool", lambda: nc.gpsimd.affine_select(out=self.maskA[:], in_=self.maskA[:], pattern=[[1, 128]],
                                                     compare_op=ALU.is_ge, fill=0.0, base=0, channel_multiplier=-1),
             reads=[bc], writes=[bc])
        if CH == 64:
            S.op("pool", lambda: nc.gpsimd.memset(self.maskA[0:64, 64:128], 0.0), writes=[bc])
        for q in range(4):
            S.op("pool", lambda q=q: nc.gpsimd.tensor_copy(out=self.maskA4[:, q, :], in_=self.maskA[:, :]), reads=[bc], writes=[bc])
        S.dma("sp", self.flag[:], I["flag"][:, :], writes=[bc])
        S.op("dve", lambda: nc.vector.memset(self.Sst[:], 0.0), writes=[self.b_S])
        S.op("dve", lambda: nc.vector.memset(self.halo0[:], 0.0), writes=[self.b_halo0])
        S.op("dve", lambda: nc.vector.memset(self.gates[:], 0.0), writes=[self.b_gates])
        with ExitStack() as es:
            a = self.sb(es, "lb_a", [128, DEPTH, KC], F32)
            e = self.sb(es, "lb_e", [128, DEPTH, KC], F32)
            m = self.sb(es, "lb_m", [128, KC], F32)
            s = self.sb(es, "lb_s", [128, KC], F32)
            cum = self.sb(es, "lb_c", [128, KC], F32)
            b = Buf("lbtmp")
            for l in range(DEPTH):
                S.dma("sp", a[:, l, :], I["lower_bounds"][l], writes=[b])
            S.op("dve", lambda: nc.vector.tensor_copy(out=m[:], in_=a[:, 0, :]), reads=[b], writes=[b])
            for l in range(1, DEPTH):
                S.op("dve", lambda l=l: nc.vector.tensor_tensor(out=m[:], in0=m[:], in1=a[:, l, :], op=ALU.max), reads=[b], writes=[b])
            for l in range(DEPTH):
                S.op("dve", lambda l=l: nc.vector.tensor_tensor(out=a[:, l, :], in0=a[:, l, :], in1=m[:], op=ALU.subtract), reads=[b], writes=[b])
            S.op("act", lambda: nc.scalar.activation(out=e[:], in_=a[:], func=AF.Exp), reads=[b], writes=[b])
            S.op("dve", lambda: nc.vector.tensor_copy(out=s[:], in_=e[:, 0, :]), reads=[b], writes=[b])
            for l in range(1, DEPTH):
                S.op("dve", lambda l=l: nc.vector.tensor_tensor(out=s[:], in0=s[:], in1=e[:, l, :], op=ALU.add), reads=[b], writes=[b])
            S.op("dve", lambda: nc.vector.reciprocal(out=s[:], in_=s[:]), reads=[b], writes=[b])
            for l in range(DEPTH):
                S.op("dve", lambda l=l: nc.vector.tensor_tensor(out=e[:, l, :], in0=e[:, l, :], in1=s[:], op=ALU.mult), reads=[b], writes=[b])
            S.op("dve", lambda: nc.vector.memset(cum[:], 0.0), writes=[b])
            for l in range(DEPTH):
                S.op("dve", lambda l=l: nc.vector.tensor_tensor(out=cum[:], in0=cum[:], in1=e[:, l, :], op=ALU.add), reads=[b], writes=[b])
                S.op("dve", lambda l=l: nc.vector.tensor_tensor(out=self.lbt[:, l, :], in0=cum[:], in1=e[:, 0, :], op=ALU.subtract), reads=[b], writes=[bc])
            S.op("act", lambda: nc.scalar.activation(out=self.lnoml[:], in_=self.lbt[:], func=AF.Ln, scale=-1.0, bias=1.0),
                 reads=[bc], writes=[bc])
            S.barrier()

    def ln_tile(self, es_bufs, src, n, g_t, b_t, ti, out_h, out_scale_h=None):
        raise NotImplementedError

    def layer_norm_tile(self, sc, x_ap, xbuf, n, g_t, b_t, gb_buf, out_ap, out_buf, eps=LN_EPS):
        nc, S = self.nc, self.S
        st, mv, rstd, b = sc["stats"], sc["mv"], sc["rstd"], sc["b"]
        for hh in range(2):
            S.op("dve", lambda hh=hh: nc.vector.bn_stats(out=st[0:n, hh, :], in_=x_ap[:, hh * 512:(hh + 1) * 512]),
                 reads=[xbuf], writes=[b])
        S.op("dve", lambda: nc.vector.bn_aggr(out=mv[0:n, :], in_=st[0:n, :, :]), reads=[b], writes=[b])
        S.op("dve", lambda: nc.vector.tensor_scalar(out=rstd[0:n, :], in0=mv[0:n, 1:2], scalar1=eps, scalar2=None, op0=ALU.add),
             reads=[b], writes=[b])
        S.op("pool", lambda: nc.gpsimd.tensor_tensor(out=rstd[0:n, :], in0=rstd[0:n, :], in1=self.neghalf[0:n, 0:1], op=ALU.pow),
             reads=[b, self.b_const], writes=[b])
        S.op("dve", lambda: nc.vector.tensor_scalar(out=out_ap, in0=x_ap, scalar1=mv[0:n, 0:1], scalar2=rstd[0:n, 0:1],
                                                    op0=ALU.subtract, op1=ALU.mult),
             reads=[xbuf, b], writes=[out_buf])
        S.op("dve", lambda: nc.vector.tensor_tensor(out=out_ap, in0=out_ap, in1=g_t[0:n, :], op=ALU.mult),
             reads=[gb_buf], writes=[out_buf])
        S.op("dve", lambda: nc.vector.tensor_tensor(out=out_ap, in0=out_ap, in1=b_t[0:n, :], op=ALU.add),
             reads=[gb_buf], writes=[out_buf])

    def to_hT(self, h_ap, hbuf, n, ti, hb_ap, hb_buf):
        nc, S = self.nc, self.S
        t0, _ = TILES[ti]
        S.op("act", lambda: nc.scalar.copy(out=hb_ap, in_=h_ap), reads=[hbuf], writes=[hb_buf])
        pt, pb = self.pstr.next()
        for kc in range(KC):
            S.op("pe", lambda kc=kc: nc.tensor.transpose(pt[:, kc * 128:kc * 128 + n], hb_ap[:, kc * 128:(kc + 1) * 128], self.ident[0:n, 0:n]),
                 reads=[hb_buf, self.b_const], writes=[pb], signal=(kc == KC - 1))
        S.op("dve", lambda: nc.vector.tensor_copy(out=self.hT[:, :, t0:t0 + n],
                                                  in_=pt[:].rearrange("p (k t) -> p k t", t=128)[:, :, 0:n]),
             reads=[pb], writes=[self.b_hT[ti]])

    def load_bcast(self, es, name, src_row):
        t = self.sb(es, name, [128, D], F32)
        return t

    def ln_scratch(self, es, tag):
        return {"stats": self.sb(es, tag + "_st", [128, 2, 6], F32), "mv": self.sb(es, tag + "_mv", [128, 2], F32),
                "rstd": self.sb(es, tag + "_rs", [128, 1], F32), "b": Buf(tag)}

    def phase_ln_in(self):
        nc, S, I = self.nc, self.S, self.I
        with ExitStack() as es:
            g_t = self.sb(es, "lni_g", [128, D], F32)
            b_t = self.sb(es, "lni_b", [128, D], F32)
            gb = Buf("lni_gb")
            xr = self.ring(es, "lni_x", [128, D], F32, 4)
            hr = self.ring(es, "lni_h", [128, D], F32, 2)
            ar = self.ring(es, "lni_a", [128, D], F32, 2)
            hbr = self.ring(es, "lni_hb", [128, D], BF16, 2)
            sc = [self.ln_scratch(es, f"lni{i}") for i in range(2)]
            loaded = {}

            def load(ti):
                t0, n = TILES[ti]
                xt, xb = xr.next()
                S.dma("sp", xt[0:n, :], I["xin"][t0:t0 + n, :], writes=[xb])
                loaded[ti] = (xt, xb)

            load(0)
            load(1)
            S.dma("sp", g_t[:], I["ln_in_g"][0:1, :].to_broadcast([128, D]), writes=[gb])
            S.dma("sp", b_t[:], I["ln_in_b"][0:1, :].to_broadcast([128, D]), writes=[gb])

            def tile_body(ti):
                t0, n = TILES[ti]
                xt, xb = loaded.pop(ti)
                ht, hb = hr.next()
                self.layer_norm_tile(sc[ti % 2], xt[0:n, :], xb, n, g_t, b_t, gb, ht[0:n, :], hb)
                if ti == 0:
                    S.op("dve", lambda: nc.vector.tensor_scalar(out=ht[0:n, :], in0=ht[0:n, :], scalar1=self.flag[0:n, 0:1],
                                                                scalar2=None, op0=ALU.mult), reads=[self.b_const], writes=[hb])
                at, ab = ar.next()
                S.op("act", lambda: nc.scalar.mul(out=at[0:n, :], in_=ht[0:n, :], mul=ALPHA), reads=[hb], writes=[ab])
                S.dma("pool", self.h_scr[t0:t0 + n, :], at[0:n, :], reads=[ab], writes=[self.b_hs[ti]])
                hbt, hbb = hbr.next()
                self.to_hT(ht[0:n, :], hb, n, ti, hbt[0:n, :], hbb)

            for p0 in range(0, len(TILES), 2):
                pair = [q for q in (p0, p0 + 1) if q < len(TILES)]
                for q in pair:
                    if q + 2 < len(TILES):
                        load(q + 2)
                self.IL.run([lambda q=q: tile_body(q) for q in pair])
            S.barrier()

    def phase_hgrn(self, l, state_only):
        nc, S, I = self.nc, self.S, self.I
        self.psr = self.psr4
        W = I["w_in"][l]
        BL = blocks(512)
        with ExitStack() as es:
            wF, wFb, pc = self.hw["F"]
            wI, wIb, _ = self.hw["I"]
            wQ, wQb, _ = self.hw["Q"]
            if state_only:
                wCV, wCVb, _ = self.load_w(es, "wCV", W[:, OFF_CV:OFF_CV + D], D)
                wCG, wCGb, _ = self.load_w(es, "wCG", W[:, OFF_CG:OFF_CG + D], D)
            tr = self.ring(es, "hg_t", [128, 512], F32, 14)
            self.mask_sc = self.sb(es, "mask_sc", [128, 512], F32)
            S.op("dve", lambda: nc.vector.memset(self.mask_sc[:], 1.0), writes=[self.b_const])
            S.op("dve", lambda: nc.vector.memset(self.mask_sc[:].rearrange("p (c t) -> p c t", t=CH)[:, :, 0:1], 0.0),
                 writes=[self.b_const])
            ktT = self.sb(es, "ktT", [128, NH, 512], BF16)
            b_kt = [Buf(f"kt{h}") for h in range(NH)]
            if not state_only:
                qtT = self.sb(es, "qtT", [128, NH, 512], BF16)
                b_qt = [Buf(f"qt{h}") for h in range(NH)]
                on_sb = self.ring(es, "on_sb", [128, NH, 128], BF16, 2)
                att_r = self.ring(es, "att", [128, NH, 128], BF16, 2)
                cl_r = self.ring(es, "attcl", [128, 4, 128], F32, 2)
                sq_r = self.ring(es, "sq", [128, NH, 128], BF16, 2)
                rs_r = self.ring(es, "rs", [128, NH, 128], F32, 2)
                St_r = self.ring(es, "St", [128, D], BF16, 3)
            dec = self.sb(es, "dec", [128, NH, 8], F32)
            edl = self.sb(es, "edl", [128, NH, 8], F32)
            ebr = self.sb(es, "ebr", [128, NH, 8], F32)
            b_sc = [Buf(f"chunk_scalars{h}") for h in range(NH)]
            v_r = self.ring(es, "v_sb", [128, D], BF16, 3)
            ktm_r = self.ring(es, "ktm", [128, D], BF16, 3)
            lb = self.lbt[:, l, :]
            lno = self.lnoml[:, l, :]
            bc = self.b_const

            for bi, (t0, n) in enumerate(BL):
                C = 16 if n == 16 else CH
                nch = n // C
                mid, last = C // 2 - 1, C - 1
                tiles = [(t0 + o, min(128, n - o)) for o in range(0, n, 128)]
                tis = [TILES.index(tt) for tt in tiles]
                hbufs = [self.b_hT[ti] for ti in tis]
                def head_body(h):
                    zp, zb = self.ps()
                    for kc in range(KC):
                        S.op("pe", lambda kc=kc: nc.tensor.matmul(zp[:, 0:n], lhsT=wF[:, kc, h * 128:(h + 1) * 128], rhs=self.hT[:, kc, t0:t0 + n],
                                                                  start=(kc == 0), stop=(kc == KC - 1)),
                             reads=hbufs + [wFb[(h * 128) // pc]], writes=[zb], signal=(kc == KC - 1))
                    E, Eb = tr.next()
                    L1, L1b = tr.next()
                    L2, L2b = tr.next()
                    lsn, lsnb = tr.next()
                    S.op("act", lambda: nc.scalar.activation(out=E[:, 0:n], in_=zp[:, 0:n], func=AF.Exp, scale=-1.0), reads=[zb], writes=[Eb])
                    S.op("act", lambda: nc.scalar.activation(out=L1[:, 0:n], in_=E[:, 0:n], func=AF.Ln, bias=1.0), reads=[Eb], writes=[L1b])
                    S.op("act", lambda: nc.scalar.activation(out=L2[:, 0:n], in_=E[:, 0:n], func=AF.Ln, scale=lb[:, h:h + 1], bias=1.0),
                         reads=[Eb, bc], writes=[L2b])
                    S.op("dve", lambda: nc.vector.scalar_tensor_tensor(out=lsn[:, 0:n], in0=zp[:, 0:n], scalar=-1.0, in1=L1[:, 0:n],
                                                                       op0=ALU.mult, op1=ALU.subtract), reads=[zb, L1b], writes=[lsnb])
                    S.op("dve", lambda: nc.vector.tensor_tensor(out=L2[:, 0:n], in0=L2[:, 0:n], in1=L1[:, 0:n], op=ALU.subtract),
                         reads=[L1b], writes=[L2b])
                    bt, bb = tr.next()
                    S.op("dve", lambda: nc.vector.tensor_tensor_scan(out=bt[:, 0:n], data0=self.mask_sc[:, 0:n], data1=L2[:, 0:n], initial=0.0,
                                                                     op0=ALU.mult, op1=ALU.add), reads=[L2b, bc], writes=[bb])
                    b3 = bt[:, 0:n].rearrange("p (c t) -> p c t", t=C)
                    df, dfb = tr.next()
                    S.op("dve", lambda: nc.vector.tensor_tensor(out=df[:, 0:n].rearrange("p (c t) -> p c t", t=C), in0=b3,
                                                                in1=b3[:, :, mid:mid + 1].to_broadcast([128, nch, C]), op=ALU.subtract),
                         reads=[bb], writes=[dfb])
                    d3 = df[:, 0:n].rearrange("p (c t) -> p c t", t=C)
                    S.op("act", lambda: nc.scalar.activation(out=dec[:, h, 0:nch], in_=b3[:, :, last], func=AF.Exp), reads=[bb], writes=[b_sc[h]])
                    S.op("act", lambda: nc.scalar.activation(out=ebr[:, h, 0:nch], in_=b3[:, :, mid], func=AF.Exp), reads=[bb], writes=[b_sc[h]])
                    S.op("act", lambda: nc.scalar.activation(out=edl[:, h, 0:nch], in_=d3[:, :, last], func=AF.Exp), reads=[dfb], writes=[b_sc[h]])
                    S.op("dve", lambda: nc.vector.tensor_tensor(out=lsn[:, 0:n], in0=lsn[:, 0:n], in1=df[:, 0:n], op=ALU.subtract),
                         reads=[dfb], writes=[lsnb])
                    S.op("act", lambda: nc.scalar.activation(out=ktT[:, h, 0:n], in_=lsn[:, 0:n], func=AF.Exp, bias=lno[:, h:h + 1]),
                         reads=[lsnb, bc], writes=[b_kt[h]])
                    if not state_only:
                        qp, qb = self.ps()
                        for kc in range(KC):
                            S.op("pe", lambda kc=kc: nc.tensor.matmul(qp[:, 0:n], lhsT=wQ[:, kc, h * 128:(h + 1) * 128], rhs=self.hT[:, kc, t0:t0 + n],
                                                                      start=(kc == 0), stop=(kc == KC - 1)),
                                 reads=hbufs + [wQb[(h * 128) // pc]], writes=[qb], signal=(kc == KC - 1))
                        Eq, Eqb = tr.next()
                        S.op("act", lambda: nc.scalar.activation(out=Eq[:, 0:n], in_=qp[:, 0:n], func=AF.Exp, scale=-1.0), reads=[qb], writes=[Eqb])
                        S.op("act", lambda: nc.scalar.activation(out=Eq[:, 0:n], in_=Eq[:, 0:n], func=AF.Ln, bias=1.0), reads=[Eqb], writes=[Eqb])
                        S.op("dve", lambda: nc.vector.tensor_tensor(out=Eq[:, 0:n], in0=df[:, 0:n], in1=Eq[:, 0:n], op=ALU.subtract),
                             reads=[dfb], writes=[Eqb])
                        S.op("act", lambda: nc.scalar.activation(out=Eq[:, 0:n], in_=Eq[:, 0:n], func=AF.Exp), reads=[Eqb], writes=[Eqb])
                        S.op("dve", lambda: nc.vector.tensor_tensor(out=qtT[:, h, 0:n], in0=qp[:, 0:n], in1=Eq[:, 0:n], op=ALU.mult),
                             reads=[qb, Eqb], writes=[b_qt[h]])
                for h0 in range(0, NH, 2):
                    self.IL.run([lambda h=h0: head_body(h), lambda h=h0 + 1: head_body(h)])
                psA, psB = self.psA, self.psB

                def stage1(tt0, tn, ti):
                    o0 = tt0 - t0
                    r = {}
                    vt, vb = v_r.next()
                    for hh in range(2):
                        vp, vpb = psA.next()
                        for kc in range(KC):
                            S.op("pe", lambda kc=kc: nc.tensor.matmul(vp[0:tn, :], lhsT=self.hT[:, kc, tt0:tt0 + tn], rhs=wI[:, kc, hh * 512:(hh + 1) * 512],
                                                                      start=(kc == 0), stop=(kc == KC - 1)),
                                 reads=[self.b_hT[ti], wIb[(hh * 512) // pc]], writes=[vpb], signal=(kc == KC - 1))
                        S.op("act", lambda: nc.scalar.copy(out=vt[0:tn, hh * 512:(hh + 1) * 512], in_=vp[0:tn, :]), reads=[vpb], writes=[vb])
                    pt, ptb = self.pstr.next()
                    for h in range(NH):
                        S.op("pe", lambda h=h: nc.tensor.transpose(pt[0:tn, h * 128:(h + 1) * 128], ktT[:, h, o0:o0 + tn], self.ident[:, :]),
                             reads=[b_kt[h], bc], writes=[ptb], signal=(h == NH - 1))
                    kt, kb = ktm_r.next()
                    S.op("dve", lambda: nc.vector.tensor_copy(out=kt[0:tn, :], in_=pt[0:tn, :]), reads=[ptb], writes=[kb])
                    r.update(vt=vt, vb=vb, kt=kt, kb=kb)
                    if not state_only:
                        at, atb = att_r.next()
                        for half in range(2):
                            ap_, apb = psA.next()
                            for hq in range(4):
                                h = half * 4 + hq
                                S.op("pe", lambda h=h, hq=hq: nc.tensor.matmul(ap_[0:tn, hq * 128:hq * 128 + tn], lhsT=ktT[:, h, o0:o0 + tn],
                                                                               rhs=qtT[:, h, o0:o0 + tn], start=True, stop=True),
                                     reads=[b_kt[h], b_qt[h]], writes=[apb], signal=(hq == 3))
                            cl, clb = cl_r.next()
                            S.op("dve", lambda half=half: nc.vector.tensor_scalar(
                                out=cl[0:tn, :, 0:tn], in0=ap_[0:tn, :].rearrange("p (h t) -> p h t", t=128)[:, :, 0:tn],
                                scalar1=1e30, scalar2=-1e30, op0=ALU.min, op1=ALU.max), reads=[apb], writes=[clb])
                            S.op("dve", lambda half=half: nc.vector.tensor_tensor(
                                out=at[0:tn, half * 4:half * 4 + 4, 0:tn], in0=cl[0:tn, :, 0:tn],
                                in1=self.maskA4[0:tn, :, 0:tn], op=ALU.mult),
                                reads=[clb, bc], writes=[atb])
                        r.update(at=at, atb=atb)
                    return r

                def stage2(tt0, tn, ti, r):
                    o0 = tt0 - t0
                    vt, vb, kt, kb = r["vt"], r["vb"], r["kt"], r["kb"]
                    gch = o0 // C
                    c0, cn = 0, tn
                    if not state_only:
                        at, atb = r["at"], r["atb"]
                        ont, onb = on_sb.next()
                        sqt, sqb = sq_r.next()
                        opl = []
                        Stt, Stb = St_r.next()
                        for h in range(NH):
                            S.op("act", lambda h=h: nc.scalar.activation(out=Stt[:, h * 128:(h + 1) * 128], in_=self.Sst[:, h * 128:(h + 1) * 128],
                                                                         func=AF.Identity, scale=ebr[:, h, gch:gch + 1]),
                                 reads=[self.b_S, self.b_Sh[h], b_sc[h]], writes=[Stb])
                        for half in range(2):
                            op_, opb = self.pshold.next()
                            opl.append((op_, opb))
                            for hq in range(4):
                                h = half * 4 + hq
                                S.op("pe", lambda h=h, hq=hq, op_=op_: nc.tensor.matmul(op_[:, hq * 128:hq * 128 + tn], lhsT=vt[0:tn, h * 128:(h + 1) * 128],
                                                                                     rhs=at[0:tn, h, 0:tn], start=(hq == 0), stop=False),
                                     reads=[vb, atb], writes=[opb], signal=False)
                        for half in range(2):
                            op_, opb = opl[half]
                            for hq in range(4):
                                h = half * 4 + hq
                                S.op("pe", lambda h=h, hq=hq, op_=op_: nc.tensor.matmul(op_[:, hq * 128:hq * 128 + tn], lhsT=Stt[:, h * 128:(h + 1) * 128],
                                                                                     rhs=qtT[:, h, o0:o0 + tn], start=False, stop=True),
                                     reads=[Stb, b_qt[h]], writes=[opb], signal=(hq == 3))
                    for half in range(2):
                        mp, mpb = psB.next()
                        for hq in range(4):
                            h = half * 4 + hq
                            S.op("pe", lambda h=h, hq=hq: nc.tensor.matmul(mp[:, hq * 128:(hq + 1) * 128], lhsT=kt[c0:c0 + cn, h * 128:(h + 1) * 128],
                                                                           rhs=vt[c0:c0 + cn, h * 128:(h + 1) * 128], start=True, stop=True),
                                 reads=[kb, vb], writes=[mpb], signal=(hq == 3))
                        for hq in range(4):
                            h = half * 4 + hq
                            S.op("dve", lambda h=h: nc.vector.tensor_scalar(out=self.Sst[:, h * 128:(h + 1) * 128], in0=self.Sst[:, h * 128:(h + 1) * 128],
                                                                            scalar1=dec[:, h, gch:gch + 1], scalar2=None, op0=ALU.mult),
                                 reads=[b_sc[h], self.b_S], writes=[self.b_Sh[h]])
                            S.op("dve", lambda h=h, hq=hq: nc.vector.scalar_tensor_tensor(out=self.Sst[:, h * 128:(h + 1) * 128], in0=mp[:, hq * 128:(hq + 1) * 128],
                                                                                          scalar=edl[:, h, gch:gch + 1], in1=self.Sst[:, h * 128:(h + 1) * 128],
                                                                                          op0=ALU.mult, op1=ALU.add),
                                 reads=[mpb, b_sc[h]], writes=[self.b_Sh[h]])
                    if not state_only:
                        for half in range(2):
                            op_, opb = opl[half]
                            S.op("act", lambda half=half, op_=op_: nc.scalar.activation(
                                out=sqt[:, half * 4:half * 4 + 4, 0:tn], in_=op_[:, :].rearrange("p (h t) -> p h t", t=128)[:, :, 0:tn], func=AF.Square),
                                reads=[opb], writes=[sqb])
                        rst, rsb = rs_r.next()
                        for half in range(2):
                            sp_, spb = psB.next()
                            if tn == 128:
                                S.op("pe", lambda half=half, sp_=sp_: nc.tensor.matmul(sp_[:, :], lhsT=self.onesb[:, :],
                                                                                       rhs=sqt[:, half * 4:half * 4 + 4, :].rearrange("p h t -> p (h t)"), start=True, stop=True),
                                     reads=[sqb, bc], writes=[spb])
                            else:
                                for hq in range(4):
                                    S.op("pe", lambda half=half, sp_=sp_, hq=hq: nc.tensor.matmul(sp_[:, hq * 128:hq * 128 + tn], lhsT=self.onesb[:, :],
                                                                                                rhs=sqt[:, half * 4 + hq, 0:tn], start=True, stop=True),
                                         reads=[sqb, bc], writes=[spb], signal=(hq == 3))
                            S.op("act", lambda half=half, sp_=sp_: nc.scalar.activation(
                                out=rst[:, half * 4:half * 4 + 4, 0:tn], in_=sp_[:, :].rearrange("p (h t) -> p h t", t=128)[:, :, 0:tn],
                                func=AF.Ln, scale=1.0 / 128.0, bias=RMS_EPS), reads=[spb], writes=[rsb])
                            S.op("act", lambda half=half: nc.scalar.activation(out=rst[:, half * 4:half * 4 + 4, 0:tn], in_=rst[:, half * 4:half * 4 + 4, 0:tn],
                                                                               func=AF.Exp, scale=-0.5), reads=[rsb], writes=[rsb])
                            op_, opb = opl[half]
                            S.op("dve", lambda half=half, op_=op_: nc.vector.tensor_tensor(
                                out=ont[:, half * 4:half * 4 + 4, 0:tn], in0=op_[:, :].rearrange("p (h t) -> p h t", t=128)[:, :, 0:tn],
                                in1=rst[:, half * 4:half * 4 + 4, 0:tn], op=ALU.mult), reads=[opb, rsb], writes=[onb])
                        S.dma("pool", self.on_scr[:, :, tt0:tt0 + tn].rearrange("k p t -> p k t"), ont[:, :, 0:tn], reads=[onb], writes=[self.b_on])

                s1 = {}

                def run_s1(k):
                    s1[k] = stage1(tiles[k][0], tiles[k][1], tis[k])

                run_s1(0)
                for k in range(len(tiles)):
                    fns = [lambda k=k: stage2(tiles[k][0], tiles[k][1], tis[k], s1.pop(k))]
                    if k + 1 < len(tiles):
                        fns.append(lambda k=k: run_s1(k + 1))
                    self.IL.run(fns)
            if state_only:
                xt = self.sb(es, "xch", [128, XW], F32)
                xb = Buf("xch")
                tl = T - 32
                ti = len(TILES) - 1
                for ck in range(KC):
                    cvp, cvb = self.ps()
                    for kc in range(KC):
                        S.op("pe", lambda kc=kc: nc.tensor.matmul(cvp[:, 0:32], lhsT=wCV[:, kc, ck * 128:(ck + 1) * 128], rhs=self.hT[:, kc, tl:T],
                                                                  start=(kc == 0), stop=(kc == KC - 1)),
                             reads=[self.b_hT[ti], wCVb[(ck * 128) // pc]], writes=[cvb], signal=(kc == KC - 1))
                    cgp, cgb = self.ps()
                    for kc in range(KC):
                        S.op("pe", lambda kc=kc: nc.tensor.matmul(cgp[:, 0:32], lhsT=wCG[:, kc, ck * 128:(ck + 1) * 128], rhs=self.hT[:, kc, tl:T],
                                                                  start=(kc == 0), stop=(kc == KC - 1)),
                             reads=[self.b_hT[ti], wCGb[(ck * 128) // pc]], writes=[cgb], signal=(kc == KC - 1))
                    e1, e1b = tr.next()
                    S.op("act", lambda: nc.scalar.activation(out=e1[:, 0:32], in_=cgp[:, 0:32], func=AF.Exp, scale=-1.0), reads=[cgb], writes=[e1b])
                    S.op("act", lambda: nc.scalar.activation(out=e1[:, 0:32], in_=e1[:, 0:32], func=AF.Ln, bias=1.0), reads=[e1b], writes=[e1b])
                    S.op("act", lambda: nc.scalar.activation(out=e1[:, 0:32], in_=e1[:, 0:32], func=AF.Exp, scale=-1.0), reads=[e1b], writes=[e1b])
                    S.op("dve", lambda: nc.vector.tensor_tensor(out=xt[:, 1024 + ck * HALO:1024 + (ck + 1) * HALO], in0=cvp[:, 2:32], in1=e1[:, 2:32], op=ALU.mult),
                         reads=[cvb, e1b], writes=[xb])
                S.op("dve", lambda: nc.vector.tensor_copy(out=xt[:, 0:1024], in_=self.Sst[:, :]), reads=[self.b_S] + self.b_Sh, writes=[xb])
                self.b_ccin = Buf("cc_in")
                S.dma("pool", self.cc_in[:, :], xt[:, :], reads=[xb], writes=[self.b_ccin])
            S.barrier()

    def exchange(self):
        nc, S = self.nc, self.S
        b_ccout = Buf("cc_out")
        S.collective(lambda: nc.gpsimd.collective_compute("AllGather", ALU.bypass, replica_groups=[[0, 1], [2, 3], [4, 5], [6, 7]],
                                                          ins=[self.cc_in[:, :]], outs=[self.cc_out[:, :]]),
                     reads=[self.b_ccin], writes=[b_ccout])
        with ExitStack() as es:
            xt = self.sb(es, "xin_t", [128, XW], F32)
            xb = Buf("xin_t")
            S.dma("sp", xt[:, :], self.cc_out[0:128, :], reads=[b_ccout], writes=[xb])
            S.op("dve", lambda: nc.vector.tensor_scalar(out=self.Sst[:, :], in0=xt[:, 0:1024], scalar1=self.flag[:, 0:1], scalar2=None, op0=ALU.mult),
                 reads=[xb, self.b_const], writes=[self.b_S] + self.b_Sh)
            S.op("dve", lambda: nc.vector.tensor_scalar(out=self.halo0[:, :, :], in0=xt[:, 1024:XW].rearrange("p (k t) -> p k t", t=HALO),
                                                        scalar1=self.flag[:, 0:1], scalar2=2.0, op0=ALU.mult, op1=ALU.mult),
                 reads=[xb, self.b_const], writes=[self.b_halo0])
            S.barrier()

    def phase_conv(self, l):
        nc, S, I = self.nc, self.S, self.I
        self.psr = self.psr6
        W = I["w_in"][l]
        BL = blocks(512)
        bc = self.b_const
        with ExitStack() as es:
            wCV, wCVb, pc = self.load_w(es, "wCV", W[:, OFF_CV:OFF_CV + D], D)
            wCG, wCGb, _ = self.load_w(es, "wCG", W[:, OFF_CG:OFF_CG + D], D)
            cw = self.sb(es, "cw", [128, KC, CW], F32)
            par = self.sb(es, "cpar", [128, 3, KC], F32)
            b_par = Buf("cpar")
            S.dma("sp", cw[:], I["conv_w"][l], writes=[b_par])
            S.dma("sp", par[:, 0, :], I["conv_b"][l], writes=[b_par])
            S.dma("sp", par[:, 1, :], I["conv_ln_g"][l], writes=[b_par])
            S.dma("sp", par[:, 2, :], I["conv_ln_b"][l], writes=[b_par])
            S.op("dve", lambda: nc.vector.tensor_scalar(out=cw[:], in0=cw[:], scalar1=0.5, scalar2=None, op0=ALU.mult), reads=[b_par], writes=[b_par])
            diag = self.sb(es, "diag", [128, KC, CW, 128], BF16)
            b_diag = Buf("diag")
            for ck in range(KC):
                S.op("pool", lambda ck=ck: nc.gpsimd.tensor_tensor(
                    out=diag[:, ck, :, :], in0=self.ident[:, :].rearrange("p (o n) -> p o n", o=1).to_broadcast([128, CW, 128]),
                    in1=cw[:, ck, :].rearrange("p (j o) -> p j o", o=1).to_broadcast([128, CW, 128]), op=ALU.mult),
                    reads=[b_par, bc], writes=[b_diag])
            uT = self.sb(es, "uT", [128, KC, HALO + 512], BF16)
            b_u = Buf("uT")
            b_uk = [Buf(f"uT{i}") for i in range(KC)]
            utmp = self.sb(es, "utmp", [128, KC, HALO], BF16)
            S.op("dve", lambda: nc.vector.tensor_copy(out=uT[:, :, 0:HALO], in_=self.halo0[:, :, :]), reads=[self.b_halo0], writes=[b_u] + b_uk)
            y_sb = self.sb(es, "cy", [128, KC, 512], F32)
            ysq = self.sb(es, "cysq", [128, KC, 512], BF16)
            b_y, b_ysq = Buf("cy"), Buf("cysq")
            b_yk = [Buf(f"cyk{i}") for i in range(KC)]
            yc_r = self.ring(es, "ycT", [128, KC, 512], BF16, 1)
            tr = self.ring(es, "cv_t", [128, 512], F32, 5)
            for bi, (t0, n) in enumerate(BL):
                tis = [TILES.index((t0 + o, min(128, n - o))) for o in range(0, n, 128)]
                hbufs = [self.b_hT[ti] for ti in tis]
                def proj_body(ck):
                    cvp, cvb = self.ps()
                    for kc in range(KC):
                        S.op("pe", lambda kc=kc: nc.tensor.matmul(cvp[:, 0:n], lhsT=wCV[:, kc, ck * 128:(ck + 1) * 128], rhs=self.hT[:, kc, t0:t0 + n],
                                                                  start=(kc == 0), stop=(kc == KC - 1)),
                             reads=hbufs + [wCVb[(ck * 128) // pc]], writes=[cvb], signal=(kc == KC - 1))
                    cgp, cgb = self.ps()
                    for kc in range(KC):
                        S.op("pe", lambda kc=kc: nc.tensor.matmul(cgp[:, 0:n], lhsT=wCG[:, kc, ck * 128:(ck + 1) * 128], rhs=self.hT[:, kc, t0:t0 + n],
                                                                  start=(kc == 0), stop=(kc == KC - 1)),
                             reads=hbufs + [wCGb[(ck * 128) // pc]], writes=[cgb], signal=(kc == KC - 1))
                    th, thb = tr.next()
                    S.op("act", lambda: nc.scalar.activation(out=th[:, 0:n], in_=cgp[:, 0:n], func=AF.Tanh, scale=0.5), reads=[cgb], writes=[thb])
                    S.op("dve", lambda: nc.vector.scalar_tensor_tensor(out=uT[:, ck, HALO:HALO + n], in0=th[:, 0:n], scalar=1.0, in1=cvp[:, 0:n],
                                                                       op0=ALU.add, op1=ALU.mult), reads=[thb, cvb, b_u], writes=[b_uk[ck]])
                for c0 in range(0, KC, 2):
                    self.IL.run([lambda c=c0: proj_body(c), lambda c=c0 + 1: proj_body(c)])

                def conv_body(ck):
                    yp, ypb = self.ps()
                    for j in range(CW):
                        S.op("pe", lambda j=j: nc.tensor.matmul(yp[:, 0:n], lhsT=diag[:, ck, j, :], rhs=uT[:, ck, j:j + n], start=(j == 0), stop=(j == CW - 1)),
                             reads=[b_u, b_uk[ck], b_diag], writes=[ypb], signal=(j == CW - 1))
                    S.op("act", lambda: nc.scalar.activation(out=y_sb[:, ck, 0:n], in_=yp[:, 0:n], func=AF.Identity, bias=par[:, 0, ck:ck + 1]),
                         reads=[ypb, b_par], writes=[b_y, b_yk[ck]])
                    S.op("act", lambda: nc.scalar.activation(out=ysq[:, ck, 0:n], in_=yp[:, 0:n], func=AF.Square, bias=par[:, 0, ck:ck + 1]),
                         reads=[ypb, b_par], writes=[b_ysq])
                for c0 in range(0, KC, 2):
                    self.IL.run([lambda c=c0: conv_body(c), lambda c=c0 + 1: conv_body(c)])
                S.op("pool", lambda: nc.gpsimd.tensor_copy(out=utmp[:, :, :], in_=uT[:, :, n:n + HALO]), reads=[b_u] + b_uk, writes=[b_u])
                S.op("pool", lambda: nc.gpsimd.tensor_copy(out=uT[:, :, 0:HALO], in_=utmp[:, :, :]), reads=[b_u], writes=[b_u] + b_uk)
                mp, mpb = self.ps()
                for ck in range(KC):
                    S.op("pe", lambda ck=ck: nc.tensor.matmul(mp[:, 0:n], lhsT=self.onesf[:, :], rhs=y_sb[:, ck, 0:n], start=(ck == 0), stop=(ck == KC - 1)),
                         reads=[b_y, bc], writes=[mpb], signal=(ck == KC - 1))
                qp, qpb = self.ps()
                for ck in range(KC):
                    S.op("pe", lambda ck=ck: nc.tensor.matmul(qp[:, 0:n], lhsT=self.onesb[:, :], rhs=ysq[:, ck, 0:n], start=(ck == 0), stop=(ck == KC - 1)),
                         reads=[b_ysq, bc], writes=[qpb], signal=(ck == KC - 1))
                mean, meb = tr.next()
                var, vab = tr.next()
                S.op("act", lambda: nc.scalar.mul(out=mean[:, 0:n], in_=mp[:, 0:n], mul=1.0 / D), reads=[mpb], writes=[meb])
                S.op("dve", lambda: nc.vector.tensor_tensor(out=var[:, 0:n], in0=mean[:, 0:n], in1=mean[:, 0:n], op=ALU.mult), reads=[meb], writes=[vab])
                S.op("dve", lambda: nc.vector.scalar_tensor_tensor(out=var[:, 0:n], in0=qp[:, 0:n], scalar=1.0 / D, in1=var[:, 0:n], op0=ALU.mult, op1=ALU.subtract),
                     reads=[qpb], writes=[vab])
                S.op("act", lambda: nc.scalar.activation(out=var[:, 0:n], in_=var[:, 0:n], func=AF.Ln, bias=LN_EPS), reads=[vab], writes=[vab])
                S.op("act", lambda: nc.scalar.activation(out=var[:, 0:n], in_=var[:, 0:n], func=AF.Exp, scale=-0.5), reads=[vab], writes=[vab])
                S.op("dve", lambda: nc.vector.scalar_tensor_tensor(out=mean[:, 0:n], in0=mean[:, 0:n], scalar=-1.0, in1=var[:, 0:n], op0=ALU.mult, op1=ALU.mult),
                     reads=[vab], writes=[meb])
                yct, ycb = yc_r.next()
                def norm_body(ck):
                    S.op("dve", lambda ck=ck: nc.vector.tensor_tensor(out=y_sb[:, ck, 0:n], in0=y_sb[:, ck, 0:n], in1=var[:, 0:n], op=ALU.mult), reads=[vab, b_y], writes=[b_yk[ck]])
                    S.op("pool", lambda ck=ck: nc.gpsimd.tensor_tensor(out=y_sb[:, ck, 0:n], in0=y_sb[:, ck, 0:n], in1=mean[:, 0:n], op=ALU.add), reads=[meb, b_yk[ck]], writes=[b_yk[ck]])
                    S.op("act", lambda ck=ck: nc.scalar.activation(out=yct[:, ck, 0:n], in_=y_sb[:, ck, 0:n], func=AF.Silu, scale=par[:, 1, ck:ck + 1], bias=par[:, 2, ck:ck + 1]),
                         reads=[b_yk[ck], b_par, ycb], writes=[ycb_k[ck]])
                ycb_k = [Buf(f"ycb{i}") for i in range(KC)]
                for c0 in range(0, KC, 4):
                    self.IL.run([lambda c=c0 + i: norm_body(c) for i in range(4)])
                S.dma("pool", self.yc_scr[:, :, t0:t0 + n].rearrange("k p t -> p k t"), yct[:, :, 0:n], reads=[ycb] + ycb_k, writes=[self.b_yc, ycb])
            S.barrier()

    def phase_mix(self, l):
        nc, S, I = self.nc, self.S, self.I
        self.psr = self.psr6
        W = I["w_in"][l]
        NB = 256
        BL = blocks(NB)
        bc = self.b_const
        moe = (l % 2 == 1)
        with ExitStack() as es:
            wOG, wOGb, pc = self.load_w(es, "wOG", W[:, OFF_OG:OFF_OG + D], D)
            wHG, wHGb, _ = self.load_w(es, "wHG", I["w_hg_out"][l], D)
            wGA, wGAb, _ = self.load_w(es, "wGA", W[:, OFF_GA:OFF_GA + D], D)
            wCO, wCOb, _ = self.load_w(es, "wCO", I["w_conv_out"][l], D)
            wGB, wGBb, _ = self.load_w(es, "wGB", W[:, OFF_GB:OFF_GB + D], D)
            wO, wOb, _ = self.load_w(es, "wO", I["w_out"][l], D)
            hg = self.sb(es, "hg_g", [128, KC], F32)
            b_par = Buf("mixpar")
            S.dma("sp", hg[:], I["hg_norm_g"][l], writes=[b_par])
            g_t = self.sb(es, "ln1_g", [128, D], F32)
            b_t = self.sb(es, "ln1_b", [128, D], F32)
            gb = Buf("ln1_gb")
            if moe:
                wr = self.sb(es, "wr", [128, KC, N_EXP], F32)
                S.dma("sp", wr[:], I["moe_router"][0], writes=[b_par])
                lg_all = self.sb(es, "lg_all", [128, len(TILES), N_EXP], F32)
                b_lg = Buf("lg_all")
                S.op("dve", lambda: nc.vector.memset(lg_all[:], 0.0), writes=[b_lg])
                h32T_r = self.ring(es, "h32T", [128, KC, 128], F32, 2)
            on_r = self.ring(es, "on_blk", [128, KC, NB], BF16, 2)
            yc_r = self.ring(es, "yc_blk", [128, KC, NB], BF16, 2)
            gt_r = self.ring(es, "gatedT", [128, KC, NB], BF16, 1)
            mx_r = self.ring(es, "mixedT", [128, KC, NB], BF16, 1)
            tr = self.ring(es, "mx_t", [128, NB], F32, 4)
            ho_r = self.ring(es, "h_old", [128, D], F32, 2)
            ah_r = self.ring(es, "ah_t", [128, D], F32, 2)
            hbr = self.ring(es, "mx_hb", [128, D], BF16, 2)
            sc = [self.ln_scratch(es, f"ln1s{i}") for i in range(2)]
            loaded = {}

            def load(bi):
                t0, n = BL[bi]
                ont, onb = on_r.next()
                yct, ycb = yc_r.next()
                S.dma("sp", ont[:, :, 0:n], self.on_scr[:, :, t0:t0 + n].rearrange("k p t -> p k t"), reads=[self.b_on], writes=[onb])
                S.dma("sp", yct[:, :, 0:n], self.yc_scr[:, :, t0:t0 + n].rearrange("k p t -> p k t"), reads=[self.b_yc], writes=[ycb])
                loaded[bi] = (ont, onb, yct, ycb)

            load(0)
            for bi, (t0, n) in enumerate(BL):
                if bi + 1 < len(BL):
                    load(bi + 1)
                if bi == 0:
                    S.dma("sp", g_t[:], I["ln1_g"][l:l + 1, :].to_broadcast([128, D]), writes=[gb])
                    S.dma("sp", b_t[:], I["ln1_b"][l:l + 1, :].to_broadcast([128, D]), writes=[gb])
                ont, onb, yct, ycb = loaded.pop(bi)
                tiles = [(t0 + o, min(128, n - o)) for o in range(0, n, 128)]
                tis = [TILES.index(tt) for tt in tiles]
                hbufs = [self.b_hT[ti] for ti in tis]
                hold = []
                for (tt0, tn), ti_ in zip(tiles, tis):
                    hot, hob = ho_r.next()
                    S.dma("sp", hot[0:tn, :], self.h_scr[tt0:tt0 + tn, :], reads=[self.b_hs[ti_]], writes=[hob])
                    hold.append((hot, hob))
                gtt, gtb = gt_r.next()
                gtk = [Buf(f"gt{i}") for i in range(KC)]
                S.op("dve", lambda: nc.vector.memset(gtt[:, 0, 0:1], 0.0), writes=[gtb] + gtk)

                def og_body(ck):
                    ogp, ogb = self.ps()
                    for kc in range(KC):
                        S.op("pe", lambda kc=kc: nc.tensor.matmul(ogp[:, 0:n], lhsT=wOG[:, kc, ck * 128:(ck + 1) * 128], rhs=self.hT[:, kc, t0:t0 + n],
                                                                  start=(kc == 0), stop=(kc == KC - 1)),
                             reads=hbufs + [wOGb[(ck * 128) // pc]], writes=[ogb], signal=(kc == KC - 1))
                    sg, sgb = tr.next()
                    S.op("act", lambda: nc.scalar.activation(out=sg[:, 0:n], in_=ogp[:, 0:n], func=AF.Silu), reads=[ogb], writes=[sgb])
                    S.op("dve", lambda: nc.vector.scalar_tensor_tensor(out=gtt[:, ck, 0:n], in0=ont[:, ck, 0:n], scalar=hg[:, ck:ck + 1], in1=sg[:, 0:n],
                                                                       op0=ALU.mult, op1=ALU.mult), reads=[onb, sgb, b_par, gtb], writes=[gtk[ck]])
                for c0 in range(0, KC, 2):
                    self.IL.run([lambda c=c0: og_body(c), lambda c=c0 + 1: og_body(c)])
                mxt, mxb = mx_r.next()
                mxk = [Buf(f"mx{i}") for i in range(KC)]
                S.op("dve", lambda: nc.vector.memset(mxt[:, 0, 0:1], 0.0), writes=[mxb] + mxk)

                def m_body(m):
                    yrp, yrb = self.ps()
                    for kc in range(KC):
                        S.op("pe", lambda kc=kc: nc.tensor.matmul(yrp[:, 0:n], lhsT=wHG[:, kc, m * 128:(m + 1) * 128], rhs=gtt[:, kc, 0:n],
                                                                  start=(kc == 0), stop=(kc == KC - 1)),
                             reads=gtk + [gtb, wHGb[(m * 128) // pc]], writes=[yrb], signal=(kc == KC - 1))
                    gap, gab = self.ps()
                    for kc in range(KC):
                        S.op("pe", lambda kc=kc: nc.tensor.matmul(gap[:, 0:n], lhsT=wGA[:, kc, m * 128:(m + 1) * 128], rhs=self.hT[:, kc, t0:t0 + n],
                                                                  start=(kc == 0), stop=(kc == KC - 1)),
                             reads=hbufs + [wGAb[(m * 128) // pc]], writes=[gab], signal=(kc == KC - 1))
                    ta, tab = tr.next()
                    S.op("act", lambda: nc.scalar.activation(out=ta[:, 0:n], in_=gap[:, 0:n], func=AF.Tanh, scale=0.5), reads=[gab], writes=[tab])
                    S.op("dve", lambda: nc.vector.scalar_tensor_tensor(out=ta[:, 0:n], in0=ta[:, 0:n], scalar=1.0, in1=yrp[:, 0:n], op0=ALU.add, op1=ALU.mult),
                         reads=[yrb], writes=[tab])
                    ycp, ycpb = self.ps()
                    for kc in range(KC):
                        S.op("pe", lambda kc=kc: nc.tensor.matmul(ycp[:, 0:n], lhsT=wCO[:, kc, m * 128:(m + 1) * 128], rhs=yct[:, kc, 0:n],
                                                                  start=(kc == 0), stop=(kc == KC - 1)),
                             reads=[ycb, wCOb[(m * 128) // pc]], writes=[ycpb], signal=(kc == KC - 1))
                    gbp, gbb = self.ps()
                    for kc in range(KC):
                        S.op("pe", lambda kc=kc: nc.tensor.matmul(gbp[:, 0:n], lhsT=wGB[:, kc, m * 128:(m + 1) * 128], rhs=self.hT[:, kc, t0:t0 + n],
                                                                  start=(kc == 0), stop=(kc == KC - 1)),
                             reads=hbufs + [wGBb[(m * 128) // pc]], writes=[gbb], signal=(kc == KC - 1))
                    tb_, tbb = tr.next()
                    S.op("act", lambda: nc.scalar.activation(out=tb_[:, 0:n], in_=gbp[:, 0:n], func=AF.Tanh, scale=0.5), reads=[gbb], writes=[tbb])
                    S.op("dve", lambda: nc.vector.scalar_tensor_tensor(out=tb_[:, 0:n], in0=tb_[:, 0:n], scalar=1.0, in1=ycp[:, 0:n], op0=ALU.add, op1=ALU.mult),
                         reads=[ycpb], writes=[tbb])
                    S.op("dve", lambda: nc.vector.tensor_tensor(out=mxt[:, m, 0:n], in0=ta[:, 0:n], in1=tb_[:, 0:n], op=ALU.add), reads=[tab, tbb, mxb], writes=[mxk[m]])
                for m0 in range(0, KC, 2):
                    self.IL.run([lambda m=m0: m_body(m), lambda m=m0 + 1: m_body(m)])
                def ln1_body(tt0, tn, ti, hot, hob):
                    o0 = tt0 - t0
                    rt, rb = hot, hob
                    for hh in range(2):
                        pp, ppb = self.ps()
                        for kc in range(KC):
                            S.op("pe", lambda kc=kc: nc.tensor.matmul(pp[0:tn, :], lhsT=mxt[:, kc, o0:o0 + tn], rhs=wO[:, kc, hh * 512:(hh + 1) * 512],
                                                                      start=(kc == 0), stop=(kc == KC - 1)),
                                 reads=mxk + [mxb, wOb[(hh * 512) // pc]], writes=[ppb], signal=(kc == KC - 1))
                        S.op("dve", lambda hh=hh, pp=pp: nc.vector.scalar_tensor_tensor(out=rt[0:tn, hh * 512:(hh + 1) * 512], in0=pp[0:tn, :], scalar=0.5,
                                                                                       in1=hot[0:tn, hh * 512:(hh + 1) * 512], op0=ALU.mult, op1=ALU.add),
                             reads=[ppb, hob], writes=[rb])
                    hnt, hnb = hot, hob
                    self.layer_norm_tile(sc[ti % 2], rt[0:tn, :], rb, tn, g_t, b_t, gb, hnt[0:tn, :], hnb)
                    if ti == 0:
                        S.op("dve", lambda: nc.vector.tensor_scalar(out=hnt[0:tn, :], in0=hnt[0:tn, :], scalar1=self.flag[0:tn, 0:1],
                                                                    scalar2=None, op0=ALU.mult), reads=[bc], writes=[hnb])
                    aht, ahb = ah_r.next()
                    S.op("act", lambda: nc.scalar.mul(out=aht[0:tn, :], in_=hnt[0:tn, :], mul=ALPHA), reads=[hnb], writes=[ahb])
                    S.dma("pool", self.h_scr[tt0:tt0 + tn, :], aht[0:tn, :], reads=[ahb], writes=[self.b_hs[ti]])
                    hbt, hbb = hbr.next()
                    self.to_hT(hnt[0:tn, :], hnb, tn, ti, hbt[0:tn, :], hbb)
                    if moe:
                        h32, h32b = h32T_r.next()
                        for half in range(2):
                            tp, tpb = self.ps()
                            for q in range(4):
                                kc = half * 4 + q
                                S.op("pe", lambda kc=kc, q=q: nc.tensor.transpose(tp[:, q * 128:q * 128 + tn], hnt[0:tn, kc * 128:(kc + 1) * 128], self.identf[0:tn, 0:tn]),
                                     reads=[hnb, bc], writes=[tpb], signal=(q == 3))
                            S.op("act", lambda half=half, tp=tp: nc.scalar.copy(out=h32[:, half * 4:half * 4 + 4, 0:tn],
                                                                               in_=tp[:, :].rearrange("p (k t) -> p k t", t=128)[:, :, 0:tn]),
                                 reads=[tpb], writes=[h32b])
                        lp, lpb = self.ps()
                        for kc in range(KC):
                            S.op("pe", lambda kc=kc: nc.tensor.matmul(lp[0:tn, 0:N_EXP], lhsT=h32[:, kc, 0:tn], rhs=wr[:, kc, :], start=(kc == 0), stop=(kc == KC - 1)),
                                 reads=[h32b, b_par], writes=[lpb], signal=(kc == KC - 1))
                        S.op("dve", lambda: nc.vector.tensor_copy(out=lg_all[0:tn, ti, :], in_=lp[0:tn, 0:N_EXP]), reads=[lpb], writes=[b_lg])
                self.IL.run([lambda a=a: ln1_body(a[0][0], a[0][1], a[1], a[2][0], a[2][1]) for a in zip(tiles, tis, hold)])
            if moe:
                NT = len(TILES)
                mx8 = self.sb(es, "mx8", [128, NT, 8], F32)
                msk = self.sb(es, "msk", [128, NT, 8], F32)
                den = self.sb(es, "den", [128, NT, 1], F32)
                b_g = Buf("gtmp")
                for ti in range(NT):
                    S.op("dve", lambda ti=ti: nc.vector.max(out=mx8[:, ti, :], in_=lg_all[:, ti, :]), reads=[b_lg], writes=[b_g], signal=(ti == NT - 1))
                S.op("dve", lambda: nc.vector.tensor_tensor(out=msk[:], in0=lg_all[:], in1=mx8[:, :, 1:2].to_broadcast([128, NT, 8]), op=ALU.is_ge),
                     reads=[b_lg, b_g], writes=[b_g])
                S.op("dve", lambda: nc.vector.tensor_tensor(out=lg_all[:], in0=lg_all[:], in1=mx8[:, :, 0:1].to_broadcast([128, NT, 8]), op=ALU.subtract),
                     reads=[b_g], writes=[b_lg])
                S.op("act", lambda: nc.scalar.activation(out=lg_all[:], in_=lg_all[:], func=AF.Exp), reads=[b_lg], writes=[b_lg])
                S.op("dve", lambda: nc.vector.tensor_tensor(out=lg_all[:], in0=lg_all[:], in1=msk[:], op=ALU.mult), reads=[b_g], writes=[b_lg])
                S.op("dve", lambda: nc.vector.tensor_reduce(out=den[:].rearrange("p n o -> p (n o)"), in_=lg_all[:], axis=AX.X, op=ALU.add), reads=[b_lg], writes=[b_g])
                S.op("dve", lambda: nc.vector.reciprocal(out=den[:], in_=den[:]), reads=[b_g], writes=[b_g])
                S.op("dve", lambda: nc.vector.tensor_tensor(out=self.gates[:], in0=lg_all[:], in1=den[:, :, 0:1].to_broadcast([128, NT, 8]), op=ALU.mult),
                     reads=[b_lg, b_g], writes=[self.b_gates])
            S.barrier()

    def phase_ffn(self, l, experts, dff, moe):
        nc, S, I = self.nc, self.S, self.I
        self.psr = self.psr6
        BL = blocks(512)
        bc = self.b_const
        G = 512
        groups = [(g0, min(G, dff - g0)) for g0 in range(0, dff, G)]
        last_layer = (l == DEPTH - 1)
        with ExitStack() as es:
            acc = self.sb(es, "acc", [128, len(TILES), D], F32)
            b_acc = [Buf(f"acc{i}") for i in range(len(TILES))]
            for ti, (t0, n) in enumerate(TILES):
                S.dma("sp", acc[0:n, ti, :], self.h_scr[t0:t0 + n, :], reads=[self.b_hs[ti]], writes=[b_acc[ti]])
            g_t = self.sb(es, "ln2_g", [128, D], F32)
            b_t = self.sb(es, "ln2_b", [128, D], F32)
            gb = Buf("ln2_gb")
            S.dma("sp", g_t[:], I["ln2_g"][l:l + 1, :].to_broadcast([128, D]), writes=[gb])
            S.dma("sp", b_t[:], I["ln2_b"][l:l + 1, :].to_broadcast([128, D]), writes=[gb])
            NWB = 2
            wg_r = self.ring(es, "wg", [128, KC, G], BF16, NWB)
            wu_r = self.ring(es, "wu", [128, KC, G], BF16, NWB)
            wd_r = self.ring(es, "wd", [128, G // 128, D], BF16, NWB)
            aT_r = self.ring(es, "aT", [128, G // 128, 512], BF16, 2)
            tr = self.ring(es, "ff_t", [128, 512], F32, 4)
            work = [(e, g) for e in range(len(experts)) for g in groups]
            loaded = {}

            def load(wi):
                e, (g0, gn) = work[wi]
                wgs, wus, wds = experts[e]
                wgt, wgb = wg_r.next()
                wut, wub = wu_r.next()
                wdt, wdb = wd_r.next()
                S.dma("pool", wgt[:, :, 0:gn], wgs[:, g0:g0 + gn].rearrange("(kc p) n -> p kc n", p=128), writes=[wgb])
                S.dma("pool", wut[:, :, 0:gn], wus[:, g0:g0 + gn].rearrange("(kc p) n -> p kc n", p=128), writes=[wub])
                for hh in range(2):
                    S.dma("pool", wdt[:, 0:gn // 128, hh * 512:(hh + 1) * 512],
                          wds[g0:g0 + gn, hh * 512:(hh + 1) * 512].rearrange("(j p) n -> p j n", p=128), writes=[wdb])
                loaded[wi] = (wgt, wgb, wut, wub, wdt, wdb)

            hn_r = self.ring(es, "h2_new", [128, D], F32, 2)
            ah_r = self.ring(es, "h2_ah", [128, D], F32, 2)
            hbr = self.ring(es, "h2_hb", [128, D], BF16, 2)
            sc = [self.ln_scratch(es, f"ln2s{i}") for i in range(2)]

            def ln2_body(ti):
                t0, n = TILES[ti]
                hnt, hnb = hn_r.next()
                self.layer_norm_tile(sc[ti % 2], acc[0:n, ti, :], b_acc[ti], n, g_t, b_t, gb, hnt[0:n, :], hnb)
                if last_layer:
                    S.dma("pool", self.y[t0:t0 + n, :], hnt[0:n, :], reads=[hnb], writes=[self.b_y])
                else:
                    if ti == 0:
                        S.op("dve", lambda: nc.vector.tensor_scalar(out=hnt[0:n, :], in0=hnt[0:n, :], scalar1=self.flag[0:n, 0:1],
                                                                    scalar2=None, op0=ALU.mult), reads=[bc], writes=[hnb])
                    aht, ahb = ah_r.next()
                    S.op("act", lambda: nc.scalar.mul(out=aht[0:n, :], in_=hnt[0:n, :], mul=ALPHA), reads=[hnb], writes=[ahb])
                    S.dma("pool", self.h_scr[t0:t0 + n, :], aht[0:n, :], reads=[ahb], writes=[self.b_hs[ti]])
                    hbt, hbb = hbr.next()
                    self.to_hT(hnt[0:n, :], hnb, n, ti, hbt[0:n, :], hbb)

            load(0)
            for wi, (e, (g0, gn)) in enumerate(work):
                if wi + 1 < len(work):
                    load(wi + 1)
                wgt, wgb, wut, wub, wdt, wdb = loaded.pop(wi)
                last_item = (wi == len(work) - 1)
                nj = gn // 128
                for bi, (t0, n) in enumerate(BL):
                    tiles = [(t0 + o, min(128, n - o)) for o in range(0, n, 128)]
                    tis = [TILES.index(tt) for tt in tiles]
                    hbufs = [self.b_hT[ti] for ti in tis]
                    aT, aTb = aT_r.next()
                    for j in range(nj):
                        gp, gpb = self.ps()
                        for kc in range(KC):
                            S.op("pe", lambda kc=kc: nc.tensor.matmul(gp[:, 0:n], lhsT=wgt[:, kc, j * 128:(j + 1) * 128], rhs=self.hT[:, kc, t0:t0 + n],
                                                                      start=(kc == 0), stop=(kc == KC - 1)),
                                 reads=hbufs + [wgb], writes=[gpb], signal=(kc == KC - 1))
                        up, upb = self.ps()
                        for kc in range(KC):
                            S.op("pe", lambda kc=kc: nc.tensor.matmul(up[:, 0:n], lhsT=wut[:, kc, j * 128:(j + 1) * 128], rhs=self.hT[:, kc, t0:t0 + n],
                                                                      start=(kc == 0), stop=(kc == KC - 1)),
                                 reads=hbufs + [wub], writes=[upb], signal=(kc == KC - 1))
                        sg, sgb = tr.next()
                        S.op("act", lambda: nc.scalar.activation(out=sg[:, 0:n], in_=gp[:, 0:n], func=AF.Silu), reads=[gpb], writes=[sgb])
                        S.op("dve", lambda j=j: nc.vector.tensor_tensor(out=aT[:, j, 0:n], in0=up[:, 0:n], in1=sg[:, 0:n], op=ALU.mult),
                             reads=[upb, sgb], writes=[aTb])
                    for (tt0, tn), ti in zip(tiles, tis):
                        o0 = tt0 - t0
                        for hh in range(2):
                            dp, dpb = self.ps()
                            for j in range(nj):
                                S.op("pe", lambda j=j: nc.tensor.matmul(dp[0:tn, :], lhsT=aT[:, j, o0:o0 + tn], rhs=wdt[:, j, hh * 512:(hh + 1) * 512],
                                                                        start=(j == 0), stop=(j == nj - 1)),
                                     reads=[aTb, wdb], writes=[dpb], signal=(j == nj - 1))
                            if moe:
                                S.op("dve", lambda hh=hh, dp=dp: nc.vector.scalar_tensor_tensor(
                                    out=acc[0:tn, ti, hh * 512:(hh + 1) * 512], in0=dp[0:tn, :], scalar=self.gates[0:tn, ti, e:e + 1],
                                    in1=acc[0:tn, ti, hh * 512:(hh + 1) * 512], op0=ALU.mult, op1=ALU.add),
                                    reads=[dpb, self.b_gates], writes=[b_acc[ti]])
                            else:
                                S.op("dve", lambda hh=hh, dp=dp: nc.vector.tensor_tensor(
                                    out=acc[0:tn, ti, hh * 512:(hh + 1) * 512], in0=dp[0:tn, :], in1=acc[0:tn, ti, hh * 512:(hh + 1) * 512], op=ALU.add),
                                    reads=[dpb], writes=[b_acc[ti]])
                        if last_item:
                            ln2_body(ti)
            S.barrier()


def pm(a):
    a = np.asarray(a, dtype=np.float32)
    return np.ascontiguousarray(a.reshape(a.shape[:-1] + (KC, 128)).swapaxes(-1, -2))


def make_in_maps(inputs, stop_after=None):
    x = np.asarray(inputs["x"], dtype=np.float32)
    meta = np.asarray(inputs["meta_tokens"], dtype=np.float32)
    B = x.shape[0]
    shared = {
        "ln_in_g": np.asarray(inputs["ln_in_g"], np.float32).reshape(1, D),
        "ln_in_b": np.asarray(inputs["ln_in_b"], np.float32).reshape(1, D),
        "w_in": np.ascontiguousarray(inputs["w_in"], dtype=np.float32),
        "lower_bounds": pm(inputs["lower_bounds"]),
        "hg_norm_g": pm(inputs["hg_norm_g"]),
        "w_hg_out": np.ascontiguousarray(inputs["w_hg_out"], dtype=np.float32),
        "conv_w": np.ascontiguousarray(np.asarray(inputs["conv_w"], np.float32).reshape(DEPTH, CW, KC, 128).transpose(0, 3, 2, 1)),
        "conv_b": pm(inputs["conv_b"]),
        "conv_ln_g": pm(inputs["conv_ln_g"]),
        "conv_ln_b": pm(inputs["conv_ln_b"]),
        "w_conv_out": np.ascontiguousarray(inputs["w_conv_out"], dtype=np.float32),
        "w_out": np.ascontiguousarray(inputs["w_out"], dtype=np.float32),
        "ln1_g": np.ascontiguousarray(inputs["ln1_g"], dtype=np.float32),
        "ln1_b": np.ascontiguousarray(inputs["ln1_b"], dtype=np.float32),
        "ln2_g": np.ascontiguousarray(inputs["ln2_g"], dtype=np.float32),
        "ln2_b": np.ascontiguousarray(inputs["ln2_b"], dtype=np.float32),
    }
    full = stop_after is None
    if full or stop_after >= 5:
        shared["ffn_w_gate"] = np.ascontiguousarray(inputs["ffn_w_gate"], dtype=np.float32)
        shared["ffn_w_up"] = np.ascontiguousarray(inputs["ffn_w_up"], dtype=np.float32)
        shared["ffn_w_down"] = np.ascontiguousarray(inputs["ffn_w_down"], dtype=np.float32)
    if full or stop_after >= 10:
        shared["moe_router"] = np.ascontiguousarray(
            np.asarray(inputs["moe_router"], np.float32).reshape(1, KC, 128, N_EXP).transpose(0, 2, 1, 3))
        shared["moe_w_gate"] = np.ascontiguousarray(inputs["moe_w_gate"], dtype=np.float32)
        shared["moe_w_up"] = np.ascontiguousarray(inputs["moe_w_up"], dtype=np.float32)
        shared["moe_w_down"] = np.ascontiguousarray(inputs["moe_w_down"], dtype=np.float32)
    maps = []
    NA = T - 32
    for c in range(8):
        b, half = c // 2, c % 2
        if half == 0:
            xin = np.concatenate([np.zeros((16, D), np.float32), meta, x[b, :NA]], axis=0)
        else:
            xin = x[b, NA:]
        m = dict(shared)
        m["xin"] = np.ascontiguousarray(xin)
        m["flag"] = np.full((128, 1), float(half), np.float32)
        maps.append(m)
    return maps


_PROG_CACHE = {}


def kernel(**inputs):
    x = np.asarray(inputs["x"])
    B, SEQ, _ = x.shape
    if None not in _PROG_CACHE:
        _PROG_CACHE[None] = Prog()
    prog = _PROG_CACHE[None]
    maps = make_in_maps(inputs)
    res = run_bass_kernel_spmd(prog.nc, maps, core_ids=list(range(8)))
    NA = T - 32
    out = np.empty((B, SEQ, D), np.float32)
    for c in range(8):
        b, half = c // 2, c % 2
        y = res.results[c]["y"]
        if half == 0:
            out[b, :NA] = y[32:]
        else:
            out[b, NA:] = y
    return out
```

```python
import numpy as np
import threading
from contextlib import ExitStack
import concourse.bass as bass
import concourse.mybir as mybir
from concourse.bass_utils import run_bass_kernel_spmd

F32 = mybir.dt.float32
BF16 = mybir.dt.bfloat16
AF = mybir.ActivationFunctionType
ALU = mybir.AluOpType
AX = mybir.AxisListType

D = 1024
KC = 8
T = 2064
PRE = 16
NH = 8
DEPTH = 2
D_FF = 2816
N_EXP = 8
D_FFE = 3584
CW = 31
HALO = CW - 1
OFF_Q, OFF_F, OFF_I, OFF_OG, OFF_CV, OFF_CG, OFF_GA, OFF_GB = [i * 1024 for i in range(8)]
ALPHA = float((2 * DEPTH) ** 0.25)
LN_EPS = 1e-5
RMS_EPS = 1e-6
XW = 1024 + KC * HALO
CH = 128


def blocks(n):
    out = [(0, PRE)]
    t = PRE
    while t < T:
        out.append((t, n))
        t += n
    return out


TILES = blocks(128)


class Buf:
    __slots__ = ("name", "w", "r")

    def __init__(self, name="b"):
        self.name = name
        self.w = None
        self.r = {}


class Sched:
    def __init__(self, nc, n_dma_sems=40):
        self.nc = nc
        self.eng = {"pe": nc.tensor, "act": nc.scalar, "dve": nc.vector, "pool": nc.gpsimd, "sp": nc.sync}
        self.sem = {k: nc.semaphore("s_" + k).__enter__() for k in self.eng}
        self.cnt = {k: 0 for k in self.eng}
        self.waited = {k: {} for k in self.eng}
        self.dma_sems = [nc.semaphore(f"s_dma{i}").__enter__() for i in range(n_dma_sems)]
        self.dma_cnt = [0] * n_dma_sems
        self.dma_rr = 0
        self.nwaits = 0
        self.ninstr = 0
        self.switch = None

    def _wait(self, e, dep, war=False):
        kind, key, val = dep
        if kind == "eng":
            if key == e and (war or e in ("pe", "sp")):
                return
            assert val <= self.cnt[key], (e, dep, self.cnt[key])
            sem = self.sem[key]
        else:
            sem = self.dma_sems[key]
        wk = (kind, key)
        if self.waited[e].get(wk, 0) >= val:
            return
        self.eng[e].wait_ge(sem, val)
        self.waited[e][wk] = val
        self.nwaits += 1

    def _deps(self, e, reads, writes):
        for b in reads:
            if b.w is not None:
                self._wait(e, b.w)
        for b in writes:
            if b.w is not None:
                self._wait(e, b.w)
            for d in b.r.values():
                self._wait(e, d, war=True)

    def _mark(self, tok, reads, writes):
        for b in reads:
            b.r[(tok[0], tok[1])] = tok
        for b in writes:
            b.w = tok
            b.r = {}

    def op(self, e, fn, reads=(), writes=(), signal=True):
        self._deps(e, reads, writes)
        ins = fn()
        self.ninstr += 1
        if signal or e != "pe":
            self.cnt[e] += 1
            ins.then_inc(self.sem[e], 1)
            tok = ("eng", e, self.cnt[e])
        else:
            tok = ("eng", e, self.cnt[e] + 1)
        self._mark(tok, reads, writes)
        if self.switch is not None:
            self.switch()
        return ins

    def dma(self, q, out, in_, reads=(), writes=(), **kw):
        self._deps(q, reads, writes)
        si = self.dma_rr
        self.dma_rr = (self.dma_rr + 1) % len(self.dma_sems)
        ins = self.eng[q].dma_start(out=out, in_=in_, **kw)
        self.dma_cnt[si] += 16
        ins.then_inc(self.dma_sems[si], 16)
        tok = ("dma", si, self.dma_cnt[si])
        self._mark(tok, reads, writes)
        self.ninstr += 1
        if self.switch is not None:
            self.switch()
        return tok

    def collective(self, fn, reads=(), writes=()):
        self._deps("pool", reads, writes)
        si = self.dma_rr
        self.dma_rr = (self.dma_rr + 1) % len(self.dma_sems)
        ins = fn()
        self.dma_cnt[si] += 1
        ins.then_inc(self.dma_sems[si])
        tok = ("dma", si, self.dma_cnt[si])
        self._mark(tok, reads, writes)
        return tok

    def barrier(self):
        if getattr(self, "on_barrier", None):
            self.on_barrier()
        for e in self.eng:
            for e2 in self.eng:
                if e2 != e and self.cnt[e2] > 0:
                    self._wait(e, ("eng", e2, self.cnt[e2]))
            for si, c in enumerate(self.dma_cnt):
                if c > 0:
                    self._wait(e, ("dma", si, c))


class Interleaver:
    def __init__(self, sched):
        self.S = sched

    def run(self, fns):
        if len(fns) == 1:
            fns[0]()
            return
        n = len(fns)
        self.ev = [threading.Event() for _ in range(n)]
        self.alive = [True] * n
        self.tid = {}
        errs = []

        def worker(i):
            self.ev[i].wait()
            self.ev[i].clear()
            try:
                fns[i]()
            except BaseException as e:
                errs.append(e)
            self.alive[i] = False
            j = self._next(i)
            if j is not None:
                self.ev[j].set()

        ths = [threading.Thread(target=worker, args=(i,)) for i in range(n)]
        for i, t in enumerate(ths):
            t.start()
            self.tid[t.ident] = i
        self.S.switch = self._switch
        self.ev[0].set()
        for t in ths:
            t.join()
        self.S.switch = None
        if errs:
            raise errs[0]

    def _next(self, i):
        n = len(self.alive)
        for d in range(1, n):
            j = (i + d) % n
            if self.alive[j]:
                return j
        return None

    def _switch(self):
        i = self.tid.get(threading.get_ident())
        if i is None:
            return
        j = self._next(i)
        if j is None:
            return
        self.ev[j].set()
        self.ev[i].wait()
        self.ev[i].clear()


class Ring:
    def __init__(self, tiles):
        self.items = [(t, Buf()) for t in tiles]
        self.i = 0

    def next(self):
        it = self.items[self.i]
        self.i = (self.i + 1) % len(self.items)
        return it


class Prog:
    def __init__(self, stop_after=None, debug=False):
        self.stop_after = stop_after
        self.debug = debug
        nc = bass.Bass("TRN2", target_bir_lowering=False)
        self.nc = nc
        self.S = Sched(nc)
        self.IL = Interleaver(self.S)
        self.wq = []
        self.S.on_barrier = lambda: self.wq.clear()
        self.es = ExitStack()
        self.build()

    def din(self, name, shape, dt=F32):
        return self.nc.dram_tensor(name, list(shape), dt, kind="ExternalInput").ap()

    def dscr(self, name, shape, dt=F32, cc=False):
        if self.debug and not cc:
            return self.nc.dram_tensor(name, list(shape), dt, kind="ExternalOutput").ap()
        return self.nc.dram_tensor(name, list(shape), dt).ap()

    def sb(self, es, name, shape, dt):
        self._uid = getattr(self, "_uid", 0) + 1
        return es.enter_context(self.nc.sbuf_tensor(f"sb{self._uid}_{name}", list(shape), dt))

    def ring(self, es, name, shape, dt, n):
        return Ring([self.sb(es, f"{name}{i}", shape, dt) for i in range(n)])

    def ps(self):
        return self.psr.next()

    def load_w(self, es, name, src, ncols, eng="pool", piece=512):
        t = self.sb(es, name, [128, KC, ncols], BF16)
        bufs = []
        for c0 in range(0, ncols, piece):
            b = Buf(name)
            prev = []
            self.S.dma(eng, t[:, :, c0:c0 + piece],
                       src[:, c0:c0 + piece].rearrange("(kc p) n -> p kc n", p=128), reads=prev, writes=[b])
            self.wq.append(b)
            bufs.append(b)
        return t, bufs, piece

    def build(self):
        nc, S = self.nc, self.S
        es = self.es
        I = {}
        I["xin"] = self.din("xin", [T, D])
        I["flag"] = self.din("flag", [128, 1])
        I["ln_in_g"] = self.din("ln_in_g", [1, D])
        I["ln_in_b"] = self.din("ln_in_b", [1, D])
        I["w_in"] = self.din("w_in", [DEPTH, D, 8 * D])
        I["lower_bounds"] = self.din("lower_bounds", [DEPTH, 128, KC])
        I["hg_norm_g"] = self.din("hg_norm_g", [DEPTH, 128, KC])
        I["w_hg_out"] = self.din("w_hg_out", [DEPTH, D, D])
        I["conv_w"] = self.din("conv_w", [DEPTH, 128, KC, CW])
        I["conv_b"] = self.din("conv_b", [DEPTH, 128, KC])
        I["conv_ln_g"] = self.din("conv_ln_g", [DEPTH, 128, KC])
        I["conv_ln_b"] = self.din("conv_ln_b", [DEPTH, 128, KC])
        I["w_conv_out"] = self.din("w_conv_out", [DEPTH, D, D])
        I["w_out"] = self.din("w_out", [DEPTH, D, D])
        I["ln1_g"] = self.din("ln1_g", [DEPTH, D])
        I["ln1_b"] = self.din("ln1_b", [DEPTH, D])
        I["ln2_g"] = self.din("ln2_g", [DEPTH, D])
        I["ln2_b"] = self.din("ln2_b", [DEPTH, D])
        full = self.stop_after is None
        if full or self.stop_after >= 5:
            I["ffn_w_gate"] = self.din("ffn_w_gate", [1, D, D_FF])
            I["ffn_w_up"] = self.din("ffn_w_up", [1, D, D_FF])
            I["ffn_w_down"] = self.din("ffn_w_down", [1, D_FF, D])
        if full or self.stop_after >= 10:
            I["moe_router"] = self.din("moe_router", [1, 128, KC, N_EXP])
            I["moe_w_gate"] = self.din("moe_w_gate", [1, N_EXP, D, D_FFE])
            I["moe_w_up"] = self.din("moe_w_up", [1, N_EXP, D, D_FFE])
            I["moe_w_down"] = self.din("moe_w_down", [1, N_EXP, D_FFE, D])
        self.I = I
        self.y = nc.dram_tensor("y", [T, D], F32, kind="ExternalOutput").ap()
        self.h_scr = self.dscr("h_scr", [T, D])
        self.on_scr = self.dscr("on_scr", [KC, 128, T], BF16)
        self.yc_scr = self.dscr("yc_scr", [KC, 128, T], BF16)
        self.cc_in = self.dscr("cc_in", [128, XW], cc=True)
        self.cc_out = self.dscr("cc_out", [256, XW], cc=True)
        self.b_hscr = Buf("h_scr")
        self.b_hs = [Buf(f"h_scr{i}") for i in range(len(TILES))]
        self.b_on = Buf("on_scr")
        self.b_yc = Buf("yc_scr")
        self.b_y = Buf("y")

        self.hT = self.sb(es, "hT", [128, KC, T], BF16)
        self.b_hT = [Buf(f"hT{i}") for i in range(len(TILES))]
        self.ident = self.sb(es, "ident", [128, 128], BF16)
        self.identf = self.sb(es, "identf", [128, 128], F32)
        self.onesf = self.sb(es, "onesf", [128, 128], F32)
        self.onesb = self.sb(es, "onesb", [128, 128], BF16)
        self.maskA = self.sb(es, "maskA", [128, 128], F32)
        self.maskA4 = self.sb(es, "maskA4", [128, 4, 128], F32)
        self.neghalf = self.sb(es, "neghalf", [128, 1], F32)
        self.flag = self.sb(es, "flag", [128, 1], F32)
        self.lbt = self.sb(es, "lbt", [128, DEPTH, KC], F32)
        self.lnoml = self.sb(es, "lnoml", [128, DEPTH, KC], F32)
        self.Sst = self.sb(es, "Sst", [128, D], F32)
        self.halo0 = self.sb(es, "halo0", [128, KC, HALO], BF16)
        self.gates = self.sb(es, "gates", [128, len(TILES), N_EXP], F32)
        self.b_const = Buf("const")
        self.b_S = Buf("S")
        self.b_Sh = [Buf(f"S{h}") for h in range(NH)]
        self.b_halo0 = Buf("halo0")
        self.b_gates = Buf("gates")
        self.psr = Ring([es.enter_context(nc.psum_tensor(f"ps{i}", [128, 512], F32)) for i in range(4)])
        self.pshold = Ring([es.enter_context(nc.psum_tensor(f"psh{i}", [128, 512], F32)) for i in range(2)])
        self.pstr = Ring([es.enter_context(nc.psum_tensor(f"pst{i}", [128, 1024], BF16)) for i in range(2)])

        self.psr4 = self.psr
        self.psr6 = Ring([])
        self.psr6.items = self.psr.items + self.pshold.items
        self.psA = Ring([])
        self.psA.items = self.psr.items[0:2]
        self.psB = Ring([])
        self.psB.items = self.psr.items[2:4]
        self.setup_consts()
        self.wes = ExitStack()
        hw = self.load_hgrn_weights(self.wes, 0)
        self.phase_ln_in()
        if self.stop_after == 0:
            return self.finish()
        for l in range(DEPTH):
            base = 5 * l
            if l > 0:
                self.wes = ExitStack()
                hw = self.load_hgrn_weights(self.wes, l)
            self.hw = hw
            self.phase_hgrn(l, state_only=True)
            self.exchange()
            if self.stop_after == base + 1:
                return self.finish()
            self.phase_hgrn(l, state_only=False)
            self.wes.close()
            self.wes = None
            if self.stop_after == base + 2:
                return self.finish()
            self.phase_conv(l)
            if self.stop_after == base + 3:
                return self.finish()
            self.phase_mix(l)
            if self.stop_after == base + 4:
                return self.finish()
            if l % 2 == 0:
                self.phase_ffn(l, [(I["ffn_w_gate"][0], I["ffn_w_up"][0], I["ffn_w_down"][0])], D_FF, moe=False)
            else:
                ex = [(I["moe_w_gate"][0, e], I["moe_w_up"][0, e], I["moe_w_down"][0, e]) for e in range(N_EXP)]
                self.phase_ffn(l, ex, D_FFE, moe=True)
            if self.stop_after == base + 5:
                return self.finish()
        self.finish()

    def finish(self):
        S = self.S
        S.barrier()
        if getattr(self, "wes", None) is not None:
            self.wes.close()
            self.wes = None
        self.es.close()

    def load_hgrn_weights(self, es, l):
        W = self.I["w_in"][l]
        wF = self.load_w(es, "wF", W[:, OFF_F:OFF_F + D], D)
        wI = self.load_w(es, "wI", W[:, OFF_I:OFF_I + D], D)
        wQ = self.load_w(es, "wQ", W[:, OFF_Q:OFF_Q + D], D)
        return dict(F=wF, I=wI, Q=wQ)

    def setup_consts(self):
        nc, S, I = self.nc, self.S, self.I
        bc = self.b_const
        S.op("pool", lambda: nc.gpsimd.memset(self.ident[:], 0.0), writes=[bc])
        S.op("pool", lambda: nc.gpsimd.affine_select(out=self.ident[:], in_=self.ident[:], pattern=[[-1, 128]],
                                                     compare_op=ALU.not_equal, fill=1.0, base=0, channel_multiplier=1),
             reads=[bc], writes=[bc])
        S.op("pool", lambda: nc.gpsimd.memset(self.identf[:], 0.0), writes=[bc])
        S.op("pool", lambda: nc.gpsimd.affine_select(out=self.identf[:], in_=self.identf[:], pattern=[[-1, 128]],
                                                     compare_op=ALU.not_equal, fill=1.0, base=0, channel_multiplier=1),
             reads=[bc], writes=[bc])
        S.op("dve", lambda: nc.vector.memset(self.onesf[:], 1.0), writes=[bc])
        S.op("dve", lambda: nc.vector.memset(self.onesb[:], 1.0), writes=[bc])
        S.op("dve", lambda: nc.vector.memset(self.neghalf[:], -0.5), writes=[bc])
        S.op("pool", lambda: nc.gpsimd.memset(self.maskA[:], 1.0), writes=[bc])
        S.op("pool", lambda: nc.gpsimd.affine_select(out=self.maskA[:], in_=self.maskA[:], pattern=[[1, 128]],
                                                     compare_op=ALU.is_ge, fill=0.0, base=0, channel_multiplier=-1),
             reads=[bc], writes=[bc])
        if CH == 64:
            S.op("pool", lambda: nc.gpsimd.memset(self.maskA[0:64, 64:128], 0.0), writes=[bc])
        for q in range(4):
            S.op("pool", lambda q=q: nc.gpsimd.tensor_copy(out=self.maskA4[:, q, :], in_=self.maskA[:, :]), reads=[bc], writes=[bc])
        S.dma("sp", self.flag[:], I["flag"][:, :], writes=[bc])
        S.op("dve", lambda: nc.vector.memset(self.Sst[:], 0.0), writes=[self.b_S])
        S.op("dve", lambda: nc.vector.memset(self.halo0[:], 0.0), writes=[self.b_halo0])
        S.op("dve", lambda: nc.vector.memset(self.gates[:], 0.0), writes=[self.b_gates])
        with ExitStack() as es:
            a = self.sb(es, "lb_a", [128, DEPTH, KC], F32)
            e = self.sb(es, "lb_e", [128, DEPTH, KC], F32)
            m = self.sb(es, "lb_m", [128, KC], F32)
            s = self.sb(es, "lb_s", [128, KC], F32)
            cum = self.sb(es, "lb_c", [128, KC], F32)
            b = Buf("lbtmp")
            for l in range(DEPTH):
                S.dma("sp", a[:, l, :], I["lower_bounds"][l], writes=[b])
            S.op("dve", lambda: nc.vector.tensor_copy(out=m[:], in_=a[:, 0, :]), reads=[b], writes=[b])
            for l in range(1, DEPTH):
                S.op("dve", lambda l=l: nc.vector.tensor_tensor(out=m[:], in0=m[:], in1=a[:, l, :], op=ALU.max), reads=[b], writes=[b])
            for l in range(DEPTH):
                S.op("dve", lambda l=l: nc.vector.tensor_tensor(out=a[:, l, :], in0=a[:, l, :], in1=m[:], op=ALU.subtract), reads=[b], writes=[b])
            S.op("act", lambda: nc.scalar.activation(out=e[:], in_=a[:], func=AF.Exp), reads=[b], writes=[b])
            S.op("dve", lambda: nc.vector.tensor_copy(out=s[:], in_=e[:, 0, :]), reads=[b], writes=[b])
            for l in range(1, DEPTH):
                S.op("dve", lambda l=l: nc.vector.tensor_tensor(out=s[:], in0=s[:], in1=e[:, l, :], op=ALU.add), reads=[b], writes=[b])
            S.op("dve", lambda: nc.vector.reciprocal(out=s[:], in_=s[:]), reads=[b], writes=[b])
            for l in range(DEPTH):
                S.op("dve", lambda l=l: nc.vector.tensor_tensor(out=e[:, l, :], in0=e[:, l, :], in1=s[:], op=ALU.mult), reads=[b], writes=[b])
            S.op("dve", lambda: nc.vector.memset(cum[:], 0.0), writes=[b])
            for l in range(DEPTH):
                S.op("dve", lambda l=l: nc.vector.tensor_tensor(out=cum[:], in0=cum[:], in1=e[:, l, :], op=ALU.add), reads=[b], writes=[b])
                S.op("dve", lambda l=l: nc.vector.tensor_tensor(out=self.lbt[:, l, :], in0=cum[:], in1=e[:, 0, :], op=ALU.subtract), reads=[b], writes=[bc])
            S.op("act", lambda: nc.scalar.activation(out=self.lnoml[:], in_=self.lbt[:], func=AF.Ln, scale=-1.0, bias=1.0),
                 reads=[bc], writes=[bc])
            S.barrier()

    def ln_tile(self, es_bufs, src, n, g_t, b_t, ti, out_h, out_scale_h=None):
        raise NotImplementedError

    def layer_norm_tile(self, sc, x_ap, xbuf, n, g_t, b_t, gb_buf, out_ap, out_buf, eps=LN_EPS):
        nc, S = self.nc, self.S
        st, mv, rstd, b = sc["stats"], sc["mv"], sc["rstd"], sc["b"]
        for hh in range(2):
            S.op("dve", lambda hh=hh: nc.vector.bn_stats(out=st[0:n, hh, :], in_=x_ap[:, hh * 512:(hh + 1) * 512]),
                 reads=[xbuf], writes=[b])
        S.op("dve", lambda: nc.vector.bn_aggr(out=mv[0:n, :], in_=st[0:n, :, :]), reads=[b], writes=[b])
        S.op("dve", lambda: nc.vector.tensor_scalar(out=rstd[0:n, :], in0=mv[0:n, 1:2], scalar1=eps, scalar2=None, op0=ALU.add),
             reads=[b], writes=[b])
        S.op("pool", lambda: nc.gpsimd.tensor_tensor(out=rstd[0:n, :], in0=rstd[0:n, :], in1=self.neghalf[0:n, 0:1], op=ALU.pow),
             reads=[b, self.b_const], writes=[b])
        S.op("dve", lambda: nc.vector.tensor_scalar(out=out_ap, in0=x_ap, scalar1=mv[0:n, 0:1], scalar2=rstd[0:n, 0:1],
                                                    op0=ALU.subtract, op1=ALU.mult),
             reads=[xbuf, b], writes=[out_buf])
        S.op("dve", lambda: nc.vector.tensor_tensor(out=out_ap, in0=out_ap, in1=g_t[0:n, :], op=ALU.mult),
             reads=[gb_buf], writes=[out_buf])
        S.op("dve", lambda: nc.vector.tensor_tensor(out=out_ap, in0=out_ap, in1=b_t[0:n, :], op=ALU.add),
             reads=[gb_buf], writes=[out_buf])

    def to_hT(self, h_ap, hbuf, n, ti, hb_ap, hb_buf):
        nc, S = self.nc, self.S
        t0, _ = TILES[ti]
        S.op("act", lambda: nc.scalar.copy(out=hb_ap, in_=h_ap), reads=[hbuf], writes=[hb_buf])
        pt, pb = self.pstr.next()
        for kc in range(KC):
            S.op("pe", lambda kc=kc: nc.tensor.transpose(pt[:, kc * 128:kc * 128 + n], hb_ap[:, kc * 128:(kc + 1) * 128], self.ident[0:n, 0:n]),
                 reads=[hb_buf, self.b_const], writes=[pb], signal=(kc == KC - 1))
        S.op("dve", lambda: nc.vector.tensor_copy(out=self.hT[:, :, t0:t0 + n],
                                                  in_=pt[:].rearrange("p (k t) -> p k t", t=128)[:, :, 0:n]),
             reads=[pb], writes=[self.b_hT[ti]])

    def load_bcast(self, es, name, src_row):
        t = self.sb(es, name, [128, D], F32)
        return t

    def ln_scratch(self, es, tag):
        return {"stats": self.sb(es, tag + "_st", [128, 2, 6], F32), "mv": self.sb(es, tag + "_mv", [128, 2], F32),
                "rstd": self.sb(es, tag + "_rs", [128, 1], F32), "b": Buf(tag)}

    def phase_ln_in(self):
        nc, S, I = self.nc, self.S, self.I
        with ExitStack() as es:
            g_t = self.sb(es, "lni_g", [128, D], F32)
            b_t = self.sb(es, "lni_b", [128, D], F32)
            gb = Buf("lni_gb")
            xr = self.ring(es, "lni_x", [128, D], F32, 4)
            hr = self.ring(es, "lni_h", [128, D], F32, 2)
            ar = self.ring(es, "lni_a", [128, D], F32, 2)
            hbr = self.ring(es, "lni_hb", [128, D], BF16, 2)
            sc = [self.ln_scratch(es, f"lni{i}") for i in range(2)]
            loaded = {}

            def load(ti):
                t0, n = TILES[ti]
                xt, xb = xr.next()
                S.dma("sp", xt[0:n, :], I["xin"][t0:t0 + n, :], writes=[xb])
                loaded[ti] = (xt, xb)

            load(0)
            load(1)
            S.dma("sp", g_t[:], I["ln_in_g"][0:1, :].to_broadcast([128, D]), writes=[gb])
            S.dma("sp", b_t[:], I["ln_in_b"][0:1, :].to_broadcast([128, D]), writes=[gb])

            def tile_body(ti):
                t0, n = TILES[ti]
                xt, xb = loaded.pop(ti)
                ht, hb = hr.next()
                self.layer_norm_tile(sc[ti % 2], xt[0:n, :], xb, n, g_t, b_t, gb, ht[0:n, :], hb)
                if ti == 0:
                    S.op("dve", lambda: nc.vector.tensor_scalar(out=ht[0:n, :], in0=ht[0:n, :], scalar1=self.flag[0:n, 0:1],
                                                                scalar2=None, op0=ALU.mult), reads=[self.b_const], writes=[hb])
                at, ab = ar.next()
                S.op("act", lambda: nc.scalar.mul(out=at[0:n, :], in_=ht[0:n, :], mul=ALPHA), reads=[hb], writes=[ab])
                S.dma("pool", self.h_scr[t0:t0 + n, :], at[0:n, :], reads=[ab], writes=[self.b_hs[ti]])
                hbt, hbb = hbr.next()
                self.to_hT(ht[0:n, :], hb, n, ti, hbt[0:n, :], hbb)

            for p0 in range(0, len(TILES), 2):
                pair = [q for q in (p0, p0 + 1) if q < len(TILES)]
                for q in pair:
                    if q + 2 < len(TILES):
                        load(q + 2)
                self.IL.run([lambda q=q: tile_body(q) for q in pair])
            S.barrier()

    def phase_hgrn(self, l, state_only):
        nc, S, I = self.nc, self.S, self.I
        self.psr = self.psr4
        W = I["w_in"][l]
        BL = blocks(512)
        with ExitStack() as es:
            wF, wFb, pc = self.hw["F"]
            wI, wIb, _ = self.hw["I"]
            wQ, wQb, _ = self.hw["Q"]
            if state_only:
                wCV, wCVb, _ = self.load_w(es, "wCV", W[:, OFF_CV:OFF_CV + D], D)
                wCG, wCGb, _ = self.load_w(es, "wCG", W[:, OFF_CG:OFF_CG + D], D)
            tr = self.ring(es, "hg_t", [128, 512], F32, 14)
            self.mask_sc = self.sb(es, "mask_sc", [128, 512], F32)
            S.op("dve", lambda: nc.vector.memset(self.mask_sc[:], 1.0), writes=[self.b_const])
            S.op("dve", lambda: nc.vector.memset(self.mask_sc[:].rearrange("p (c t) -> p c t", t=CH)[:, :, 0:1], 0.0),
                 writes=[self.b_const])
            ktT = self.sb(es, "ktT", [128, NH, 512], BF16)
            b_kt = [Buf(f"kt{h}") for h in range(NH)]
            if not state_only:
                qtT = self.sb(es, "qtT", [128, NH, 512], BF16)
                b_qt = [Buf(f"qt{h}") for h in range(NH)]
                on_sb = self.ring(es, "on_sb", [128, NH, 128], BF16, 2)
                att_r = self.ring(es, "att", [128, NH, 128], BF16, 2)
                cl_r = self.ring(es, "attcl", [128, 4, 128], F32, 2)
                sq_r = self.ring(es, "sq", [128, NH, 128], BF16, 2)
                rs_r = self.ring(es, "rs", [128, NH, 128], F32, 2)
                St_r = self.ring(es, "St", [128, D], BF16, 3)
            dec = self.sb(es, "dec", [128, NH, 8], F32)
            edl = self.sb(es, "edl", [128, NH, 8], F32)
            ebr = self.sb(es, "ebr", [128, NH, 8], F32)
            b_sc = [Buf(f"chunk_scalars{h}") for h in range(NH)]
            v_r = self.ring(es, "v_sb", [128, D], BF16, 3)
            ktm_r = self.ring(es, "ktm", [128, D], BF16, 3)
            lb = self.lbt[:, l, :]
            lno = self.lnoml[:, l, :]
            bc = self.b_const

            for bi, (t0, n) in enumerate(BL):
                C = 16 if n == 16 else CH
                nch = n // C
                mid, last = C // 2 - 1, C - 1
                tiles = [(t0 + o, min(128, n - o)) for o in range(0, n, 128)]
                tis = [TILES.index(tt) for tt in tiles]
                hbufs = [self.b_hT[ti] for ti in tis]
                def head_body(h):
                    zp, zb = self.ps()
                    for kc in range(KC):
                        S.op("pe", lambda kc=kc: nc.tensor.matmul(zp[:, 0:n], lhsT=wF[:, kc, h * 128:(h + 1) * 128], rhs=self.hT[:, kc, t0:t0 + n],
                                                                  start=(kc == 0), stop=(kc == KC - 1)),
                             reads=hbufs + [wFb[(h * 128) // pc]], writes=[zb], signal=(kc == KC - 1))
                    E, Eb = tr.next()
                    L1, L1b = tr.next()
                    L2, L2b = tr.next()
                    lsn, lsnb = tr.next()
                    S.op("act", lambda: nc.scalar.activation(out=E[:, 0:n], in_=zp[:, 0:n], func=AF.Exp, scale=-1.0), reads=[zb], writes=[Eb])
                    S.op("act", lambda: nc.scalar.activation(out=L1[:, 0:n], in_=E[:, 0:n], func=AF.Ln, bias=1.0), reads=[Eb], writes=[L1b])
                    S.op("act", lambda: nc.scalar.activation(out=L2[:, 0:n], in_=E[:, 0:n], func=AF.Ln, scale=lb[:, h:h + 1], bias=1.0),
                         reads=[Eb, bc], writes=[L2b])
                    S.op("dve", lambda: nc.vector.scalar_tensor_tensor(out=lsn[:, 0:n], in0=zp[:, 0:n], scalar=-1.0, in1=L1[:, 0:n],
                                                                       op0=ALU.mult, op1=ALU.subtract), reads=[zb, L1b], writes=[lsnb])
                    S.op("dve", lambda: nc.vector.tensor_tensor(out=L2[:, 0:n], in0=L2[:, 0:n], in1=L1[:, 0:n], op=ALU.subtract),
                         reads=[L1b], writes=[L2b])
                    bt, bb = tr.next()
                    S.op("dve", lambda: nc.vector.tensor_tensor_scan(out=bt[:, 0:n], data0=self.mask_sc[:, 0:n], data1=L2[:, 0:n], initial=0.0,
                                                                     op0=ALU.mult, op1=ALU.add), reads=[L2b, bc], writes=[bb])
                    b3 = bt[:, 0:n].rearrange("p (c t) -> p c t", t=C)
                    df, dfb = tr.next()
                    S.op("dve", lambda: nc.vector.tensor_tensor(out=df[:, 0:n].rearrange("p (c t) -> p c t", t=C), in0=b3,
                                                                in1=b3[:, :, mid:mid + 1].to_broadcast([128, nch, C]), op=ALU.subtract),
                         reads=[bb], writes=[dfb])
                    d3 = df[:, 0:n].rearrange("p (c t) -> p c t", t=C)
                    S.op("act", lambda: nc.scalar.activation(out=dec[:, h, 0:nch], in_=b3[:, :, last], func=AF.Exp), reads=[bb], writes=[b_sc[h]])
                    S.op("act", lambda: nc.scalar.activation(out=ebr[:, h, 0:nch], in_=b3[:, :, mid], func=AF.Exp), reads=[bb], writes=[b_sc[h]])
                    S.op("act", lambda: nc.scalar.activation(out=edl[:, h, 0:nch], in_=d3[:, :, last], func=AF.Exp), reads=[dfb], writes=[b_sc[h]])
                    S.op("dve", lambda: nc.vector.tensor_tensor(out=lsn[:, 0:n], in0=lsn[:, 0:n], in1=df[:, 0:n], op=ALU.subtract),
                         reads=[dfb], writes=[lsnb])
                    S.op("act", lambda: nc.scalar.activation(out=ktT[:, h, 0:n], in_=lsn[:, 0:n], func=AF.Exp, bias=lno[:, h:h + 1]),
                         reads=[lsnb, bc], writes=[b_kt[h]])
                    if not state_only:
                        qp, qb = self.ps()
                        for kc in range(KC):
                            S.op("pe", lambda kc=kc: nc.tensor.matmul(qp[:, 0:n], lhsT=wQ[:, kc, h * 128:(h + 1) * 128], rhs=self.hT[:, kc, t0:t0 + n],
                                                                      start=(kc == 0), stop=(kc == KC - 1)),
                                 reads=hbufs + [wQb[(h * 128) // pc]], writes=[qb], signal=(kc == KC - 1))
                        Eq, Eqb = tr.next()
                        S.op("act", lambda: nc.scalar.activation(out=Eq[:, 0:n], in_=qp[:, 0:n], func=AF.Exp, scale=-1.0), reads=[qb], writes=[Eqb])
                        S.op("act", lambda: nc.scalar.activation(out=Eq[:, 0:n], in_=Eq[:, 0:n], func=AF.Ln, bias=1.0), reads=[Eqb], writes=[Eqb])
                        S.op("dve", lambda: nc.vector.tensor_tensor(out=Eq[:, 0:n], in0=df[:, 0:n], in1=Eq[:, 0:n], op=ALU.subtract),
                             reads=[dfb], writes=[Eqb])
                        S.op("act", lambda: nc.scalar.activation(out=Eq[:, 0:n], in_=Eq[:, 0:n], func=AF.Exp), reads=[Eqb], writes=[Eqb])
                        S.op("dve", lambda: nc.vector.tensor_tensor(out=qtT[:, h, 0:n], in0=qp[:, 0:n], in1=Eq[:, 0:n], op=ALU.mult),
                             reads=[qb, Eqb], writes=[b_qt[h]])
                for h0 in range(0, NH, 2):
                    self.IL.run([lambda h=h0: head_body(h), lambda h=h0 + 1: head_body(h)])
                psA, psB = self.psA, self.psB

                def stage1(tt0, tn, ti):
                    o0 = tt0 - t0
                    r = {}
                    vt, vb = v_r.next()
                    for hh in range(2):
                        vp, vpb = psA.next()
                        for kc in range(KC):
                            S.op("pe", lambda kc=kc: nc.tensor.matmul(vp[0:tn, :], lhsT=self.hT[:, kc, tt0:tt0 + tn], rhs=wI[:, kc, hh * 512:(hh + 1) * 512],
                                                                      start=(kc == 0), stop=(kc == KC - 1)),
                                 reads=[self.b_hT[ti], wIb[(hh * 512) // pc]], writes=[vpb], signal=(kc == KC - 1))
                        S.op("act", lambda: nc.scalar.copy(out=vt[0:tn, hh * 512:(hh + 1) * 512], in_=vp[0:tn, :]), reads=[vpb], writes=[vb])
                    pt, ptb = self.pstr.next()
                    for h in range(NH):
                        S.op("pe", lambda h=h: nc.tensor.transpose(pt[0:tn, h * 128:(h + 1) * 128], ktT[:, h, o0:o0 + tn], self.ident[:, :]),
                             reads=[b_kt[h], bc], writes=[ptb], signal=(h == NH - 1))
                    kt, kb = ktm_r.next()
                    S.op("dve", lambda: nc.vector.tensor_copy(out=kt[0:tn, :], in_=pt[0:tn, :]), reads=[ptb], writes=[kb])
                    r.update(vt=vt, vb=vb, kt=kt, kb=kb)
                    if not state_only:
                        at, atb = att_r.next()
                        for half in range(2):
                            ap_, apb = psA.next()
                            for hq in range(4):
                                h = half * 4 + hq
                                S.op("pe", lambda h=h, hq=hq: nc.tensor.matmul(ap_[0:tn, hq * 128:hq * 128 + tn], lhsT=ktT[:, h, o0:o0 + tn],
                                                                               rhs=qtT[:, h, o0:o0 + tn], start=True, stop=True),
                                     reads=[b_kt[h], b_qt[h]], writes=[apb], signal=(hq == 3))
                            cl, clb = cl_r.next()
                            S.op("dve", lambda half=half: nc.vector.tensor_scalar(
                                out=cl[0:tn, :, 0:tn], in0=ap_[0:tn, :].rearrange("p (h t) -> p h t", t=128)[:, :, 0:tn],
                                scalar1=1e30, scalar2=-1e30, op0=ALU.min, op1=ALU.max), reads=[apb], writes=[clb])
                            S.op("dve", lambda half=half: nc.vector.tensor_tensor(
                                out=at[0:tn, half * 4:half * 4 + 4, 0:tn], in0=cl[0:tn, :, 0:tn],
                                in1=self.maskA4[0:tn, :, 0:tn], op=ALU.mult),
                                reads=[clb, bc], writes=[atb])
                        r.update(at=at, atb=atb)
                    return r

                def stage2(tt0, tn, ti, r):
                    o0 = tt0 - t0
                    vt, vb, kt, kb = r["vt"], r["vb"], r["kt"], r["kb"]
                    gch = o0 // C
                    c0, cn = 0, tn
                    if not state_only:
                        at, atb = r["at"], r["atb"]
                        ont, onb = on_sb.next()
                        sqt, sqb = sq_r.next()
                        opl = []
                        Stt, Stb = St_r.next()
                        for h in range(NH):
                            S.op("act", lambda h=h: nc.scalar.activation(out=Stt[:, h * 128:(h + 1) * 128], in_=self.Sst[:, h * 128:(h + 1) * 128],
                                                                         func=AF.Identity, scale=ebr[:, h, gch:gch + 1]),
                                 reads=[self.b_S, self.b_Sh[h], b_sc[h]], writes=[Stb])
                        for half in range(2):
                            op_, opb = self.pshold.next()
                            opl.append((op_, opb))
                            for hq in range(4):
                                h = half * 4 + hq
                                S.op("pe", lambda h=h, hq=hq, op_=op_: nc.tensor.matmul(op_[:, hq * 128:hq * 128 + tn], lhsT=vt[0:tn, h * 128:(h + 1) * 128],
                                                                                     rhs=at[0:tn, h, 0:tn], start=(hq == 0), stop=False),
                                     reads=[vb, atb], writes=[opb], signal=False)
                        for half in range(2):
                            op_, opb = opl[half]
                            for hq in range(4):
                                h = half * 4 + hq
                                S.op("pe", lambda h=h, hq=hq, op_=op_: nc.tensor.matmul(op_[:, hq * 128:hq * 128 + tn], lhsT=Stt[:, h * 128:(h + 1) * 128],
                                                                                     rhs=qtT[:, h, o0:o0 + tn], start=False, stop=True),
                                     reads=[Stb, b_qt[h]], writes=[opb], signal=(hq == 3))
                    for half in range(2):
                        mp, mpb = psB.next()
                        for hq in range(4):
                            h = half * 4 + hq
                            S.op("pe", lambda h=h, hq=hq: nc.tensor.matmul(mp[:, hq * 128:(hq + 1) * 128], lhsT=kt[c0:c0 + cn, h * 128:(h + 1) * 128],
                                                                           rhs=vt[c0:c0 + cn, h * 128:(h + 1) * 128], start=True, stop=True),
                                 reads=[kb, vb], writes=[mpb], signal=(hq == 3))
                        for hq in range(4):
                            h = half * 4 + hq
                            S.op("dve", lambda h=h: nc.vector.tensor_scalar(out=self.Sst[:, h * 128:(h + 1) * 128], in0=self.Sst[:, h * 128:(h + 1) * 128],
                                                                            scalar1=dec[:, h, gch:gch + 1], scalar2=None, op0=ALU.mult),
                                 reads=[b_sc[h], self.b_S], writes=[self.b_Sh[h]])
                            S.op("dve", lambda h=h, hq=hq: nc.vector.scalar_tensor_tensor(out=self.Sst[:, h * 128:(h + 1) * 128], in0=mp[:, hq * 128:(hq + 1) * 128],
                                                                                          scalar=edl[:, h, gch:gch + 1], in1=self.Sst[:, h * 128:(h + 1) * 128],
                                                                                          op0=ALU.mult, op1=ALU.add),
                                 reads=[mpb, b_sc[h]], writes=[self.b_Sh[h]])
                    if not state_only:
                        for half in range(2):
                            op_, opb = opl[half]
                            S.op("act", lambda half=half, op_=op_: nc.scalar.activation(
                                out=sqt[:, half * 4:half * 4 + 4, 0:tn], in_=op_[:, :].rearrange("p (h t) -> p h t", t=128)[:, :, 0:tn], func=AF.Square),
                                reads=[opb], writes=[sqb])
                        rst, rsb = rs_r.next()
                        for half in range(2):
                            sp_, spb = psB.next()
                            if tn == 128:
                                S.op("pe", lambda half=half, sp_=sp_: nc.tensor.matmul(sp_[:, :], lhsT=self.onesb[:, :],
                                                                                       rhs=sqt[:, half * 4:half * 4 + 4, :].rearrange("p h t -> p (h t)"), start=True, stop=True),
                                     reads=[sqb, bc], writes=[spb])
                            else:
                                for hq in range(4):
                                    S.op("pe", lambda half=half, sp_=sp_, hq=hq: nc.tensor.matmul(sp_[:, hq * 128:hq * 128 + tn], lhsT=self.onesb[:, :],
                                                                                                rhs=sqt[:, half * 4 + hq, 0:tn], start=True, stop=True),
                                         reads=[sqb, bc], writes=[spb], signal=(hq == 3))
                            S.op("act", lambda half=half, sp_=sp_: nc.scalar.activation(
                                out=rst[:, half * 4:half * 4 + 4, 0:tn], in_=sp_[:, :].rearrange("p (h t) -> p h t", t=128)[:, :, 0:tn],
                                func=AF.Ln, scale=1.0 / 128.0, bias=RMS_EPS), reads=[spb], writes=[rsb])
                            S.op("act", lambda half=half: nc.scalar.activation(out=rst[:, half * 4:half * 4 + 4, 0:tn], in_=rst[:, half * 4:half * 4 + 4, 0:tn],
                                                                               func=AF.Exp, scale=-0.5), reads=[rsb], writes=[rsb])
                            op_, opb = opl[half]
                            S.op("dve", lambda half=half, op_=op_: nc.vector.tensor_tensor(
                                out=ont[:, half * 4:half * 4 + 4, 0:tn], in0=op_[:, :].rearrange("p (h t) -> p h t", t=128)[:, :, 0:tn],
                                in1=rst[:, half * 4:half * 4 + 4, 0:tn], op=ALU.mult), reads=[opb, rsb], writes=[onb])
                        S.dma("pool", self.on_scr[:, :, tt0:tt0 + tn].rearrange("k p t -> p k t"), ont[:, :, 0:tn], reads=[onb], writes=[self.b_on])

                s1 = {}

                def run_s1(k):
                    s1[k] = stage1(tiles[k][0], tiles[k][1], tis[k])

                run_s1(0)
                for k in range(len(tiles)):
                    fns = [lambda k=k: stage2(tiles[k][0], tiles[k][1], tis[k], s1.pop(k))]
                    if k + 1 < len(tiles):
                        fns.append(lambda k=k: run_s1(k + 1))
                    self.IL.run(fns)
            if state_only:
                xt = self.sb(es, "xch", [128, XW], F32)
                xb = Buf("xch")
                tl = T - 32
                ti = len(TILES) - 1
                for ck in range(KC):
                    cvp, cvb = self.ps()
                    for kc in range(KC):
                        S.op("pe", lambda kc=kc: nc.tensor.matmul(cvp[:, 0:32], lhsT=wCV[:, kc, ck * 128:(ck + 1) * 128], rhs=self.hT[:, kc, tl:T],
                                                                  start=(kc == 0), stop=(kc == KC - 1)),
                             reads=[self.b_hT[ti], wCVb[(ck * 128) // pc]], writes=[cvb], signal=(kc == KC - 1))
                    cgp, cgb = self.ps()
                    for kc in range(KC):
                        S.op("pe", lambda kc=kc: nc.tensor.matmul(cgp[:, 0:32], lhsT=wCG[:, kc, ck * 128:(ck + 1) * 128], rhs=self.hT[:, kc, tl:T],
                                                                  start=(kc == 0), stop=(kc == KC - 1)),
                             reads=[self.b_hT[ti], wCGb[(ck * 128) // pc]], writes=[cgb], signal=(kc == KC - 1))
                    e1, e1b = tr.next()
                    S.op("act", lambda: nc.scalar.activation(out=e1[:, 0:32], in_=cgp[:, 0:32], func=AF.Exp, scale=-1.0), reads=[cgb], writes=[e1b])
                    S.op("act", lambda: nc.scalar.activation(out=e1[:, 0:32], in_=e1[:, 0:32], func=AF.Ln, bias=1.0), reads=[e1b], writes=[e1b])
                    S.op("act", lambda: nc.scalar.activation(out=e1[:, 0:32], in_=e1[:, 0:32], func=AF.Exp, scale=-1.0), reads=[e1b], writes=[e1b])
                    S.op("dve", lambda: nc.vector.tensor_tensor(out=xt[:, 1024 + ck * HALO:1024 + (ck + 1) * HALO], in0=cvp[:, 2:32], in1=e1[:, 2:32], op=ALU.mult),
                         reads=[cvb, e1b], writes=[xb])
                S.op("dve", lambda: nc.vector.tensor_copy(out=xt[:, 0:1024], in_=self.Sst[:, :]), reads=[self.b_S] + self.b_Sh, writes=[xb])
                self.b_ccin = Buf("cc_in")
                S.dma("pool", self.cc_in[:, :], xt[:, :], reads=[xb], writes=[self.b_ccin])
            S.barrier()

    def exchange(self):
        nc, S = self.nc, self.S
        b_ccout = Buf("cc_out")
        S.collective(lambda: nc.gpsimd.collective_compute("AllGather", ALU.bypass, replica_groups=[[0, 1], [2, 3], [4, 5], [6, 7]],
                                                          ins=[self.cc_in[:, :]], outs=[self.cc_out[:, :]]),
                     reads=[self.b_ccin], writes=[b_ccout])
        with ExitStack() as es:
            xt = self.sb(es, "xin_t", [128, XW], F32)
            xb = Buf("xin_t")
            S.dma("sp", xt[:, :], self.cc_out[0:128, :], reads=[b_ccout], writes=[xb])
            S.op("dve", lambda: nc.vector.tensor_scalar(out=self.Sst[:, :], in0=xt[:, 0:1024], scalar1=self.flag[:, 0:1], scalar2=None, op0=ALU.mult),
                 reads=[xb, self.b_const], writes=[self.b_S] + self.b_Sh)
            S.op("dve", lambda: nc.vector.tensor_scalar(out=self.halo0[:, :, :], in0=xt[:, 1024:XW].rearrange("p (k t) -> p k t", t=HALO),
                                                        scalar1=self.flag[:, 0:1], scalar2=2.0, op0=ALU.mult, op1=ALU.mult),
                 reads=[xb, self.b_const], writes=[self.b_halo0])
            S.barrier()

    def phase_conv(self, l):
        nc, S, I = self.nc, self.S, self.I
        self.psr = self.psr6
        W = I["w_in"][l]
        BL = blocks(512)
        bc = self.b_const
        with ExitStack() as es:
            wCV, wCVb, pc = self.load_w(es, "wCV", W[:, OFF_CV:OFF_CV + D], D)
            wCG, wCGb, _ = self.load_w(es, "wCG", W[:, OFF_CG:OFF_CG + D], D)
            cw = self.sb(es, "cw", [128, KC, CW], F32)
            par = self.sb(es, "cpar", [128, 3, KC], F32)
            b_par = Buf("cpar")
            S.dma("sp", cw[:], I["conv_w"][l], writes=[b_par])
            S.dma("sp", par[:, 0, :], I["conv_b"][l], writes=[b_par])
            S.dma("sp", par[:, 1, :], I["conv_ln_g"][l], writes=[b_par])
            S.dma("sp", par[:, 2, :], I["conv_ln_b"][l], writes=[b_par])
            S.op("dve", lambda: nc.vector.tensor_scalar(out=cw[:], in0=cw[:], scalar1=0.5, scalar2=None, op0=ALU.mult), reads=[b_par], writes=[b_par])
            diag = self.sb(es, "diag", [128, KC, CW, 128], BF16)
            b_diag = Buf("diag")
            for ck in range(KC):
                S.op("pool", lambda ck=ck: nc.gpsimd.tensor_tensor(
                    out=diag[:, ck, :, :], in0=self.ident[:, :].rearrange("p (o n) -> p o n", o=1).to_broadcast([128, CW, 128]),
                    in1=cw[:, ck, :].rearrange("p (j o) -> p j o", o=1).to_broadcast([128, CW, 128]), op=ALU.mult),
                    reads=[b_par, bc], writes=[b_diag])
            uT = self.sb(es, "uT", [128, KC, HALO + 512], BF16)
            b_u = Buf("uT")
            b_uk = [Buf(f"uT{i}") for i in range(KC)]
            utmp = self.sb(es, "utmp", [128, KC, HALO], BF16)
            S.op("dve", lambda: nc.vector.tensor_copy(out=uT[:, :, 0:HALO], in_=self.halo0[:, :, :]), reads=[self.b_halo0], writes=[b_u] + b_uk)
            y_sb = self.sb(es, "cy", [128, KC, 512], F32)
            ysq = self.sb(es, "cysq", [128, KC, 512], BF16)
            b_y, b_ysq = Buf("cy"), Buf("cysq")
            b_yk = [Buf(f"cyk{i}") for i in range(KC)]
            yc_r = self.ring(es, "ycT", [128, KC, 512], BF16, 1)
            tr = self.ring(es, "cv_t", [128, 512], F32, 5)
            for bi, (t0, n) in enumerate(BL):
                tis = [TILES.index((t0 + o, min(128, n - o))) for o in range(0, n, 128)]
                hbufs = [self.b_hT[ti] for ti in tis]
                def proj_body(ck):
                    cvp, cvb = self.ps()
                    for kc in range(KC):
                        S.op("pe", lambda kc=kc: nc.tensor.matmul(cvp[:, 0:n], lhsT=wCV[:, kc, ck * 128:(ck + 1) * 128], rhs=self.hT[:, kc, t0:t0 + n],
                                                                  start=(kc == 0), stop=(kc == KC - 1)),
                             reads=hbufs + [wCVb[(ck * 128) // pc]], writes=[cvb], signal=(kc == KC - 1))
                    cgp, cgb = self.ps()
                    for kc in range(KC):
                        S.op("pe", lambda kc=kc: nc.tensor.matmul(cgp[:, 0:n], lhsT=wCG[:, kc, ck * 128:(ck + 1) * 128], rhs=self.hT[:, kc, t0:t0 + n],
                                                                  start=(kc == 0), stop=(kc == KC - 1)),
                             reads=hbufs + [wCGb[(ck * 128) // pc]], writes=[cgb], signal=(kc == KC - 1))
                    th, thb = tr.next()
                    S.op("act", lambda: nc.scalar.activation(out=th[:, 0:n], in_=cgp[:, 0:n], func=AF.Tanh, scale=0.5), reads=[cgb], writes=[thb])
                    S.op("dve", lambda: nc.vector.scalar_tensor_tensor(out=uT[:, ck, HALO:HALO + n], in0=th[:, 0:n], scalar=1.0, in1=cvp[:, 0:n],
                                                                       op0=ALU.add, op1=ALU.mult), reads=[thb, cvb, b_u], writes=[b_uk[ck]])
                for c0 in range(0, KC, 2):
                    self.IL.run([lambda c=c0: proj_body(c), lambda c=c0 + 1: proj_body(c)])

                def conv_body(ck):
                    yp, ypb = self.ps()
                    for j in range(CW):
                        S.op("pe", lambda j=j: nc.tensor.matmul(yp[:, 0:n], lhsT=diag[:, ck, j, :], rhs=uT[:, ck, j:j + n], start=(j == 0), stop=(j == CW - 1)),
                             reads=[b_u, b_uk[ck], b_diag], writes=[ypb], signal=(j == CW - 1))
                    S.op("act", lambda: nc.scalar.activation(out=y_sb[:, ck, 0:n], in_=yp[:, 0:n], func=AF.Identity, bias=par[:, 0, ck:ck + 1]),
                         reads=[ypb, b_par], writes=[b_y, b_yk[ck]])
                    S.op("act", lambda: nc.scalar.activation(out=ysq[:, ck, 0:n], in_=yp[:, 0:n], func=AF.Square, bias=par[:, 0, ck:ck + 1]),
                         reads=[ypb, b_par], writes=[b_ysq])
                for c0 in range(0, KC, 2):
                    self.IL.run([lambda c=c0: conv_body(c), lambda c=c0 + 1: conv_body(c)])
                S.op("pool", lambda: nc.gpsimd.tensor_copy(out=utmp[:, :, :], in_=uT[:, :, n:n + HALO]), reads=[b_u] + b_uk, writes=[b_u])
                S.op("pool", lambda: nc.gpsimd.tensor_copy(out=uT[:, :, 0:HALO], in_=utmp[:, :, :]), reads=[b_u], writes=[b_u] + b_uk)
                mp, mpb = self.ps()
                for ck in range(KC):
                    S.op("pe", lambda ck=ck: nc.tensor.matmul(mp[:, 0:n], lhsT=self.onesf[:, :], rhs=y_sb[:, ck, 0:n], start=(ck == 0), stop=(ck == KC - 1)),
                         reads=[b_y, bc], writes=[mpb], signal=(ck == KC - 1))
                qp, qpb = self.ps()
                for ck in range(KC):
                    S.op("pe", lambda ck=ck: nc.tensor.matmul(qp[:, 0:n], lhsT=self.onesb[:, :], rhs=ysq[:, ck, 0:n], start=(ck == 0), stop=(ck == KC - 1)),
                         reads=[b_ysq, bc], writes=[qpb], signal=(ck == KC - 1))
                mean, meb = tr.next()
                var, vab = tr.next()
                S.op("act", lambda: nc.scalar.mul(out=mean[:, 0:n], in_=mp[:, 0:n], mul=1.0 / D), reads=[mpb], writes=[meb])
                S.op("dve", lambda: nc.vector.tensor_tensor(out=var[:, 0:n], in0=mean[:, 0:n], in1=mean[:, 0:n], op=ALU.mult), reads=[meb], writes=[vab])
                S.op("dve", lambda: nc.vector.scalar_tensor_tensor(out=var[:, 0:n], in0=qp[:, 0:n], scalar=1.0 / D, in1=var[:, 0:n], op0=ALU.mult, op1=ALU.subtract),
                     reads=[qpb], writes=[vab])
                S.op("act", lambda: nc.scalar.activation(out=var[:, 0:n], in_=var[:, 0:n], func=AF.Ln, bias=LN_EPS), reads=[vab], writes=[vab])
                S.op("act", lambda: nc.scalar.activation(out=var[:, 0:n], in_=var[:, 0:n], func=AF.Exp, scale=-0.5), reads=[vab], writes=[vab])
                S.op("dve", lambda: nc.vector.scalar_tensor_tensor(out=mean[:, 0:n], in0=mean[:, 0:n], scalar=-1.0, in1=var[:, 0:n], op0=ALU.mult, op1=ALU.mult),
                     reads=[vab], writes=[meb])
                yct, ycb = yc_r.next()
                def norm_body(ck):
                    S.op("dve", lambda ck=ck: nc.vector.tensor_tensor(out=y_sb[:, ck, 0:n], in0=y_sb[:, ck, 0:n], in1=var[:, 0:n], op=ALU.mult), reads=[vab, b_y], writes=[b_yk[ck]])
                    S.op("pool", lambda ck=ck: nc.gpsimd.tensor_tensor(out=y_sb[:, ck, 0:n], in0=y_sb[:, ck, 0:n], in1=mean[:, 0:n], op=ALU.add), reads=[meb, b_yk[ck]], writes=[b_yk[ck]])
                    S.op("act", lambda ck=ck: nc.scalar.activation(out=yct[:, ck, 0:n], in_=y_sb[:, ck, 0:n], func=AF.Silu, scale=par[:, 1, ck:ck + 1], bias=par[:, 2, ck:ck + 1]),
                         reads=[b_yk[ck], b_par, ycb], writes=[ycb_k[ck]])
                ycb_k = [Buf(f"ycb{i}") for i in range(KC)]
                for c0 in range(0, KC, 4):
                    self.IL.run([lambda c=c0 + i: norm_body(c) for i in range(4)])
                S.dma("pool", self.yc_scr[:, :, t0:t0 + n].rearrange("k p t -> p k t"), yct[:, :, 0:n], reads=[ycb] + ycb_k, writes=[self.b_yc, ycb])
            S.barrier()

    def phase_mix(self, l):
        nc, S, I = self.nc, self.S, self.I
        self.psr = self.psr6
        W = I["w_in"][l]
        NB = 256
        BL = blocks(NB)
        bc = self.b_const
        moe = (l % 2 == 1)
        with ExitStack() as es:
            wOG, wOGb, pc = self.load_w(es, "wOG", W[:, OFF_OG:OFF_OG + D], D)
            wHG, wHGb, _ = self.load_w(es, "wHG", I["w_hg_out"][l], D)
            wGA, wGAb, _ = self.load_w(es, "wGA", W[:, OFF_GA:OFF_GA + D], D)
            wCO, wCOb, _ = self.load_w(es, "wCO", I["w_conv_out"][l], D)
            wGB, wGBb, _ = self.load_w(es, "wGB", W[:, OFF_GB:OFF_GB + D], D)
            wO, wOb, _ = self.load_w(es, "wO", I["w_out"][l], D)
            hg = self.sb(es, "hg_g", [128, KC], F32)
            b_par = Buf("mixpar")
            S.dma("sp", hg[:], I["hg_norm_g"][l], writes=[b_par])
            g_t = self.sb(es, "ln1_g", [128, D], F32)
            b_t = self.sb(es, "ln1_b", [128, D], F32)
            gb = Buf("ln1_gb")
            if moe:
                wr = self.sb(es, "wr", [128, KC, N_EXP], F32)
                S.dma("sp", wr[:], I["moe_router"][0], writes=[b_par])
                lg_all = self.sb(es, "lg_all", [128, len(TILES), N_EXP], F32)
                b_lg = Buf("lg_all")
                S.op("dve", lambda: nc.vector.memset(lg_all[:], 0.0), writes=[b_lg])
                h32T_r = self.ring(es, "h32T", [128, KC, 128], F32, 2)
            on_r = self.ring(es, "on_blk", [128, KC, NB], BF16, 2)
            yc_r = self.ring(es, "yc_blk", [128, KC, NB], BF16, 2)
            gt_r = self.ring(es, "gatedT", [128, KC, NB], BF16, 1)
            mx_r = self.ring(es, "mixedT", [128, KC, NB], BF16, 1)
            tr = self.ring(es, "mx_t", [128, NB], F32, 4)
            ho_r = self.ring(es, "h_old", [128, D], F32, 2)
            ah_r = self.ring(es, "ah_t", [128, D], F32, 2)
            hbr = self.ring(es, "mx_hb", [128, D], BF16, 2)
            sc = [self.ln_scratch(es, f"ln1s{i}") for i in range(2)]
            loaded = {}

            def load(bi):
                t0, n = BL[bi]
                ont, onb = on_r.next()
                yct, ycb = yc_r.next()
                S.dma("sp", ont[:, :, 0:n], self.on_scr[:, :, t0:t0 + n].rearrange("k p t -> p k t"), reads=[self.b_on], writes=[onb])
                S.dma("sp", yct[:, :, 0:n], self.yc_scr[:, :, t0:t0 + n].rearrange("k p t -> p k t"), reads=[self.b_yc], writes=[ycb])
                loaded[bi] = (ont, onb, yct, ycb)

            load(0)
            for bi, (t0, n) in enumerate(BL):
                if bi + 1 < len(BL):
                    load(bi + 1)
                if bi == 0:
                    S.dma("sp", g_t[:], I["ln1_g"][l:l + 1, :].to_broadcast([128, D]), writes=[gb])
                    S.dma("sp", b_t[:], I["ln1_b"][l:l + 1, :].to_broadcast([128, D]), writes=[gb])
                ont, onb, yct, ycb = loaded.pop(bi)
                tiles = [(t0 + o, min(128, n - o)) for o in range(0, n, 128)]
                tis = [TILES.index(tt) for tt in tiles]
                hbufs = [self.b_hT[ti] for ti in tis]
                hold = []
                for (tt0, tn), ti_ in zip(tiles, tis):
                    hot, hob = ho_r.next()
                    S.dma("sp", hot[0:tn, :], self.h_scr[tt0:tt0 + tn, :], reads=[self.b_hs[ti_]], writes=[hob])
                    hold.append((hot, hob))
                gtt, gtb = gt_r.next()
                gtk = [Buf(f"gt{i}") for i in range(KC)]
                S.op("dve", lambda: nc.vector.memset(gtt[:, 0, 0:1], 0.0), writes=[gtb] + gtk)

                def og_body(ck):
                    ogp, ogb = self.ps()
                    for kc in range(KC):
                        S.op("pe", lambda kc=kc: nc.tensor.matmul(ogp[:, 0:n], lhsT=wOG[:, kc, ck * 128:(ck + 1) * 128], rhs=self.hT[:, kc, t0:t0 + n],
                                                                  start=(kc == 0), stop=(kc == KC - 1)),
                             reads=hbufs + [wOGb[(ck * 128) // pc]], writes=[ogb], signal=(kc == KC - 1))
                    sg, sgb = tr.next()
                    S.op("act", lambda: nc.scalar.activation(out=sg[:, 0:n], in_=ogp[:, 0:n], func=AF.Silu), reads=[ogb], writes=[sgb])
                    S.op("dve", lambda: nc.vector.scalar_tensor_tensor(out=gtt[:, ck, 0:n], in0=ont[:, ck, 0:n], scalar=hg[:, ck:ck + 1], in1=sg[:, 0:n],
                                                                       op0=ALU.mult, op1=ALU.mult), reads=[onb, sgb, b_par, gtb], writes=[gtk[ck]])
                for c0 in range(0, KC, 2):
                    self.IL.run([lambda c=c0: og_body(c), lambda c=c0 + 1: og_body(c)])
                mxt, mxb = mx_r.next()
                mxk = [Buf(f"mx{i}") for i in range(KC)]
                S.op("dve", lambda: nc.vector.memset(mxt[:, 0, 0:1], 0.0), writes=[mxb] + mxk)

                def m_body(m):
                    yrp, yrb = self.ps()
                    for kc in range(KC):
                        S.op("pe", lambda kc=kc: nc.tensor.matmul(yrp[:, 0:n], lhsT=wHG[:, kc, m * 128:(m + 1) * 128], rhs=gtt[:, kc, 0:n],
                                                                  start=(kc == 0), stop=(kc == KC - 1)),
                             reads=gtk + [gtb, wHGb[(m * 128) // pc]], writes=[yrb], signal=(kc == KC - 1))
                    gap, gab = self.ps()
                    for kc in range(KC):
                        S.op("pe", lambda kc=kc: nc.tensor.matmul(gap[:, 0:n], lhsT=wGA[:, kc, m * 128:(m + 1) * 128], rhs=self.hT[:, kc, t0:t0 + n],
                                                                  start=(kc == 0), stop=(kc == KC - 1)),
                             reads=hbufs + [wGAb[(m * 128) // pc]], writes=[gab], signal=(kc == KC - 1))
                    ta, tab = tr.next()
                    S.op("act", lambda: nc.scalar.activation(out=ta[:, 0:n], in_=gap[:, 0:n], func=AF.Tanh, scale=0.5), reads=[gab], writes=[tab])
                    S.op("dve", lambda: nc.vector.scalar_tensor_tensor(out=ta[:, 0:n], in0=ta[:, 0:n], scalar=1.0, in1=yrp[:, 0:n], op0=ALU.add, op1=ALU.mult),
                         reads=[yrb], writes=[tab])
                    ycp, ycpb = self.ps()
                    for kc in range(KC):
                        S.op("pe", lambda kc=kc: nc.tensor.matmul(ycp[:, 0:n], lhsT=wCO[:, kc, m * 128:(m + 1) * 128], rhs=yct[:, kc, 0:n],
                                                                  start=(kc == 0), stop=(kc == KC - 1)),
                             reads=[ycb, wCOb[(m * 128) // pc]], writes=[ycpb], signal=(kc == KC - 1))
                    gbp, gbb = self.ps()
                    for kc in range(KC):
                        S.op("pe", lambda kc=kc: nc.tensor.matmul(gbp[:, 0:n], lhsT=wGB[:, kc, m * 128:(m + 1) * 128], rhs=self.hT[:, kc, t0:t0 + n],
                                                                  start=(kc == 0), stop=(kc == KC - 1)),
                             reads=hbufs + [wGBb[(m * 128) // pc]], writes=[gbb], signal=(kc == KC - 1))
                    tb_, tbb = tr.next()
                    S.op("act", lambda: nc.scalar.activation(out=tb_[:, 0:n], in_=gbp[:, 0:n], func=AF.Tanh, scale=0.5), reads=[gbb], writes=[tbb])
                    S.op("dve", lambda: nc.vector.scalar_tensor_tensor(out=tb_[:, 0:n], in0=tb_[:, 0:n], scalar=1.0, in1=ycp[:, 0:n], op0=ALU.add, op1=ALU.mult),
                         reads=[ycpb], writes=[tbb])
                    S.op("dve", lambda: nc.vector.tensor_tensor(out=mxt[:, m, 0:n], in0=ta[:, 0:n], in1=tb_[:, 0:n], op=ALU.add), reads=[tab, tbb, mxb], writes=[mxk[m]])
                for m0 in range(0, KC, 2):
                    self.IL.run([lambda m=m0: m_body(m), lambda m=m0 + 1: m_body(m)])
                def ln1_body(tt0, tn, ti, hot, hob):
                    o0 = tt0 - t0
                    rt, rb = hot, hob
                    for hh in range(2):
                        pp, ppb = self.ps()
                        for kc in range(KC):
                            S.op("pe", lambda kc=kc: nc.tensor.matmul(pp[0:tn, :], lhsT=mxt[:, kc, o0:o0 + tn], rhs=wO[:, kc, hh * 512:(hh + 1) * 512],
                                                                      start=(kc == 0), stop=(kc == KC - 1)),
                                 reads=mxk + [mxb, wOb[(hh * 512) // pc]], writes=[ppb], signal=(kc == KC - 1))
                        S.op("dve", lambda hh=hh, pp=pp: nc.vector.scalar_tensor_tensor(out=rt[0:tn, hh * 512:(hh + 1) * 512], in0=pp[0:tn, :], scalar=0.5,
                                                                                       in1=hot[0:tn, hh * 512:(hh + 1) * 512], op0=ALU.mult, op1=ALU.add),
                             reads=[ppb, hob], writes=[rb])
                    hnt, hnb = hot, hob
                    self.layer_norm_tile(sc[ti % 2], rt[0:tn, :], rb, tn, g_t, b_t, gb, hnt[0:tn, :], hnb)
                    if ti == 0:
                        S.op("dve", lambda: nc.vector.tensor_scalar(out=hnt[0:tn, :], in0=hnt[0:tn, :], scalar1=self.flag[0:tn, 0:1],
                                                                    scalar2=None, op0=ALU.mult), reads=[bc], writes=[hnb])
                    aht, ahb = ah_r.next()
                    S.op("act", lambda: nc.scalar.mul(out=aht[0:tn, :], in_=hnt[0:tn, :], mul=ALPHA), reads=[hnb], writes=[ahb])
                    S.dma("pool", self.h_scr[tt0:tt0 + tn, :], aht[0:tn, :], reads=[ahb], writes=[self.b_hs[ti]])
                    hbt, hbb = hbr.next()
                    self.to_hT(hnt[0:tn, :], hnb, tn, ti, hbt[0:tn, :], hbb)
                    if moe:
                        h32, h32b = h32T_r.next()
                        for half in range(2):
                            tp, tpb = self.ps()
                            for q in range(4):
                                kc = half * 4 + q
                                S.op("pe", lambda kc=kc, q=q: nc.tensor.transpose(tp[:, q * 128:q * 128 + tn], hnt[0:tn, kc * 128:(kc + 1) * 128], self.identf[0:tn, 0:tn]),
                                     reads=[hnb, bc], writes=[tpb], signal=(q == 3))
                            S.op("act", lambda half=half, tp=tp: nc.scalar.copy(out=h32[:, half * 4:half * 4 + 4, 0:tn],
                                                                               in_=tp[:, :].rearrange("p (k t) -> p k t", t=128)[:, :, 0:tn]),
                                 reads=[tpb], writes=[h32b])
                        lp, lpb = self.ps()
                        for kc in range(KC):
                            S.op("pe", lambda kc=kc: nc.tensor.matmul(lp[0:tn, 0:N_EXP], lhsT=h32[:, kc, 0:tn], rhs=wr[:, kc, :], start=(kc == 0), stop=(kc == KC - 1)),
                                 reads=[h32b, b_par], writes=[lpb], signal=(kc == KC - 1))
                        S.op("dve", lambda: nc.vector.tensor_copy(out=lg_all[0:tn, ti, :], in_=lp[0:tn, 0:N_EXP]), reads=[lpb], writes=[b_lg])
                self.IL.run([lambda a=a: ln1_body(a[0][0], a[0][1], a[1], a[2][0], a[2][1]) for a in zip(tiles, tis, hold)])
            if moe:
                NT = len(TILES)
                mx8 = self.sb(es, "mx8", [128, NT, 8], F32)
                msk = self.sb(es, "msk", [128, NT, 8], F32)
                den = self.sb(es, "den", [128, NT, 1], F32)
                b_g = Buf("gtmp")
                for ti in range(NT):
                    S.op("dve", lambda ti=ti: nc.vector.max(out=mx8[:, ti, :], in_=lg_all[:, ti, :]), reads=[b_lg], writes=[b_g], signal=(ti == NT - 1))
                S.op("dve", lambda: nc.vector.tensor_tensor(out=msk[:], in0=lg_all[:], in1=mx8[:, :, 1:2].to_broadcast([128, NT, 8]), op=ALU.is_ge),
                     reads=[b_lg, b_g], writes=[b_g])
                S.op("dve", lambda: nc.vector.tensor_tensor(out=lg_all[:], in0=lg_all[:], in1=mx8[:, :, 0:1].to_broadcast([128, NT, 8]), op=ALU.subtract),
                     reads=[b_g], writes=[b_lg])
                S.op("act", lambda: nc.scalar.activation(out=lg_all[:], in_=lg_all[:], func=AF.Exp), reads=[b_lg], writes=[b_lg])
                S.op("dve", lambda: nc.vector.tensor_tensor(out=lg_all[:], in0=lg_all[:], in1=msk[:], op=ALU.mult), reads=[b_g], writes=[b_lg])
                S.op("dve", lambda: nc.vector.tensor_reduce(out=den[:].rearrange("p n o -> p (n o)"), in_=lg_all[:], axis=AX.X, op=ALU.add), reads=[b_lg], writes=[b_g])
                S.op("dve", lambda: nc.vector.reciprocal(out=den[:], in_=den[:]), reads=[b_g], writes=[b_g])
                S.op("dve", lambda: nc.vector.tensor_tensor(out=self.gates[:], in0=lg_all[:], in1=den[:, :, 0:1].to_broadcast([128, NT, 8]), op=ALU.mult),
                     reads=[b_lg, b_g], writes=[self.b_gates])
            S.barrier()

    def phase_ffn(self, l, experts, dff, moe):
        nc, S, I = self.nc, self.S, self.I
        self.psr = self.psr6
        BL = blocks(512)
        bc = self.b_const
        G = 512
        groups = [(g0, min(G, dff - g0)) for g0 in range(0, dff, G)]
        last_layer = (l == DEPTH - 1)
        with ExitStack() as es:
            acc = self.sb(es, "acc", [128, len(TILES), D], F32)
            b_acc = [Buf(f"acc{i}") for i in range(len(TILES))]
            for ti, (t0, n) in enumerate(TILES):
                S.dma("sp", acc[0:n, ti, :], self.h_scr[t0:t0 + n, :], reads=[self.b_hs[ti]], writes=[b_acc[ti]])
            g_t = self.sb(es, "ln2_g", [128, D], F32)
            b_t = self.sb(es, "ln2_b", [128, D], F32)
            gb = Buf("ln2_gb")
            S.dma("sp", g_t[:], I["ln2_g"][l:l + 1, :].to_broadcast([128, D]), writes=[gb])
            S.dma("sp", b_t[:], I["ln2_b"][l:l + 1, :].to_broadcast([128, D]), writes=[gb])
            NWB = 2
            wg_r = self.ring(es, "wg", [128, KC, G], BF16, NWB)
            wu_r = self.ring(es, "wu", [128, KC, G], BF16, NWB)
            wd_r = self.ring(es, "wd", [128, G // 128, D], BF16, NWB)
            aT_r = self.ring(es, "aT", [128, G // 128, 512], BF16, 2)
            tr = self.ring(es, "ff_t", [128, 512], F32, 4)
            work = [(e, g) for e in range(len(experts)) for g in groups]
            loaded = {}

            def load(wi):
                e, (g0, gn) = work[wi]
                wgs, wus, wds = experts[e]
                wgt, wgb = wg_r.next()
                wut, wub = wu_r.next()
                wdt, wdb = wd_r.next()
                S.dma("pool", wgt[:, :, 0:gn], wgs[:, g0:g0 + gn].rearrange("(kc p) n -> p kc n", p=128), writes=[wgb])
                S.dma("pool", wut[:, :, 0:gn], wus[:, g0:g0 + gn].rearrange("(kc p) n -> p kc n", p=128), writes=[wub])
                for hh in range(2):
                    S.dma("pool", wdt[:, 0:gn // 128, hh * 512:(hh + 1) * 512],
                          wds[g0:g0 + gn, hh * 512:(hh + 1) * 512].rearrange("(j p) n -> p j n", p=128), writes=[wdb])
                loaded[wi] = (wgt, wgb, wut, wub, wdt, wdb)

            hn_r = self.ring(es, "h2_new", [128, D], F32, 2)
            ah_r = self.ring(es, "h2_ah", [128, D], F32, 2)
            hbr = self.ring(es, "h2_hb", [128, D], BF16, 2)
            sc = [self.ln_scratch(es, f"ln2s{i}") for i in range(2)]

            def ln2_body(ti):
                t0, n = TILES[ti]
                hnt, hnb = hn_r.next()
                self.layer_norm_tile(sc[ti % 2], acc[0:n, ti, :], b_acc[ti], n, g_t, b_t, gb, hnt[0:n, :], hnb)
                if last_layer:
                    S.dma("pool", self.y[t0:t0 + n, :], hnt[0:n, :], reads=[hnb], writes=[self.b_y])
                else:
                    if ti == 0:
                        S.op("dve", lambda: nc.vector.tensor_scalar(out=hnt[0:n, :], in0=hnt[0:n, :], scalar1=self.flag[0:n, 0:1],
                                                                    scalar2=None, op0=ALU.mult), reads=[bc], writes=[hnb])
                    aht, ahb = ah_r.next()
                    S.op("act", lambda: nc.scalar.mul(out=aht[0:n, :], in_=hnt[0:n, :], mul=ALPHA), reads=[hnb], writes=[ahb])
                    S.dma("pool", self.h_scr[t0:t0 + n, :], aht[0:n, :], reads=[ahb], writes=[self.b_hs[ti]])
                    hbt, hbb = hbr.next()
                    self.to_hT(hnt[0:n, :], hnb, n, ti, hbt[0:n, :], hbb)

            load(0)
            for wi, (e, (g0, gn)) in enumerate(work):
                if wi + 1 < len(work):
                    load(wi + 1)
                wgt, wgb, wut, wub, wdt, wdb = loaded.pop(wi)
                last_item = (wi == len(work) - 1)
                nj = gn // 128
                pending = None
                for bi, (t0, n) in enumerate(BL):
                    tiles = [(t0 + o, min(128, n - o)) for o in range(0, n, 128)]
                    tis = [TILES.index(tt) for tt in tiles]
                    hbufs = [self.b_hT[ti] for ti in tis]
                    aT, aTb = aT_r.next()
                    for j in range(nj):
                        gp, gpb = self.ps()
                        for kc in range(KC):
                            S.op("pe", lambda kc=kc: nc.tensor.matmul(gp[:, 0:n], lhsT=wgt[:, kc, j * 128:(j + 1) * 128], rhs=self.hT[:, kc, t0:t0 + n],
                                                                      start=(kc == 0), stop=(kc == KC - 1)),
                                 reads=hbufs + [wgb], writes=[gpb], signal=(kc == KC - 1))
                        up, upb = self.ps()
                        for kc in range(KC):
                            S.op("pe", lambda kc=kc: nc.tensor.matmul(up[:, 0:n], lhsT=wut[:, kc, j * 128:(j + 1) * 128], rhs=self.hT[:, kc, t0:t0 + n],
                                                                      start=(kc == 0), stop=(kc == KC - 1)),
                                 reads=hbufs + [wub], writes=[upb], signal=(kc == KC - 1))
                        sg, sgb = tr.next()
                        S.op("act", lambda: nc.scalar.activation(out=sg[:, 0:n], in_=gp[:, 0:n], func=AF.Silu), reads=[gpb], writes=[sgb])
                        S.op("dve", lambda j=j: nc.vector.tensor_tensor(out=aT[:, j, 0:n], in0=up[:, 0:n], in1=sg[:, 0:n], op=ALU.mult),
                             reads=[upb, sgb], writes=[aTb])
                    cur = (aT, aTb, tiles, tis, t0)
                    todo = ([pending] if pending is not None else []) + ([cur] if bi == len(BL) - 1 else [])
                    pending = cur
                    for (aT, aTb, tiles, tis, t0) in todo:
                      for (tt0, tn), ti in zip(tiles, tis):
                        o0 = tt0 - t0
                        for hh in range(2):
                            dp, dpb = self.ps()
                            for j in range(nj):
                                S.op("pe", lambda j=j: nc.tensor.matmul(dp[0:tn, :], lhsT=aT[:, j, o0:o0 + tn], rhs=wdt[:, j, hh * 512:(hh + 1) * 512],
                                                                        start=(j == 0), stop=(j == nj - 1)),
                                     reads=[aTb, wdb], writes=[dpb], signal=(j == nj - 1))
                            if moe:
                                S.op("dve", lambda hh=hh, dp=dp: nc.vector.scalar_tensor_tensor(
                                    out=acc[0:tn, ti, hh * 512:(hh + 1) * 512], in0=dp[0:tn, :], scalar=self.gates[0:tn, ti, e:e + 1],
                                    in1=acc[0:tn, ti, hh * 512:(hh + 1) * 512], op0=ALU.mult, op1=ALU.add),
                                    reads=[dpb, self.b_gates], writes=[b_acc[ti]])
                            else:
                                S.op("dve", lambda hh=hh, dp=dp: nc.vector.tensor_tensor(
                                    out=acc[0:tn, ti, hh * 512:(hh + 1) * 512], in0=dp[0:tn, :], in1=acc[0:tn, ti, hh * 512:(hh + 1) * 512], op=ALU.add),
                                    reads=[dpb], writes=[b_acc[ti]])
                        if last_item:
                            ln2_body(ti)
            S.barrier()


def pm(a):
    a = np.asarray(a, dtype=np.float32)
    return np.ascontiguousarray(a.reshape(a.shape[:-1] + (KC, 128)).swapaxes(-1, -2))


def make_in_maps(inputs, stop_after=None):
    x = np.asarray(inputs["x"], dtype=np.float32)
    meta = np.asarray(inputs["meta_tokens"], dtype=np.float32)
    B = x.shape[0]
    shared = {
        "ln_in_g": np.asarray(inputs["ln_in_g"], np.float32).reshape(1, D),
        "ln_in_b": np.asarray(inputs["ln_in_b"], np.float32).reshape(1, D),
        "w_in": np.ascontiguousarray(inputs["w_in"], dtype=np.float32),
        "lower_bounds": pm(inputs["lower_bounds"]),
        "hg_norm_g": pm(inputs["hg_norm_g"]),
        "w_hg_out": np.ascontiguousarray(inputs["w_hg_out"], dtype=np.float32),
        "conv_w": np.ascontiguousarray(np.asarray(inputs["conv_w"], np.float32).reshape(DEPTH, CW, KC, 128).transpose(0, 3, 2, 1)),
        "conv_b": pm(inputs["conv_b"]),
        "conv_ln_g": pm(inputs["conv_ln_g"]),
        "conv_ln_b": pm(inputs["conv_ln_b"]),
        "w_conv_out": np.ascontiguousarray(inputs["w_conv_out"], dtype=np.float32),
        "w_out": np.ascontiguousarray(inputs["w_out"], dtype=np.float32),
        "ln1_g": np.ascontiguousarray(inputs["ln1_g"], dtype=np.float32),
        "ln1_b": np.ascontiguousarray(inputs["ln1_b"], dtype=np.float32),
        "ln2_g": np.ascontiguousarray(inputs["ln2_g"], dtype=np.float32),
        "ln2_b": np.ascontiguousarray(inputs["ln2_b"], dtype=np.float32),
    }
    full = stop_after is None
    if full or stop_after >= 5:
        shared["ffn_w_gate"] = np.ascontiguousarray(inputs["ffn_w_gate"], dtype=np.float32)
        shared["ffn_w_up"] = np.ascontiguousarray(inputs["ffn_w_up"], dtype=np.float32)
        shared["ffn_w_down"] = np.ascontiguousarray(inputs["ffn_w_down"], dtype=np.float32)
    if full or stop_after >= 10:
        shared["moe_router"] = np.ascontiguousarray(
            np.asarray(inputs["moe_router"], np.float32).reshape(1, KC, 128, N_EXP).transpose(0, 2, 1, 3))
        shared["moe_w_gate"] = np.ascontiguousarray(inputs["moe_w_gate"], dtype=np.float32)
        shared["moe_w_up"] = np.ascontiguousarray(inputs["moe_w_up"], dtype=np.float32)
        shared["moe_w_down"] = np.ascontiguousarray(inputs["moe_w_down"], dtype=np.float32)
    maps = []
    NA = T - 32
    for c in range(8):
        b, half = c // 2, c % 2
        if half == 0:
            xin = np.concatenate([np.zeros((16, D), np.float32), meta, x[b, :NA]], axis=0)
        else:
            xin = x[b, NA:]
        m = dict(shared)
        m["xin"] = np.ascontiguousarray(xin)
        m["flag"] = np.full((128, 1), float(half), np.float32)
        maps.append(m)
    return maps


_PROG_CACHE = {}


def kernel(**inputs):
    x = np.asarray(inputs["x"])
    B, SEQ, _ = x.shape
    if None not in _PROG_CACHE:
        _PROG_CACHE[None] = Prog()
    prog = _PROG_CACHE[None]
    maps = make_in_maps(inputs)
    res = run_bass_kernel_spmd(prog.nc, maps, core_ids=list(range(8)))
    NA = T - 32
    out = np.empty((B, SEQ, D), np.float32)
    for c in range(8):
        b, half = c // 2, c % 2
        y = res.results[c]["y"]
        if half == 0:
            out[b, :NA] = y[32:]
        else:
            out[b, NA:] = y
    return out
```
